# Optimizing a Trainium2 kernel written in Bass

```python
import math
import jax, jax.numpy as jnp
from jax import lax
import numpy as np

D_MODEL = 2048
BATCH = 4
SEQ = 2048
DEPTH = 1

HEAD_DIM = 128
N_HEADS = D_MODEL // HEAD_DIM
N_KV_HEADS = 4
ROPE_DIM = HEAD_DIM // 4
ROPE_THETA = 500000.0
N_IDX_HEADS = 16
IDX_DIM = 64
IDX_ROPE_DIM = IDX_DIM // 4
TOPK_MAX = 256
Q_BLOCK = 128
D_RNN = (4 * D_MODEL // 3) // 256 * 256
LRU_BLOCKS = 16
LRU_BLOCK = D_RNN // LRU_BLOCKS
CONV_WIDTH = 4
LRU_C = 8.0
D_FF = (8 * D_MODEL // 3 + 255) // 256 * 256
LN_EPS = 1e-5
DN_ALPHA = (2.0 * DEPTH) ** 0.25
DN_BETA = (8.0 * DEPTH) ** -0.25

IN_WIDTHS = (N_HEADS * HEAD_DIM, N_KV_HEADS * HEAD_DIM, N_KV_HEADS * HEAD_DIM,
             N_IDX_HEADS * IDX_DIM, IDX_DIM, N_IDX_HEADS, D_RNN, D_RNN, 2 * D_MODEL)
IN_TOTAL = sum(IN_WIDTHS)

kernel_name = "hybrid_dsa_rglru_macaron_deepnorm"


def layer_norm(x, g, b):
    xf = x.astype(jnp.float32)
    mu = jnp.mean(xf, axis=-1, keepdims=True)
    var = jnp.mean(jnp.square(xf - mu), axis=-1, keepdims=True)
    y = (xf - mu) * lax.rsqrt(var + LN_EPS) * g.astype(jnp.float32) + b.astype(jnp.float32)
    return y.astype(x.dtype)


def swiglu(h, w_in, w_out):
    a, b = jnp.split(h @ w_in, 2, axis=-1)
    return (jax.nn.silu(a) * b) @ w_out


def rope(t, positions, rot_dim):
    half = rot_dim // 2
    inv_freq = jnp.power(ROPE_THETA, -(jnp.arange(half, dtype=jnp.float32) * 2.0 / rot_dim))
    ang = positions.astype(jnp.float32)[..., None] * inv_freq
    cos = jnp.cos(ang)[:, :, None, :].astype(t.dtype)
    sin = jnp.sin(ang)[:, :, None, :].astype(t.dtype)
    t1 = t[..., :half]
    t2 = t[..., half:rot_dim]
    return jnp.concatenate([t1 * cos - t2 * sin, t2 * cos + t1 * sin, t[..., rot_dim:]], axis=-1)


def split_cols(p):
    offs = []
    acc = 0
    for w in IN_WIDTHS[:-1]:
        acc += w
        offs.append(acc)
    return jnp.split(p, offs, axis=-1)


def dsa_attention(q, k, v, qi, ki, wi):
    B, S = q.shape[0], q.shape[1]
    topk = min(TOPK_MAX, S // 4)
    nb = S // Q_BLOCK
    G = N_HEADS // N_KV_HEADS
    qg = q.reshape(B, S, N_KV_HEADS, G, HEAD_DIM)
    key_pos = jnp.arange(S)
    scale = HEAD_DIM ** -0.5
    gather = jax.vmap(lambda t, i: t[i])

    def blocks(a):
        return a.reshape((B, nb, Q_BLOCK) + a.shape[2:]).swapaxes(0, 1)

    def one_block(args):
        start, qb, qib, wib = args
        qpos = start + jnp.arange(Q_BLOCK)
        causal = key_pos[None, :] <= qpos[:, None]
        logits = jnp.einsum('bqhd,bsd->bqhs', qib, ki).astype(jnp.float32)
        score = jnp.einsum('bqhs,bqh->bqs', jax.nn.relu(logits), wib.astype(jnp.float32))
        score = jnp.where(causal[None], score, -jnp.inf)
        _, idx = lax.top_k(score, topk)
        valid = idx <= qpos[None, :, None]
        k_sel = gather(k, idx)
        v_sel = gather(v, idx)
        s = jnp.einsum('bqhgd,bqkhd->bqhgk', qb, k_sel).astype(jnp.float32) * scale
        s = jnp.where(valid[:, :, None, None, :], s, -jnp.inf)
        p = jax.nn.softmax(s, axis=-1).astype(v.dtype)
        o = jnp.einsum('bqhgk,bqkhd->bqhgd', p, v_sel)
        return o.reshape(B, Q_BLOCK, N_HEADS * HEAD_DIM)

    starts = jnp.arange(nb) * Q_BLOCK
    out = lax.map(one_block, (starts, blocks(qg), blocks(qi), blocks(wi)))
    return out.swapaxes(0, 1).reshape(B, S, N_HEADS * HEAD_DIM)


def rglru_branch(rx, rg, conv_w, conv_b, wa, ba, wx, bx, lam):
    B, S = rx.shape[0], rx.shape[1]
    xc = lax.conv_general_dilated(
        rx, conv_w[:, None, :], window_strides=(1,), padding=[(CONV_WIDTH - 1, 0)],
        dimension_numbers=('NWC', 'WIO', 'NWC'), feature_group_count=D_RNN) + conv_b
    xb = xc.reshape(B, S, LRU_BLOCKS, LRU_BLOCK)
    r = jax.nn.sigmoid(jnp.einsum('btnc,ncd->btnd', xb, wa).reshape(B, S, D_RNN) + ba)
    i = jax.nn.sigmoid(jnp.einsum('btnc,ncd->btnd', xb, wx).reshape(B, S, D_RNN) + bx)
    log_a = -LRU_C * r.astype(jnp.float32) * jax.nn.softplus(-lam.astype(jnp.float32))
    a = jnp.exp(log_a)
    b = jnp.sqrt(-jnp.expm1(2.0 * log_a)) * (i * xc).astype(jnp.float32)

    def combine(left, right):
        a1, b1 = left
        a2, b2 = right
        return a1 * a2, a2 * b1 + b2

    _, hseq = lax.associative_scan(combine, (a, b), axis=1)
    return hseq.astype(rx.dtype) * jax.nn.gelu(rg)


def setup_inputs(seed: int = 0) -> dict:
    key = jax.random.key(seed)
    ks = jax.random.split(key, 24)
    f32 = jnp.float32
    L = DEPTH

    def nrm(k, shape, fan_in, mult=1.0):
        return jax.random.normal(k, shape, f32) * (fan_in ** -0.5) * mult

    def gain(k):
        return 1.0 + 0.02 * jax.random.normal(k, (L, D_MODEL), f32)

    def bias(k, n):
        return 0.02 * jax.random.normal(k, (L, n), f32)

    u = jax.random.uniform(ks[13], (L, D_RNN), f32, minval=0.9, maxval=0.999)
    s = u ** (1.0 / LRU_C)
    lru_lambda = jnp.log(s) - jnp.log1p(-s)
    return {
        "x": jax.random.normal(ks[0], (BATCH, SEQ, D_MODEL), f32),
        "positions": jnp.tile(jnp.arange(SEQ, dtype=jnp.int32)[None, :], (BATCH, 1)),
        "ffn1_w_in": nrm(ks[1], (L, D_MODEL, 2 * D_FF), D_MODEL),
        "ffn1_w_out": nrm(ks[2], (L, D_FF, D_MODEL), D_FF, DN_BETA),
        "ln1_g": gain(ks[3]),
        "ln1_b": bias(ks[4], D_MODEL),
        "w_in": nrm(ks[5], (L, D_MODEL, IN_TOTAL), D_MODEL),
        "conv_w": nrm(ks[6], (L, CONV_WIDTH, D_RNN), CONV_WIDTH),
        "conv_b": bias(ks[7], D_RNN),
        "lru_wa": nrm(ks[8], (L, LRU_BLOCKS, LRU_BLOCK, LRU_BLOCK), LRU_BLOCK),
        "lru_ba": bias(ks[9], D_RNN),
        "lru_wx": nrm(ks[10], (L, LRU_BLOCKS, LRU_BLOCK, LRU_BLOCK), LRU_BLOCK),
        "lru_bx": bias(ks[11], D_RNN),
        "lru_lambda": lru_lambda,
        "w_attn_branch": nrm(ks[12], (L, N_HEADS * HEAD_DIM, D_MODEL), N_HEADS * HEAD_DIM),
        "w_rnn_branch": nrm(ks[14], (L, D_RNN, D_MODEL), D_RNN),
        "w_out": nrm(ks[15], (L, D_MODEL, D_MODEL), D_MODEL, DN_BETA),
        "ln2_g": gain(ks[16]),
        "ln2_b": bias(ks[17], D_MODEL),
        "ffn2_w_in": nrm(ks[18], (L, D_MODEL, 2 * D_FF), D_MODEL),
        "ffn2_w_out": nrm(ks[19], (L, D_FF, D_MODEL), D_FF, DN_BETA),
        "ln3_g": gain(ks[20]),
        "ln3_b": bias(ks[21], D_MODEL),
    }


def reference(x, positions, ffn1_w_in, ffn1_w_out, ln1_g, ln1_b, w_in, conv_w, conv_b,
              lru_wa, lru_ba, lru_wx, lru_bx, lru_lambda, w_attn_branch, w_rnn_branch, w_out,
              ln2_g, ln2_b, ffn2_w_in, ffn2_w_out, ln3_g, ln3_b):
    B, S, _ = x.shape
    h = x
    for l in range(DEPTH):
        h = layer_norm(DN_ALPHA * h + 0.5 * swiglu(h, ffn1_w_in[l], ffn1_w_out[l]), ln1_g[l], ln1_b[l])

        q, k, v, qi, ki, wi, rx, rg, gates = split_cols(h @ w_in[l])
        q = rope(q.reshape(B, S, N_HEADS, HEAD_DIM), positions, ROPE_DIM)
        k = rope(k.reshape(B, S, N_KV_HEADS, HEAD_DIM), positions, ROPE_DIM)
        v = v.reshape(B, S, N_KV_HEADS, HEAD_DIM)
        qi = rope(qi.reshape(B, S, N_IDX_HEADS, IDX_DIM), positions, IDX_ROPE_DIM)
        ki = rope(ki.reshape(B, S, 1, IDX_DIM), positions, IDX_ROPE_DIM)[:, :, 0, :]
        wi = wi * (N_IDX_HEADS ** -0.5 * IDX_DIM ** -0.5)
        y_attn = dsa_attention(q, k, v, qi, ki, wi)
        y_rnn = rglru_branch(rx, rg, conv_w[l], conv_b[l], lru_wa[l], lru_ba[l],
                             lru_wx[l], lru_bx[l], lru_lambda[l])
        g_attn, g_rnn = jnp.split(jax.nn.sigmoid(gates), 2, axis=-1)
        merged = g_attn * (y_attn @ w_attn_branch[l]) + g_rnn * (y_rnn @ w_rnn_branch[l])
        h = layer_norm(DN_ALPHA * h + merged @ w_out[l], ln2_g[l], ln2_b[l])

        h = layer_norm(DN_ALPHA * h + 0.5 * swiglu(h, ffn2_w_in[l], ffn2_w_out[l]), ln3_g[l], ln3_b[l])
    return h
```

```python
import contextlib
import os
import math
import numpy as np
import concourse.bass as bass
import concourse.mybir as mybir
from concourse.bass_utils import run_bass_kernel_spmd

F32 = mybir.dt.float32
BF16 = mybir.dt.bfloat16
I32 = mybir.dt.int32
AF = mybir.ActivationFunctionType
ALU = mybir.AluOpType
AX = mybir.AxisListType

D = 2048
DC = 16
T = 1024
NT = 8
FF = 5632
FC = 44
DRNN = 2560
RC = 20
ALPHA = 2.0 ** 0.25
LN_EPS = 1e-5
SCALE = 128 ** -0.5
WI_SCALE = 1.0 / 32.0
NEG = -1.0e30
ENGINES = ("pe", "act", "dve", "pool", "sp")
EPOCH = 12000


class Prog:
    def __init__(self, nc):
        self.nc = nc
        self.ops = []

    def op(self, eng, fn, reads=(), writes=(), phase=True):
        r = tuple(reads) + (("PHASE",) if phase else ())
        self.ops.append(dict(eng=eng, fn=fn, reads=r, writes=tuple(writes), dma=None))

    def dma(self, eng, fn, reads=(), writes=(), tag=None, phase=True):
        r = tuple(reads) + (("PHASE",) if phase else ())
        self.ops.append(dict(eng=eng, fn=fn, reads=r, writes=tuple(writes), dma=tag))

    def emit(self, final_wait_tags=()):
        nc = self.nc
        ops = self.ops
        n = len(ops)
        last_write = {}
        readers = {}
        need = [None] * n
        signal = [False] * n
        for i, o in enumerate(ops):
            d = set()
            for r in o["reads"]:
                lw = last_write.get(r)
                if lw is not None:
                    d.add(lw)
            for w in o["writes"]:
                lw = last_write.get(w)
                if lw is not None:
                    d.add(lw)
                rd = readers.get(w)
                if rd:
                    d.update(rd[0].values())
                    d.update(rd[1])
            d.discard(i)
            lst = []
            for j in d:
                oj = ops[j]
                if oj["dma"] is None:
                    if oj["eng"] == o["eng"] and oj["eng"] == "pe" and o["dma"] is None:
                        continue
                    signal[j] = True
                lst.append(j)
            need[i] = lst
            for r in o["reads"]:
                rd = readers.setdefault(r, ({}, []))
                if o["dma"] is None:
                    rd[0][o["eng"]] = i
                else:
                    rd[1].append(i)
            for w in o["writes"]:
                last_write[w] = i
                readers[w] = ({}, [])
        eng_count = {e: 0 for e in ENGINES}
        tag_count = {}
        semkey = [None] * n
        val = [0] * n
        keys = []
        for i, o in enumerate(ops):
            if o["dma"] is not None:
                t = o["dma"]
                c = tag_count.get(t, 0) + 16
                tag_count[t] = c
                k = ("t", t, c // 32000)
                val[i] = c - (c // 32000) * 32000 if c // 32000 else c
                assert c < 32000, t
                k = ("t", t, 0)
                val[i] = c
            elif signal[i]:
                eng_count[o["eng"]] += 1
                c = eng_count[o["eng"]]
                ep = (c - 1) // EPOCH
                k = ("e", o["eng"], ep)
                val[i] = c - ep * EPOCH
            else:
                continue
            semkey[i] = k
            if k not in keys:
                keys.append(k)
        self.stats = dict(n_ops=n, eng_count=dict(eng_count), n_sems=len(keys))
        with contextlib.ExitStack() as st:
            sems = {}
            for idx, k in enumerate(keys):
                sems[k] = st.enter_context(nc.semaphore("sm%d" % idx))
            block = st.enter_context(nc.Block())
            per_eng = {e: [i for i, o in enumerate(ops) if o["eng"] == e] for e in ENGINES}

            def run(ename, eng):
                known = {}
                for i in per_eng[ename]:
                    o = ops[i]
                    waits = {}
                    for j in need[i]:
                        k = semkey[j]
                        if val[j] > waits.get(k, 0):
                            waits[k] = val[j]
                    for k, v in waits.items():
                        if known.get(k, 0) >= v:
                            continue
                        if k[0] == "e":
                            later = [kk for kk in known if kk[0] == "e" and kk[1] == k[1] and kk[2] > k[2]]
                            if later:
                                continue
                        known[k] = v
                        eng.wait_ge(sems[k], v)
                    ins = o["fn"](eng)
                    if o["dma"] is not None:
                        ins.then_inc(sems[semkey[i]], 16)
                    elif signal[i]:
                        ins.then_inc(sems[semkey[i]], 1)
                if ename == "sp":
                    for t in final_wait_tags:
                        eng.wait_ge(sems[("t", t, 0)], tag_count[t])

            @block.sync
            def _(e):
                run("sp", e)

            @block.scalar
            def _(e):
                run("act", e)

            @block.vector
            def _(e):
                run("dve", e)

            @block.gpsimd
            def _(e):
                run("pool", e)

            @block.tensor
            def _(e):
                run("pe", e)


def w3_slots():
    sl = []
    for s in range(8):
        sl.append((256 * s, 256))
    for s in range(2):
        sl.append((2048 + 256 * s, 256))
    for s in range(2):
        sl.append((2560 + 256 * s, 256))
    for s in range(4):
        sl.append((3072 + 256 * s, 256))
    sl.append((4096, 80))
    for s in range(10):
        sl.append((4176 + 256 * s, 256))
    for s in range(10):
        sl.append((6736 + 256 * s, 256))
    for s in range(16):
        sl.append((9296 + 256 * s, 256))
    return sl


S_Q, S_K, S_V, S_QI, S_KIWI, S_RX, S_RG, S_G = 0, 8, 10, 12, 16, 17, 27, 37


def gate_pairs():
    pairs = []
    for i in range(RC):
        for j in range(RC):
            hit = False
            for nb in range(16):
                lo, hi = nb * 160, (nb + 1) * 160
                if max(lo, j * 128) < min(hi, (j + 1) * 128) and max(lo, i * 128) < min(hi, (i + 1) * 128):
                    hit = True
            if hit:
                pairs.append((j, i))
    return pairs


ARENA_BYTES = 172032
OFF_ACTT = 0
OFF_Q = 32768
OFF_R = 65536
OFF_KT = 65536
OFF_V = 81920
OFF_QIT = 98304
OFF_KIT = 114688
OFF_T = 122880
NSLOT = 3
SLOT_ELEMS = 4096


XCOPY = {}


def build(upto="all"):
    nc = bass.Bass("TRN2", target_bir_lowering=False)
    P = Prog(nc)

    def din(name, shape, dt=F32):
        return nc.dram_tensor(name, shape, dt, kind="ExternalInput").ap()

    x2 = din("x2", [2048, 2048])
    posi = din("posi", [128, 16], I32)
    invf = din("invf", [128, 24])
    ident = din("ident", [128, 128])
    cbias = din("cbias", [1024, 2048])
    flag = din("flag", [128, 1])
    w1a = din("w1a", [FC, 128, 4096])
    w1b = din("w1b", [FC, 128, 2048])
    w2a = din("w2a", [FC, 128, 4096])
    w2b = din("w2b", [FC, 128, 2048])
    w3 = din("w3", [53, 128, 4096])
    wab = din("wab", [8, 128, 4096])
    wrb = din("wrb", [16, 128, 2560])
    wo = din("wo", [16, 128, 2048])
    pairs = gate_pairs()
    NP = len(pairs)
    wga = din("wga", [128, NP * 128])
    wgx = din("wgx", [128, NP * 128])
    lnp = din("lnp", [6, 2048])
    rnnp = din("rnnp", [128, 8 * RC])
    out = nc.dram_tensor("out", [1024, 2048], F32, kind="ExternalOutput").ap()
    ypre = nc.dram_tensor("ypre", [1024, 2048], F32).ap()
    h1d = nc.dram_tensor("h1d", [1024, 2048], F32).ap()
    h2d = nc.dram_tensor("h2d", [1024, 2048], F32).ap()
    kTc_d = nc.dram_tensor("kTc_d", [128, 4096], BF16).ap()
    Vc_d = nc.dram_tensor("Vc_d", [128, 4096], BF16).ap()
    kiTc_d = nc.dram_tensor("kiTc_d", [128, 1024], BF16).ap()
    dbg_out = None
    if upto != "all":
        dbg_out = nc.dram_tensor("dbg", [4096, 2048], F32, kind="ExternalOutput").ap()

    final_tags = []
    with contextlib.ExitStack() as st:
        def sb(name, shape, dt):
            return st.enter_context(nc.sbuf_tensor(name, shape, dt))

        arena = sb("arena", [128, ARENA_BYTES // 4], F32)
        ring = [sb("ring%d" % i, [128, SLOT_ELEMS], BF16) for i in range(NSLOT)]
        ps = st.enter_context(nc.psum_tensor("ps", [128, 4096], F32))
        identb = sb("identb", [128, 128], BF16)
        posf = sb("posf", [128, 16], F32)
        posi_s = sb("posi_s", [128, 16], I32)
        invf_s = sb("invf_s", [128, 24], F32)
        flag_s = sb("flag_s", [128, 1], F32)
        rnnp_s = sb("rnnp_s", [128, 8 * RC], F32)
        c1_s = sb("c1_s", [128, RC], F32)
        c2_s = sb("c2_s", [128, RC], F32)
        hlast = sb("hlast", [128, RC], F32)
        rxhalo = sb("rxhalo", [128, RC, 4], F32)
        stats = sb("stats", [128, NT, 4, 6], F32)
        mv = sb("mv", [128, NT, 2], F32)
        sd = sb("sd", [128, NT], F32)
        rstd = sb("rstd", [128, NT], F32)
        epst = sb("epst", [128, 1], F32)
        witm = sb("witm", [128, NT, 16], F32)
        m8 = sb("m8", [128, 8], F32)
        thr = sb("thr", [128, 1], F32)
        mx = sb("mx", [128, 2], F32)
        rs = sb("rs", [128, 2], F32)
        rinv = sb("rinv", [128, 2], F32)
        bar_t = sb("bar_t", [128, 4], F32)

        def carve(off, shape, dt):
            esz = 2 if dt == BF16 else 4
            nel = 1
            for s_ in shape[1:]:
                nel *= s_
            nb = nel * esz
            assert off % 4 == 0 and nb % 4 == 0 and off + nb <= ARENA_BYTES, (off, nb)
            v = arena[:, off // 4:(off + nb) // 4]
            if dt == BF16:
                v = v.bitcast(BF16)
            elif dt == I32:
                v = v.bitcast(I32)
            if len(shape) == 3:
                v = v.rearrange("p (a b) -> p a b", b=shape[2])
            return v

        def bank(b, n=512, off=0):
            return ps[:, b * 512 + off:b * 512 + off + n]

        def bankbf(b, nb=1):
            return ps[:, b * 512:(b + nb) * 512].bitcast(BF16)

        barrier_n = [0]

        def barrier():
            barrier_n[0] += 1
            P.op("dve", lambda e: e.memset(bar_t[:, 0:1], 0.0), reads=[], writes=["PHASE"], phase=False)

        ring_i = [0]

        def ring_load(src, nel):
            s = ring_i[0] % NSLOT
            ring_i[0] += 1
            k = 1
            while nel // k > 2048 or nel % k:
                k += 1
            dst = ring[s][:, 0:nel].rearrange("p (a b) -> p a b", a=k)
            srcv = src.rearrange("p (a b) -> p a b", a=k)
            P.dma("pool", lambda e: e.dma_start(out=dst, in_=srcv), writes=[("slot", s)],
                  tag=("slot", s), phase=False)
            return s

        P.dma("pool", lambda e: e.dma_start(out=identb[:], in_=ident), writes=["identb"], tag="c_ident")
        P.dma("sp", lambda e: e.dma_start(out=posi_s[:], in_=posi), writes=["posi_s"], tag="c_posi")
        P.dma("sp", lambda e: e.dma_start(out=invf_s[:], in_=invf), writes=["invf_s"], tag="c_invf")
        P.dma("sp", lambda e: e.dma_start(out=flag_s[:], in_=flag), writes=["flag_s"], tag="c_flag")
        P.dma("sp", lambda e: e.dma_start(out=rnnp_s[:], in_=rnnp), writes=["rnnp_s"], tag="c_rnnp")
        P.op("dve", lambda e: e.memset(epst[:], LN_EPS), writes=["epst"])
        P.op("dve", lambda e: e.memset(hlast[:], 0.0), writes=["hlast"])
        P.op("dve", lambda e: e.memset(rxhalo[:], 0.0), writes=["rxhalo"])
        P.op("dve", lambda e: e.tensor_copy(out=posf[:], in_=posi_s[:]), reads=["posi_s"], writes=["posf"])

        LAM = 7 * RC
        P.op("act", lambda e: e.activation(out=c1_s[:], in_=rnnp_s[:, LAM:LAM + RC], func=AF.Exp, scale=-1.0),
             reads=["rnnp_s"], writes=["c1_s"])
        P.op("dve", lambda e: e.tensor_scalar(out=c1_s[:], in0=c1_s[:], scalar1=1.0, scalar2=None, op0=ALU.add),
             reads=["c1_s"], writes=["c1_s"])
        P.op("act", lambda e: e.activation(out=c1_s[:], in_=c1_s[:], func=AF.Ln),
             reads=["c1_s"], writes=["c1_s"])
        P.op("dve", lambda e: e.tensor_scalar(out=c2_s[:], in0=c1_s[:], scalar1=-16.0, scalar2=None, op0=ALU.mult),
             reads=["c1_s"], writes=["c2_s"])
        P.op("dve", lambda e: e.tensor_scalar(out=c1_s[:], in0=c1_s[:], scalar1=-8.0, scalar2=None, op0=ALU.mult),
             reads=["c1_s"], writes=["c1_s"])

        identf = sb("identf", [128, 128], F32)
        onest = sb("onest", [128, 1], F32)
        P.op("dve", lambda e: e.memset(onest[:], 1.0), writes=["onest"])
        nmr = sb("nmr", [128, NT], F32)
        P.dma("sp", lambda e: e.dma_start(out=identf[:], in_=ident), writes=["identf"], tag="c_identf")
        cosq_r = sb("cosq_r", [128, 16, 32], F32)
        sinq_r = sb("sinq_r", [128, 16, 32], F32)
        cosi_r = sb("cosi_r", [128, 16, 32], F32)
        sini_r = sb("sini_r", [128, 16, 32], F32)
        trig_t = carve(OFF_T + 45056, [128, 16, 16], F32)
        trig_k = carve(OFF_T + 46080, [128, 16, 16], F32)
        trig_ki = carve(OFF_T + 47104, [128, 16, 16], I32)

        def trig_table(dst, ncol, col0, shift, nrep):
            tv = trig_t[:, :, 0:ncol]
            kv = trig_k[:, :, 0:ncol]
            kiv = trig_ki[:, :, 0:ncol]
            for gt in range(16):
                P.op("dve", lambda e, gt=gt: e.tensor_scalar(
                    out=trig_t[:, gt, 0:ncol], in0=invf_s[:, col0:col0 + ncol], scalar1=posf[:, gt:gt + 1],
                    scalar2=shift, op0=ALU.mult, op1=ALU.add),
                    reads=["posf", "invf_s", "trig_t"], writes=["trig_t"])
            two_pi = 2.0 * math.pi
            P.op("dve", lambda e: e.tensor_scalar(out=kv, in0=tv, scalar1=1.0 / two_pi, scalar2=None, op0=ALU.mult),
                 reads=["trig_t"], writes=["trig_k"])
            P.op("dve", lambda e: e.tensor_copy(out=kiv, in_=kv), reads=["trig_k"], writes=["trig_ki"])
            P.op("dve", lambda e: e.tensor_copy(out=kv, in_=kiv), reads=["trig_ki"], writes=["trig_k"])
            P.op("dve", lambda e: e.scalar_tensor_tensor(out=tv, in0=kv, scalar=-two_pi, in1=tv,
                                                         op0=ALU.mult, op1=ALU.add),
                 reads=["trig_k", "trig_t"], writes=["trig_t"])
            P.op("dve", lambda e: e.tensor_scalar(out=kv, in0=tv, scalar1=math.pi, scalar2=-two_pi,
                                                  op0=ALU.is_gt, op1=ALU.mult),
                 reads=["trig_t"], writes=["trig_k"])
            P.op("dve", lambda e: e.tensor_tensor(out=tv, in0=tv, in1=kv, op=ALU.add),
                 reads=["trig_t", "trig_k"], writes=["trig_t"])
            P.op("dve", lambda e: e.tensor_scalar(out=kv, in0=tv, scalar1=-math.pi, scalar2=two_pi,
                                                  op0=ALU.is_lt, op1=ALU.mult),
                 reads=["trig_t"], writes=["trig_k"])
            P.op("dve", lambda e: e.tensor_tensor(out=tv, in0=tv, in1=kv, op=ALU.add),
                 reads=["trig_t", "trig_k"], writes=["trig_t"])
            P.op("dve", lambda e: e.tensor_scalar(out=tv, in0=tv, scalar1=3.14159, scalar2=-3.14159,
                                                  op0=ALU.min, op1=ALU.max),
                 reads=["trig_t"], writes=["trig_t"])
            for r_ in range(nrep):
                P.op("act", lambda e, r_=r_: e.activation(out=dst[:, :, r_ * ncol:(r_ + 1) * ncol], in_=tv, func=AF.Sin),
                     reads=["trig_t"], writes=["trig_rep"])

        trig_table(sinq_r, 16, 0, 0.0, 2)
        trig_table(cosq_r, 16, 0, math.pi / 2, 2)
        trig_table(sini_r, 8, 16, 0.0, 4)
        trig_table(cosi_r, 8, 16, math.pi / 2, 4)
        barrier()


        actT = carve(OFF_ACTT, [128, DC, T], BF16)
        bufQ = carve(OFF_Q, [128, 16, T], BF16)
        yrT = carve(OFF_R, [128, RC, T], BF16)
        kT = carve(OFF_KT, [128, 4, 2048], BF16)
        Vt = carve(OFF_V, [128, 16, 512], BF16)
        qiT = carve(OFF_QIT, [128, 8, T], BF16)
        kiT = carve(OFF_KIT, [128, 2048], BF16)
        gT = carve(OFF_Q, [128, FC, T], BF16)

        dump_row = [0]

        def dump(src, nrows_p, ncols, reads, is_bf16=False, src_dram=False):
            if dbg_out is None:
                return
            r0 = dump_row[0]
            dump_row[0] += 128
            tag = "dump%d" % r0
            dst = dbg_out[r0:r0 + nrows_p, 0:ncols]
            if len(src.shape) == 3:
                dst = dst.rearrange("p (a b) -> p a b", b=src.shape[2])
            q = "pool" if is_bf16 else "sp"
            P.dma(q, lambda e: e.dma_start(out=dst, in_=src), reads=reads, writes=[tag], tag=tag)
            final_tags.append(tag)
            return r0

        tr_rr = [0]

        def transpose_cols(src_f32, ncol, dst_fn, src_res, dst_res, banks=(4, 5, 6, 7), eng_pref=None):
            nch = ncol // 128
            c = 0
            while c < nch:
                n = min(4, nch - c)
                b = banks[tr_rr[0] % len(banks)]
                tr_rr[0] += 1
                for i in range(n):
                    P.op("pe", lambda e, i=i, c=c, b=b: e.transpose(
                        bank(b, 128, i * 128), src_f32[:, (c + i) * 128:(c + i + 1) * 128], identf[:]),
                        reads=list(src_res) + ["identf"], writes=[("ps", b)])
                dst = dst_fn(c, n)
                srcv = bank(b, n * 128).rearrange("p (a b) -> p a b", b=128)
                eng = eng_pref or ("act" if tr_rr[0] % 2 else "dve")
                if eng == "act":
                    P.op("act", lambda e, dst=dst, srcv=srcv: e.activation(out=dst, in_=srcv, func=AF.Copy),
                         reads=[("ps", b)], writes=list(dst_res(c, n)))
                else:
                    P.op("dve", lambda e, dst=dst, srcv=srcv: e.tensor_copy(out=dst, in_=srcv),
                         reads=[("ps", b)], writes=list(dst_res(c, n)))
                c += n

        LB = {}

        def set_ln_bufs(off):
            o_ = off
            LB["yfull"] = carve(o_, [128, 2048], F32); o_ += 8192
            LB["gB"] = carve(o_, [128, 2048], F32); o_ += 8192
            LB["bB"] = carve(o_, [128, 2048], F32); o_ += 8192
            LB["rt"] = []
            LB["yt"] = []
            for k in range(2):
                LB["rt"].append(carve(o_, [128, 512], F32)); o_ += 2048
            for k in range(2):
                LB["yt"].append(carve(o_, [128, 512], F32)); o_ += 2048
            return o_

        cnt = dict(sa=0, rt=0, stg=0, pj=0, rb=0)

        def load_xT(row0, xf1):
            xf = [LB["yfull"], xf1]
            for t in range(NT):
                k = t % 2
                P.dma("sp", lambda e, t=t, k=k: e.dma_start(out=xf[k], in_=x2[row0 + t * 128:row0 + (t + 1) * 128, :]),
                      writes=[("xf", k)] + ([("yfull",)] if k == 0 else []), tag=("xf", k))
                transpose_cols(xf[k], 2048, lambda c, n, t=t: actT[:, c:c + n, t * 128:(t + 1) * 128],
                               [("xf", k)] + ([("yfull",)] if k == 0 else []), lambda c, n, t=t: [("actT", cc, t) for cc in range(c, c + n)])

        def tok_out_stage(lhs_fn, lhs_res, nk, wsrc, resid, resid_res):
            nl = nk // 4
            for cb in range(4):
                s4 = 0
                while s4 < nl:
                    na = min(2, nl - s4)
                    s = ring_i[0] % NSLOT
                    ring_i[0] += 1
                    dstv = ring[s][:, 0:na * 2048].rearrange("p (a b) -> p a b", a=na)
                    srcv = wsrc[cb * nl + s4:cb * nl + s4 + na].rearrange("a p n -> p a n")
                    P.dma("pool", lambda e, dstv=dstv, srcv=srcv: e.dma_start(out=dstv, in_=srcv),
                          writes=[("slot", s)], tag=("slot", s), phase=False)
                    for t in range(NT):
                        for a_ in range(na):
                            for j in range(4):
                                kc = (s4 + a_) * 4 + j
                                P.op("pe", lambda e, t=t, kc=kc, j=j, s=s, a_=a_: e.matmul(
                                    bank(t), lhsT=lhs_fn(kc, t),
                                    rhs=ring[s][:, a_ * 2048 + j * 512:a_ * 2048 + (j + 1) * 512],
                                    start=(kc == 0), stop=(kc == nk - 1)),
                                    reads=[("slot", s)] + lhs_res(kc, t), writes=[("ps", t)])
                    s4 += na
                for t in range(NT):
                    k = cnt["rt"] % 2
                    cnt["rt"] += 1
                    rt, yt = LB["rt"], LB["yt"]
                    P.dma("sp", lambda e, k=k, t=t, cb=cb, rt=rt: e.dma_start(
                        out=rt[k], in_=resid[t * 128:(t + 1) * 128, cb * 512:(cb + 1) * 512]),
                        reads=[(resid_res, t)], writes=[("rt", k)], tag=("rt", k))
                    P.op("dve", lambda e, k=k, t=t, rt=rt, yt=yt: e.scalar_tensor_tensor(
                        out=yt[k], in0=rt[k], scalar=ALPHA, in1=bank(t), op0=ALU.mult, op1=ALU.add),
                        reads=[("rt", k), ("ps", t)], writes=[("yt", k)])
                    P.op("dve", lambda e, k=k, t=t, cb=cb, yt=yt: e.bn_stats(out=stats[:, t, cb, :], in_=yt[k]),
                         reads=[("yt", k)], writes=[("stats", t)])
                    P.dma("sp", lambda e, k=k, t=t, cb=cb, yt=yt: e.dma_start(
                        out=ypre[t * 128:(t + 1) * 128, cb * 512:(cb + 1) * 512], in_=yt[k]),
                        reads=[("yt", k)], writes=[("ypre", t)], tag=("yts", k))

        def layer_norm_pass(ln_idx, dst_dram, dst_res, want_T):
            yfs = [LB["yfull"], LB["yfull2"]]
            yks = [("yfull",), LB["yfull2_key"]]
            gB, bB = LB["gB"], LB["bB"]
            P.dma("sp", lambda e: e.dma_start(out=gB, in_=lnp[2 * ln_idx].partition_broadcast(128)),
                  writes=["gB"], tag="gB")
            P.dma("sp", lambda e: e.dma_start(out=bB, in_=lnp[2 * ln_idx + 1].partition_broadcast(128)),
                  writes=["bB"], tag="bB")
            for t in range(NT):
                yf, yk = yfs[t % 2], yks[t % 2]
                P.op("dve", lambda e, t=t: e.bn_aggr(out=mv[:, t, :], in_=stats[:, t, :, :].rearrange("p a b -> p (a b)")),
                     reads=[("stats", t)], writes=[("mv", t)])
                P.op("act", lambda e, t=t: e.activation(out=sd[:, t:t + 1], in_=mv[:, t, 1:2], func=AF.Sqrt,
                                                        bias=epst[:, 0:1], scale=1.0),
                     reads=[("mv", t), "epst"], writes=[("sd", t)])
                P.op("dve", lambda e, t=t: e.reciprocal(out=rstd[:, t:t + 1], in_=sd[:, t:t + 1]),
                     reads=[("sd", t)], writes=[("rstd", t)])
                P.op("dve", lambda e, t=t: e.tensor_scalar(out=nmr[:, t:t + 1], in0=mv[:, t, 0:1], scalar1=rstd[:, t:t + 1],
                                                           scalar2=-1.0, op0=ALU.mult, op1=ALU.mult),
                     reads=[("mv", t), ("rstd", t)], writes=[("nmr", t)])
                P.dma("sp", lambda e, t=t, yf=yf: e.dma_start(out=yf, in_=ypre[t * 128:(t + 1) * 128, :]),
                      reads=[("ypre", t)], writes=[yk], tag=("yfl", t % 2))
                P.op("act", lambda e, t=t, yf=yf: e.activation(out=yf, in_=yf, func=AF.Identity,
                                                               scale=rstd[:, t:t + 1], bias=nmr[:, t:t + 1]),
                     reads=[yk, ("nmr", t), ("rstd", t)], writes=[yk])
                P.op("dve", lambda e, yf=yf: e.tensor_tensor(out=yf, in0=yf, in1=gB, op=ALU.mult),
                     reads=[yk, "gB"], writes=[yk])
                P.op("dve", lambda e, yf=yf: e.tensor_tensor(out=yf, in0=yf, in1=bB, op=ALU.add),
                     reads=[yk, "bB"], writes=[yk])
                if dst_dram is not None:
                    P.dma("sp", lambda e, t=t, yf=yf: e.dma_start(out=dst_dram[t * 128:(t + 1) * 128, :], in_=yf),
                          reads=[yk], writes=[(dst_res, t)], tag=("hst_" + dst_res, t % 2))
                if want_T:
                    transpose_cols(yf, 2048, lambda c, n, t=t: actT[:, c:c + n, t * 128:(t + 1) * 128],
                                   [yk], lambda c, n, t=t: [("actT", cc, t) for cc in range(c, c + n)])

        def ffn(resid, resid_res, wa, wb, ln_idx, dst_dram, dst_res, want_T, sa):
            for fc in range(FC):
                s = ring_load(wa[fc], 4096)
                base = (fc % 2) * 4
                for ab in range(2):
                    for n_ in range(2):
                        b = base + ab * 2 + n_
                        for dc in range(DC):
                            P.op("pe", lambda e, b=b, ab=ab, dc=dc, n_=n_, s=s: e.matmul(
                                bank(b), lhsT=ring[s][:, (ab * 16 + dc) * 128:(ab * 16 + dc + 1) * 128],
                                rhs=actT[:, dc, n_ * 512:(n_ + 1) * 512], start=(dc == 0), stop=(dc == DC - 1)),
                                reads=[("slot", s)] + [("actT", dc, tt) for tt in range(n_ * 4, n_ * 4 + 4)],
                                writes=[("ps", b)])
                for n_ in range(2):
                    k = cnt["sa"] % 2
                    cnt["sa"] += 1
                    ba, bb = base + n_, base + 2 + n_
                    P.op("act", lambda e, k=k, ba=ba: e.activation(out=sa[k], in_=bank(ba), func=AF.Silu),
                         reads=[("ps", ba)], writes=[("sa", k)])
                    P.op("dve", lambda e, k=k, bb=bb, fc=fc, n_=n_: e.scalar_tensor_tensor(
                        out=gT[:, fc, n_ * 512:(n_ + 1) * 512], in0=sa[k], scalar=0.5, in1=bank(bb),
                        op0=ALU.mult, op1=ALU.mult),
                        reads=[("sa", k), ("ps", bb)], writes=[("gT", fc, n_)])
            tok_out_stage(lambda kc, t: gT[:, kc, t * 128:(t + 1) * 128],
                          lambda kc, t: [("gT", kc, t // 4)], FC, wb, resid, resid_res)
            layer_norm_pass(ln_idx, dst_dram, dst_res, want_T)

        def ffn_phase_bufs():
            o_ = set_ln_bufs(OFF_T)
            sa = []
            for k in range(2):
                sa.append(carve(o_, [128, 512], F32)); o_ += 2048
            xf1 = carve(o_, [128, 2048], F32); o_ += 8192
            LB["yfull2"] = xf1
            LB["yfull2_key"] = ("xf", 1)
            return sa, xf1

        pend = [None]

        def proj_tok(slot_idx, ncols, handler, PB):
            s = ring_load(w3[slot_idx], 4096)
            for t in range(NT):
                b = cnt["pj"] % 4
                cnt["pj"] += 1
                for dc in range(DC):
                    P.op("pe", lambda e, b=b, dc=dc, t=t, s=s: e.matmul(
                        bank(b, ncols), lhsT=actT[:, dc, t * 128:(t + 1) * 128],
                        rhs=ring[s][:, dc * 256:dc * 256 + ncols], start=(dc == 0), stop=(dc == DC - 1)),
                        reads=[("slot", s), ("actT", dc, t)], writes=[("ps", b)])
                if pend[0] is not None:
                    pend[0]()
                pend[0] = handler(t, b)
            if pend[0] is not None:
                pend[0]()
                pend[0] = None

        def rope_to_stg(b, stg, k, nh, hd, half, cos_r, sin_r, gt, tmps, ncopy=None):
            ncol = nh * hd
            ncp = ncopy or ncol
            P.op("act", lambda e: e.activation(out=stg[k][:, 0:ncp], in_=bank(b, ncp), func=AF.Copy),
                 reads=[("ps", b)], writes=[("stg", k)])
            if "rope" in os.environ.get("K_SKIP", ""):
                return
            s3 = stg[k][:, 0:ncol].rearrange("p (h d) -> p h d", h=nh)
            t1 = s3[:, :, 0:half]
            t2 = s3[:, :, half:2 * half]
            w = nh * half
            cosb = cos_r[:, gt, 0:w].rearrange("p (h d) -> p h d", h=nh)
            sinb = sin_r[:, gt, 0:w].rearrange("p (h d) -> p h d", h=nh)
            m = [tmps[i][:, 0:w].rearrange("p (h d) -> p h d", h=nh) for i in range(4)]
            for i, (a_, b_) in enumerate(((t1, cosb), (t2, sinb), (t2, cosb), (t1, sinb))):
                P.op("dve", lambda e, i=i, a_=a_, b_=b_: e.tensor_tensor(out=m[i], in0=a_, in1=b_, op=ALU.mult),
                     reads=[("stg", k), "trig_rep"], writes=[("ropetmp", i)])
            P.op("dve", lambda e: e.tensor_tensor(out=t1, in0=m[0], in1=m[1], op=ALU.subtract),
                 reads=[("ropetmp", 0), ("ropetmp", 1)], writes=[("stg", k)])
            P.op("dve", lambda e: e.tensor_tensor(out=t2, in0=m[2], in1=m[3], op=ALU.add),
                 reads=[("ropetmp", 2), ("ropetmp", 3)], writes=[("stg", k)])

        def proj_group(g, PB, kT_dst, V_dst, kiT_dst, key0):
            stg, tmps = PB["stg"], PB["tmps"]

            def nxt():
                k = cnt["stg"] % 2
                cnt["stg"] += 1
                return k

            if g == 1:
                for sq in range(8):
                    def hq(t, b, sq=sq):
                        k = nxt()
                        rope_to_stg(b, stg, k, 2, 128, 16, cosq_r, sinq_r, g * 8 + t, tmps)
                        return lambda: transpose_cols(stg[k], 256,
                                       lambda c, n, t=t: bufQ[:, 2 * sq + c:2 * sq + c + n, t * 128:(t + 1) * 128],
                                       [("stg", k)], lambda c, n, t=t: [("bufQ", 2 * sq + cc, t) for cc in range(c, c + n)])
                    proj_tok(S_Q + sq, 256, hq, PB)
            SK_ = os.environ.get("K_SKIP", "").split(",")
            for sk in range(0 if "nok" in SK_ else 2):
                def hk(t, b, sk=sk):
                    k = nxt()
                    rope_to_stg(b, stg, k, 2, 128, 16, cosq_r, sinq_r, g * 8 + t, tmps)
                    return lambda: transpose_cols(stg[k], 256,
                                   lambda c, n, t=t: kT_dst[:, 2 * sk + c:2 * sk + c + n, key0 + t * 128:key0 + (t + 1) * 128],
                                   [("stg", k)], lambda c, n, t=t: [("kT", g, 2 * sk + cc, t) for cc in range(c, c + n)])
                proj_tok(S_K + sk, 256, hk, PB)
            for sv_ in range(0 if "nov" in SK_ else 2):
                def hv(t, b, sv_=sv_):
                    dstv = V_dst[:, key0 // 128 + t, sv_ * 256:(sv_ + 1) * 256]
                    P.op("act", lambda e: e.activation(out=dstv, in_=bank(b, 256), func=AF.Copy),
                         reads=[("ps", b)], writes=[("V", g, t, sv_)])
                proj_tok(S_V + sv_, 256, hv, PB)
            if g == 1:
                for sq in range(4):
                    def hqi(t, b, sq=sq):
                        k = nxt()
                        rope_to_stg(b, stg, k, 4, 64, 8, cosi_r, sini_r, g * 8 + t, tmps)
                        return lambda: transpose_cols(stg[k], 256,
                                       lambda c, n, t=t: qiT[:, 2 * sq + c:2 * sq + c + n, t * 128:(t + 1) * 128],
                                       [("stg", k)], lambda c, n, t=t: [("qiT", 2 * sq + cc, t) for cc in range(c, c + n)])
                    proj_tok(S_QI + sq, 256, hqi, PB)

            def hki(t, b):
                k = nxt()
                rope_to_stg(b, stg, k, 1, 64, 8, cosi_r, sini_r, g * 8 + t, tmps, ncopy=128)
                if g == 1:
                    P.op("dve", lambda e: e.tensor_scalar(out=witm[:, t, :], in0=stg[k][:, 64:80], scalar1=WI_SCALE,
                                                          scalar2=None, op0=ALU.mult),
                         reads=[("stg", k)], writes=[("witm", t)])
                P.op("dve", lambda e: e.tensor_copy(out=stg[k][:, 64:128], in_=stg[k][:, 0:64]),
                     reads=[("stg", k)], writes=[("stg", k)])
                return lambda: transpose_cols(stg[k], 128,
                               lambda c, n, t=t: kiT_dst[:, key0 + t * 128:key0 + (t + 1) * 128].rearrange("p (a b) -> p a b", a=1),
                               [("stg", k)], lambda c, n, t=t: [("kiT", g, t)])
            if "noki" not in SK_:
                proj_tok(S_KIWI, 128, hki, PB)

        def rnn_group(g, RB_):
            gw = RB_["gw"]
            rxh, xc, xcb = RB_["rxh"], RB_["xc"], RB_["xcb"]
            tl = RB_["tl"]
            CW = lambda j, c: rnnp_s[:, j * RC + c:j * RC + c + 1]
            for n_ in range(2):
                tok = slice(n_ * 512, (n_ + 1) * 512)
                slot_rx = {}
                slot_rg = {}

                def conv(c, n_=n_, tok=tok):
                    if c % 2 == 0:
                        slot_rx[c // 2] = ring_load(w3[S_RX + c // 2], 4096)
                    s = slot_rx[c // 2]
                    b = cnt["rb"] % 4
                    cnt["rb"] += 1
                    for dc in range(DC):
                        P.op("pe", lambda e, dc=dc, b=b, s=s: e.matmul(
                            bank(b), lhsT=ring[s][:, dc * 256 + (c % 2) * 128:dc * 256 + (c % 2) * 128 + 128],
                            rhs=actT[:, dc, tok], start=(dc == 0), stop=(dc == DC - 1)),
                            reads=[("slot", s)] + [("actT", dc, tt) for tt in range(n_ * 4, n_ * 4 + 4)],
                            writes=[("ps", b)])
                    P.op("act", lambda e, b=b: e.activation(out=rxh[:, 4:516], in_=bank(b), func=AF.Copy),
                         reads=[("ps", b)], writes=["rxh"])
                    P.op("dve", lambda e: e.tensor_copy(out=rxh[:, 0:4], in_=rxhalo[:, c, :]),
                         reads=[("rxhalo", c)], writes=["rxh"])
                    xcc = xc[c % 4]
                    P.op("dve", lambda e: e.tensor_scalar(out=xcc, in0=rxh[:, 4:516], scalar1=CW(3, c), scalar2=CW(4, c),
                                                          op0=ALU.mult, op1=ALU.add),
                         reads=["rxh", "rnnp_s"], writes=[("xc", c % 4)])
                    for j in range(3):
                        P.op("dve", lambda e, j=j: e.scalar_tensor_tensor(out=xcc, in0=rxh[:, 1 + j:1 + j + 512], scalar=CW(j, c),
                                                                          in1=xcc, op0=ALU.mult, op1=ALU.add),
                             reads=["rxh", "rnnp_s", ("xc", c % 4)], writes=[("xc", c % 4)])
                    P.op("dve", lambda e: e.tensor_copy(out=rxhalo[:, c, :], in_=rxh[:, 512:516]),
                         reads=["rxh"], writes=[("rxhalo", c)])
                    P.op("act", lambda e: e.activation(out=xcb[c % 4], in_=xcc, func=AF.Copy),
                         reads=[("xc", c % 4)], writes=[("xcb", c % 4)])

                def gates(c, n_=n_, tok=tok):
                    r_, i_, a_, a2_, gx_, h_, u_, sg_, gl_ = tl[0:9]
                    if c % 2:
                        a_, a2_ = tl[9], tl[10]
                    ka, ka2 = ("a_", c % 2), ("a2_", c % 2)
                    bA = cnt["rb"] % 4
                    bX = (cnt["rb"] + 1) % 4
                    cnt["rb"] += 2
                    for gate, bb_ in ((0, bA), (1, bX)):
                        ks = [k for k, (j, i) in enumerate(pairs) if i == c]
                        for q_, k in enumerate(ks):
                            j = pairs[k][0]
                            P.op("pe", lambda e, k=k, j=j, gate=gate, bb_=bb_, q_=q_, ks=ks: e.matmul(
                                bank(bb_), lhsT=gw[gate][:, k, :], rhs=xcb[j % 4],
                                start=(q_ == 0), stop=(q_ == len(ks) - 1)),
                                reads=["gw", ("xcb", j % 4)], writes=[("ps", bb_)])
                    P.op("act", lambda e: e.activation(out=r_, in_=bank(bA), func=AF.Sigmoid, bias=CW(5, c), scale=1.0),
                         reads=[("ps", bA), "rnnp_s"], writes=["r_"])
                    P.op("act", lambda e: e.activation(out=i_, in_=bank(bX), func=AF.Sigmoid, bias=CW(6, c), scale=1.0),
                         reads=[("ps", bX), "rnnp_s"], writes=["i_"])
                    P.op("act", lambda e: e.activation(out=a_, in_=r_, func=AF.Exp, scale=c1_s[:, c:c + 1]),
                         reads=["r_", "c1_s"], writes=[ka])
                    P.op("act", lambda e: e.activation(out=a2_, in_=r_, func=AF.Exp, scale=c2_s[:, c:c + 1]),
                         reads=["r_", "c2_s"], writes=[ka2])
                    P.op("act", lambda e: e.activation(out=a2_, in_=a2_, func=AF.Sqrt, scale=-1.0, bias=onest[:, 0:1]),
                         reads=[ka2, "onest"], writes=[ka2])
                    P.op("dve", lambda e: e.tensor_tensor(out=gx_, in0=i_, in1=xc[c % 4], op=ALU.mult),
                         reads=["i_", ("xc", c % 4)], writes=["gx_"])
                    P.op("dve", lambda e: e.tensor_tensor(out=gx_, in0=gx_, in1=a2_, op=ALU.mult),
                         reads=["gx_", ka2], writes=["gx_"])
                    P.op("dve", lambda e: e.tensor_tensor_scan(out=h_, data0=a_, data1=gx_, initial=hlast[:, c:c + 1],
                                                               op0=ALU.mult, op1=ALU.add),
                         reads=[ka, "gx_", ("hlast", c)], writes=["h_"])
                    P.op("dve", lambda e: e.tensor_copy(out=hlast[:, c:c + 1], in_=h_[:, 511:512]),
                         reads=["h_"], writes=[("hlast", c)])
                    if upto == "pc" and g == 0 and n_ == 0 and c in (0, 1, 5) and "dumps" not in os.environ.get("K_SKIP", ""):
                        dump(xc[c % 4], 128, 512, [("xc", c % 4)])
                        dump(r_, 128, 512, ["r_"])
                        dump(i_, 128, 512, ["i_"])
                        dump(a_, 128, 512, ["a_"])
                        dump(gx_, 128, 512, ["gx_"])
                        dump(h_, 128, 512, ["h_"])
                    if g == 1:
                        if c % 2 == 0:
                            slot_rg[c // 2] = ring_load(w3[S_RG + c // 2], 4096)
                        s = slot_rg[c // 2]
                        bG = cnt["rb"] % 4
                        cnt["rb"] += 1
                        for dc in range(DC):
                            P.op("pe", lambda e, dc=dc, s=s: e.matmul(
                                bank(bG), lhsT=ring[s][:, dc * 256 + (c % 2) * 128:dc * 256 + (c % 2) * 128 + 128],
                                rhs=actT[:, dc, tok], start=(dc == 0), stop=(dc == DC - 1)),
                                reads=[("slot", s)] + [("actT", dc, tt) for tt in range(n_ * 4, n_ * 4 + 4)],
                                writes=[("ps", bG)])
                        P.op("act", lambda e: e.activation(out=u_, in_=bank(bG), func=AF.Square),
                             reads=[("ps", bG)], writes=["u_"])
                        P.op("dve", lambda e: e.tensor_scalar(out=u_, in0=u_, scalar1=0.044715, scalar2=1.0,
                                                              op0=ALU.mult, op1=ALU.add),
                             reads=["u_"], writes=["u_"])
                        P.op("dve", lambda e: e.tensor_tensor(out=u_, in0=u_, in1=bank(bG), op=ALU.mult),
                             reads=["u_", ("ps", bG)], writes=["u_"])
                        P.op("act", lambda e: e.activation(out=sg_, in_=u_, func=AF.Sigmoid, scale=1.5957691216),
                             reads=["u_"], writes=["sg_"])
                        P.op("dve", lambda e: e.tensor_tensor(out=gl_, in0=sg_, in1=bank(bG), op=ALU.mult),
                             reads=["sg_", ("ps", bG)], writes=["gl_"])
                        P.op("dve", lambda e: e.tensor_tensor(out=yrT[:, c, tok], in0=gl_, in1=h_, op=ALU.mult),
                             reads=["gl_", "h_"], writes=[("yrT", c, n_)])

                for c in range(RC + 1):
                    if c < RC:
                        conv(c)
                    if c >= 1 and "nogates" not in os.environ.get("K_SKIP", "").split(","):
                        gates(c - 1)

        def rnn_bufs(off):
            o_ = off
            RB_ = {}
            g0 = carve(o_, [128, NP, 128], BF16); o_ += NP * 256
            g1 = carve(o_, [128, NP, 128], BF16); o_ += NP * 256
            RB_["gw"] = (g0, g1)
            RB_["rxh"] = carve(o_, [128, 516], F32); o_ += 2064
            RB_["xc"] = []
            for k in range(4):
                RB_["xc"].append(carve(o_, [128, 512], F32)); o_ += 2048
            RB_["xcb"] = []
            for k in range(4):
                RB_["xcb"].append(carve(o_, [128, 512], BF16)); o_ += 1024
            RB_["tl"] = []
            for k in range(11):
                RB_["tl"].append(carve(o_, [128, 512], F32)); o_ += 2048
            for gi, (gwt, src) in enumerate(((g0, wga), (g1, wgx))):
                for q_ in range(4):
                    n0 = q_ * 13
                    P.dma("pool", lambda e, gwt=gwt, src=src, n0=n0: e.dma_start(
                        out=gwt.rearrange("p a b -> p (a b)")[:, n0 * 128:(n0 + 13) * 128], in_=src[:, n0 * 128:(n0 + 13) * 128]),
                        writes=["gw"], tag=("gw", gi, q_))
            return RB_

        RNN_OFF = 106496
        sa, xf1 = ffn_phase_bufs()
        load_xT(0, xf1)
        ffn(x2[0:1024, :], "x2c", w1a, w1b, 0, dbg_out[3072:4096, :] if upto in ("ffnc", "pc") else None, "dbgh", True, sa)
        if upto in ("ffnc", "pc"):
            final_tags += [("hst_dbgh", 0), ("hst_dbgh", 1)]
        barrier()
        kTc = carve(OFF_Q, [128, 4, 1024], BF16)
        Vc = carve(OFF_Q + 8192, [128, 8, 512], BF16)
        kiTc = carve(OFF_Q + 16384, [128, 1024], BF16)
        PB = dict(stg=[carve(OFF_Q + 20480, [128, 256], F32), carve(OFF_Q + 21504, [128, 256], F32)],
                  tmps=[carve(OFF_Q + 22528 + 128 * i_, [128, 32], F32) for i_ in range(4)])
        import os
        SKIP = os.environ.get("K_SKIP", "").split(",")
        if "proj" not in SKIP:
            proj_group(0, PB, kTc, Vc, kiTc, 0)
        P.dma("sp", lambda e: e.dma_start(out=kTc_d, in_=kTc.rearrange("p a b -> p (a b)")),
              reads=[("kT", 0, h_, t_) for h_ in range(4) for t_ in range(8)], writes=["kTc_d"], tag="kTc_d")
        P.dma("sp", lambda e: e.dma_start(out=Vc_d, in_=Vc.rearrange("p a b -> p (a b)")),
              reads=[("V", 0, t_, s_) for t_ in range(8) for s_ in range(2)], writes=["Vc_d"], tag="Vc_d")
        P.dma("sp", lambda e: e.dma_start(out=kiTc_d, in_=kiTc),
              reads=[("kiT", 0, t_) for t_ in range(8)], writes=["kiTc_d"], tag="kiTc_d")
        if "rnn" not in SKIP:
            RB_ = rnn_bufs(RNN_OFF)
            rnn_group(0, RB_)
        P.op("dve", lambda e: e.tensor_scalar(out=hlast[:], in0=hlast[:], scalar1=flag_s[:, 0:1], scalar2=None, op0=ALU.mult),
             reads=[("hlast", c_) for c_ in range(RC)] + ["flag_s"], writes=[("hlast", c_) for c_ in range(RC)])
        P.op("dve", lambda e: e.tensor_scalar(out=rxhalo[:].rearrange("p a b -> p (a b)"),
                                              in0=rxhalo[:].rearrange("p a b -> p (a b)"),
                                              scalar1=flag_s[:, 0:1], scalar2=None, op0=ALU.mult),
             reads=[("rxhalo", c_) for c_ in range(RC)] + ["flag_s"], writes=[("rxhalo", c_) for c_ in range(RC)])
        if upto == "pc" and "dumps" not in SKIP:
            dump(kTc.rearrange("p a b -> p (a b)")[:, 0:2048], 128, 2048, [("kT", 0, h_, t_) for h_ in range(4) for t_ in range(8)], is_bf16=True)
            dump(Vc.rearrange("p a b -> p (a b)")[:, 0:2048], 128, 2048, [("V", 0, t_, s_) for t_ in range(8) for s_ in range(2)], is_bf16=True)
            dump(kiTc[:, 0:1024], 128, 1024, [("kiT", 0, t_) for t_ in range(8)], is_bf16=True)
            dump(hlast[:], 128, RC, [("hlast", c_) for c_ in range(RC)])
            dump(rxhalo[:].rearrange("p a b -> p (a b)"), 128, 4 * RC, [("rxhalo", c_) for c_ in range(RC)])
        barrier()
        if upto in ("ffnc", "pc"):
            P.dma("sp", lambda e: e.dma_start(out=out[0:128, :], in_=LB["yfull"]), reads=[("yfull",)], tag="outd")
            final_tags.append("outd")
            P.emit(final_wait_tags=final_tags)
            return nc, P
        sa, xf1 = ffn_phase_bufs()
        load_xT(1024, xf1)
        ffn(x2[1024:2048, :], "x2o", w1a, w1b, 0, h1d, "h1d", True, sa)
        barrier()
        o_ = OFF_T + 45056
        PB = dict(stg=[carve(o_, [128, 256], F32), carve(o_ + 1024, [128, 256], F32)],
                  tmps=[carve(o_ + 2048 + 128 * i_, [128, 32], F32) for i_ in range(4)])
        P.dma("sp", lambda e: e.dma_start(out=kT[:, :, 0:1024], in_=kTc_d.rearrange("p (a b) -> p a b", a=4)),
              reads=["kTc_d"], writes=[("kT", 0, h_, t_) for h_ in range(4) for t_ in range(8)], tag="kTr")
        P.dma("sp", lambda e: e.dma_start(out=Vt[:, 0:8, :], in_=Vc_d.rearrange("p (a b) -> p a b", a=8)),
              reads=["Vc_d"], writes=[("V", 0, t_, s_) for t_ in range(8) for s_ in range(2)], tag="Vr")
        P.dma("sp", lambda e: e.dma_start(out=kiT[:, 0:1024], in_=kiTc_d),
              reads=["kiTc_d"], writes=[("kiT", 0, t_) for t_ in range(8)], tag="kiTr")
        proj_group(1, PB, kT, Vt, kiT, 1024)
        if upto == "po":
            dump(bufQ[:, 0, :], 128, 1024, [("bufQ", 0, t_) for t_ in range(8)], is_bf16=True)
            dump(bufQ[:, 5, :], 128, 1024, [("bufQ", 5, t_) for t_ in range(8)], is_bf16=True)
            dump(kT[:, 1, :], 128, 2048, [("kT", g_, 1, t_) for g_ in range(2) for t_ in range(8)], is_bf16=True)
            dump(qiT[:, 3, :], 128, 1024, [("qiT", 3, t_) for t_ in range(8)], is_bf16=True)
            dump(kiT[:, :], 128, 2048, [("kiT", g_, t_) for g_ in range(2) for t_ in range(8)], is_bf16=True)
            dump(witm[:].rearrange("p a b -> p (a b)"), 128, 128, [("witm", t_) for t_ in range(8)])
            dump(Vt[:, 9, :], 128, 512, [("V", 1, 1, s_) for s_ in range(2)], is_bf16=True)
        barrier()
        h1T_d = nc.dram_tensor("h1T_d", [128, DC * T], BF16).ap()
        P.dma("sp", lambda e: e.dma_start(out=h1T_d, in_=actT.rearrange("p a b -> p (a b)")),
              reads=[("actT", dc_, t_) for dc_ in range(DC) for t_ in range(NT)], writes=["h1T_d"], tag="h1T_st")
        barrier()
        NQB = 8 if upto in ("all", "att", "rn", "mg") else 2
        if upto == "po":
            NQB = 0
        NKs = [1152 + 128 * j for j in range(8)]
        mb = []
        o_ = OFF_ACTT
        for j in range(8):
            mb.append(carve(o_, [128, NKs[j]], BF16)); o_ += NKs[j] * 2
        yh = [carve(o_, [128, 128], F32), carve(o_ + 512, [128, 128], F32)]
        o_ = OFF_T
        accs = []
        for k in range(4):
            accs.append(carve(o_, [128, 2048], F32)); o_ += 8192
        rls = [carve(o_, [128, 2048], F32), carve(o_ + 8192, [128, 2048], F32)]
        S = [ring[0][:, :].bitcast(F32), ring[1][:, :].bitcast(F32)]
        PTs2 = [ring[2][:, 0:2048].rearrange("p (a b) -> p a b", b=128),
                ring[2][:, 2048:4096].rearrange("p (a b) -> p a b", b=128)]
        m8s = [m8, sb("m8b", [128, 8], F32)]
        thrs = [thr, sb("thrb", [128, 1], F32)]
        hcnt = [0]

        def indexer_head(j, h):
            NK = NKs[j]
            nkc = 9 + j
            ai = j % 4
            acc = accs[ai]
            kres = [("kiT", kc // 8, kc % 8) for kc in range(nkc)]
            if h == 0:
                P.dma("sp", lambda e: e.dma_start(out=acc[:, 0:NK], in_=cbias[j * 128:(j + 1) * 128, 0:NK]),
                      writes=[("acc", ai)], tag=("cb", ai))
            pair, hf = h // 2, h % 2
            hp = hcnt[0] % 2
            hcnt[0] += 1
            rl = rls[hp]
            for half in range(2):
                k0 = half * 1024
                k1 = min(NK, k0 + 1024)
                if k1 <= k0:
                    continue
                for kb in range(2):
                    c0 = k0 + kb * 512
                    n = min(512, k1 - c0)
                    if n <= 0:
                        continue
                    P.op("pe", lambda e, kb=kb, n=n, c0=c0: e.matmul(
                        ps[:, kb * 512:kb * 512 + n],
                        lhsT=qiT[hf * 64:(hf + 1) * 64, pair, j * 128:(j + 1) * 128],
                        rhs=kiT[hf * 64:(hf + 1) * 64, c0:c0 + n], start=True, stop=True),
                        reads=[("qiT", pair, j)] + kres, writes=[("ps", kb)])
                P.op("act", lambda e, k0=k0, k1=k1: e.activation(
                    out=rl[:, k0:k1], in_=ps[:, 0:k1 - k0], func=AF.Relu),
                    reads=[("ps", 0), ("ps", 1)], writes=[("rl", hp)])
            P.op("pool", lambda e: e.tensor_scalar(
                out=rl[:, 0:NK], in0=rl[:, 0:NK], scalar1=witm[:, j, h:h + 1], scalar2=0.0,
                op0=ALU.mult, op1=ALU.add),
                reads=[("rl", hp), ("witm", j)], writes=[("rl", hp)])
            P.op("pool", lambda e: e.tensor_tensor(out=acc[:, 0:NK], in0=acc[:, 0:NK], in1=rl[:, 0:NK], op=ALU.add),
                 reads=[("rl", hp), ("acc", ai)], writes=[("acc", ai)])

        def qk(j, h):
            NK = NKs[j]
            nkc = 9 + j
            nb = (NK + 511) // 512
            kvh = h // 4
            for kb in range(nb):
                n = min(512, NK - kb * 512)
                P.op("pe", lambda e, kb=kb, n=n: e.matmul(
                    ps[:, (2 + kb) * 512:(2 + kb) * 512 + n], lhsT=bufQ[:, h, j * 128:(j + 1) * 128],
                    rhs=kT[:, kvh, kb * 512:kb * 512 + n], start=True, stop=True),
                    reads=[("bufQ", h, j)] + [("kT", kc // 8, kvh, kc % 8) for kc in range(nkc)],
                    writes=[("ps", 2 + kb)])

        def a2_head(j, h):
            NK = NKs[j]
            nkc = 9 + j
            nb = (NK + 511) // 512
            kvh = h // 4
            hh = h % 2
            Sm = S[hh]
            PTs = PTs2[hh]
            sk = ("slot", hh)
            if h == 0:
                qk(j, 0)
            P.op("dve", lambda e: e.scalar_tensor_tensor(
                out=Sm[:, 0:NK], in0=ps[:, 1024:1024 + NK], scalar=SCALE, in1=mb[j][:, 0:NK], op0=ALU.mult, op1=ALU.add),
                reads=[("ps", 2 + kb) for kb in range(nb)] + [("mb", j), sk], writes=[("S", hh)])
            if h < 15:
                qk(j, h + 1)
            P.op("dve", lambda e: e.reduce_max(out=mx[:, hh:hh + 1], in_=Sm[:, 0:NK], axis=AX.X, negate=True),
                 reads=[("S", hh), sk], writes=[("mx", hh)])
            P.op("act", lambda e: e.activation(
                out=Sm[:, 0:NK], in_=Sm[:, 0:NK], func=AF.Exp, bias=mx[:, hh:hh + 1], scale=1.0,
                accum_out=rs[:, hh:hh + 1]),
                reads=[("S", hh), ("mx", hh), sk], writes=[("S", hh), ("rs", hh)])
            kc = 0
            while kc < nkc:
                n = min(4, nkc - kc)
                for i in range(n):
                    P.op("pe", lambda e, i=i, kc=kc: e.transpose(
                        bank(6, 128, i * 128), Sm[:, (kc + i) * 128:(kc + i + 1) * 128], identf[:]),
                        reads=[("S", hh), "identf", sk], writes=[("ps", 6)])
                P.op("act", lambda e, kc=kc, n=n: e.activation(
                    out=PTs[:, kc:kc + n, :], in_=bank(6, n * 128).rearrange("p (a b) -> p a b", b=128), func=AF.Copy),
                    reads=[("ps", 6), ("slot", 2)], writes=[("PTs", hh)])
                kc += n
            for kc in range(nkc):
                P.op("pe", lambda e, kc=kc: e.matmul(
                    bank(7, 128), lhsT=PTs[:, kc, :], rhs=Vt[:, kc, kvh * 128:(kvh + 1) * 128],
                    start=(kc == 0), stop=(kc == nkc - 1)),
                    reads=[("PTs", hh), ("V", kc // 8, kc % 8, kvh // 2), ("slot", 2)], writes=[("ps", 7)])
            P.op("dve", lambda e: e.reciprocal(out=rinv[:, hh:hh + 1], in_=rs[:, hh:hh + 1]),
                 reads=[("rs", hh)], writes=[("rinv", hh)])
            P.op("act", lambda e: e.activation(
                out=yh[hh], in_=bank(7, 128), func=AF.Copy, scale=rinv[:, hh:hh + 1]),
                reads=[("ps", 7), ("rinv", hh)], writes=[("yh", hh)])
            P.op("pe", lambda e: e.transpose(bank(6, 128, 0), yh[hh], identf[:]),
                 reads=[("yh", hh), "identf"], writes=[("ps", 6)])
            P.op("act", lambda e: e.activation(out=bufQ[:, h, j * 128:(j + 1) * 128], in_=bank(6, 128), func=AF.Copy),
                 reads=[("ps", 6)], writes=[("bufQ", h, j)])

        def topk_pair(js, fillers):
            for r_ in range(32):
                for c_, j in enumerate(js):
                    NK = NKs[j]
                    acc = accs[j % 4]
                    P.op("dve", lambda e, acc=acc, NK=NK, c_=c_: e.max(out=m8s[c_][:], in_=acc[:, 0:NK]),
                         reads=[("acc", j % 4)], writes=[("m8", c_)])
                    if r_ < 31:
                        P.op("dve", lambda e, acc=acc, NK=NK, c_=c_: e.match_replace(
                            out=acc[:, 0:NK], in_to_replace=m8s[c_][:], in_values=acc[:, 0:NK], imm_value=-3.0e38),
                            reads=[("acc", j % 4), ("m8", c_)], writes=[("acc", j % 4)])
                for f_ in fillers[r_]:
                    f_()
            for c_, j in enumerate(js):
                NK = NKs[j]
                acc = accs[j % 4]
                P.op("dve", lambda e, c_=c_: e.tensor_scalar(out=thrs[c_][:], in0=m8s[c_][:, 7:8], scalar1=-1.0e29,
                                                             scalar2=None, op0=ALU.max),
                     reads=[("m8", c_)], writes=[("thr", c_)])
                P.op("dve", lambda e, acc=acc, NK=NK, j=j: e.tensor_scalar(
                    out=mb[j][:, 0:NK], in0=acc[:, 0:NK], scalar1=-2.0e38, scalar2=NEG, op0=ALU.is_gt, op1=ALU.mult),
                    reads=[("acc", j % 4)], writes=[("mb", j)])
                P.op("dve", lambda e, acc=acc, NK=NK, j=j, c_=c_: e.scalar_tensor_tensor(
                    out=mb[j][:, 0:NK], in0=acc[:, 0:NK], scalar=thrs[c_][:, 0:1], in1=mb[j][:, 0:NK],
                    op0=ALU.is_lt, op1=ALU.mult),
                    reads=[("acc", j % 4), ("thr", c_), ("mb", j)], writes=[("mb", j)])
                if j < 2:
                    P.dma("sp", lambda e, acc=acc, NK=NK, j=j: e.dma_start(
                        out=acc[:, 0:NK], in_=cbias[j * 128:(j + 1) * 128, 0:NK]),
                        reads=[("mb", j)], writes=[("acc", j % 4)], tag=("cb", j % 4))
                    P.op("dve", lambda e, acc=acc, NK=NK, j=j: e.tensor_tensor(
                        out=mb[j][:, 0:NK], in0=mb[j][:, 0:NK], in1=acc[:, 0:NK], op=ALU.add),
                        reads=[("acc", j % 4), ("mb", j)], writes=[("mb", j)])

        npair = NQB // 2
        if npair:
            for h in range(16):
                indexer_head(0, h)
            for h in range(16):
                indexer_head(1, h)
        for p_ in range(npair):
            fillers = [[] for _ in range(32)]
            if p_ + 1 < npair:
                for i_ in range(32):
                    fillers[i_].append(lambda i_=i_, p_=p_: indexer_head(2 * (p_ + 1) + i_ // 16, i_ % 16))
            if p_ >= 1:
                for i_ in range(32):
                    fillers[i_].append(lambda i_=i_, p_=p_: a2_head(2 * (p_ - 1) + i_ // 16, i_ % 16))
            topk_pair((2 * p_, 2 * p_ + 1), fillers)
        if npair:
            for i_ in range(32):
                a2_head(2 * (npair - 1) + i_ // 16, i_ % 16)
        if upto == "att":
            dump(mb[1][:, :], 128, NKs[1], [("mb", 1)], is_bf16=True)
            dump(bufQ[:, :, 128:256], 128, 2048, [("bufQ", h_, 1) for h_ in range(16)], is_bf16=True)
        barrier()
        P.dma("sp", lambda e: e.dma_start(out=actT.rearrange("p a b -> p (a b)"), in_=h1T_d),
              reads=["h1T_d"], writes=[("actT", dc_, t_) for dc_ in range(DC) for t_ in range(NT)], tag="h1T_ld")
        barrier()
        if upto in ("po", "att"):
            P.dma("sp", lambda e: e.dma_start(out=out[0:128, :], in_=LB["yfull"]), reads=[("yfull",)], tag="outd")
            final_tags.append("outd")
            P.emit(final_wait_tags=final_tags)
            return nc, P
        RB_ = rnn_bufs(RNN_OFF)
        rnn_group(1, RB_)
        if upto == "rn":
            dump(yrT[:, 0, :], 128, 1024, [("yrT", 0, n_) for n_ in range(2)], is_bf16=True)
            dump(yrT[:, 7, :], 128, 1024, [("yrT", 7, n_) for n_ in range(2)], is_bf16=True)
            dump(yrT[:, 19, :], 128, 1024, [("yrT", 19, n_) for n_ in range(2)], is_bf16=True)
            dump(bufQ[:, 3, :], 128, 1024, [("bufQ", 3, t_) for t_ in range(8)], is_bf16=True)
            P.dma("sp", lambda e: e.dma_start(out=out[0:128, :], in_=LB["yfull"]), reads=[("yfull",)], tag="outd")
            final_tags.append("outd")
            P.emit(final_wait_tags=final_tags)
            return nc, P
        barrier()
        MOFF = 106496
        mergedT = carve(MOFF, [128, 16, T], BF16)
        o_ = MOFF + 32768
        sga, sgr, m1 = [], [], []
        for lst in (sga, sgr, m1):
            for k in range(4):
                lst.append(carve(o_, [128, 512], F32)); o_ += 2048
        LB["rt"] = [carve(o_, [128, 512], F32), carve(o_ + 2048, [128, 512], F32)]
        LB["yt"] = [carve(o_ + 4096, [128, 512], F32), carve(o_ + 6144, [128, 512], F32)]
        for op_ in range(8):
            for which, sidx, dstl, b0 in ((0, S_G + op_, sga, 0), (1, S_G + 8 + op_, sgr, 4)):
                s = ring_load(w3[sidx], 4096)
                for q in range(2):
                    for n_ in range(2):
                        b = b0 + q * 2 + n_
                        for dc in range(DC):
                            P.op("pe", lambda e, b=b, dc=dc, q=q, n_=n_, s=s: e.matmul(
                                bank(b), lhsT=ring[s][:, dc * 256 + q * 128:dc * 256 + q * 128 + 128],
                                rhs=actT[:, dc, n_ * 512:(n_ + 1) * 512], start=(dc == 0), stop=(dc == DC - 1)),
                                reads=[("slot", s)] + [("actT", dc, tt) for tt in range(n_ * 4, n_ * 4 + 4)],
                                writes=[("ps", b)])
                        P.op("act", lambda e, b=b, d_=dstl[q * 2 + n_]: e.activation(out=d_, in_=bank(b), func=AF.Sigmoid),
                             reads=[("ps", b)], writes=[("sg", which, q * 2 + n_)])
            s = ring_load(wab[op_], 4096)
            for q in range(2):
                for n_ in range(2):
                    b = q * 2 + n_
                    for kc in range(16):
                        P.op("pe", lambda e, b=b, kc=kc, q=q, n_=n_, s=s: e.matmul(
                            bank(b), lhsT=ring[s][:, kc * 256 + q * 128:kc * 256 + q * 128 + 128],
                            rhs=bufQ[:, kc, n_ * 512:(n_ + 1) * 512], start=(kc == 0), stop=(kc == 15)),
                            reads=[("slot", s)] + [("bufQ", kc, tt) for tt in range(n_ * 4, n_ * 4 + 4)],
                            writes=[("ps", b)])
                    P.op("dve", lambda e, b=b, i_=q * 2 + n_: e.tensor_tensor(out=m1[i_], in0=sga[i_], in1=bank(b), op=ALU.mult),
                         reads=[("ps", b), ("sg", 0, q * 2 + n_)], writes=[("m1", q * 2 + n_)])
            for q in range(2):
                oc = 2 * op_ + q
                s = ring_load(wrb[oc], 2560)
                for n_ in range(2):
                    b = 4 + q * 2 + n_
                    for kc in range(RC):
                        P.op("pe", lambda e, b=b, kc=kc, n_=n_, s=s: e.matmul(
                            bank(b), lhsT=ring[s][:, kc * 128:(kc + 1) * 128],
                            rhs=yrT[:, kc, n_ * 512:(n_ + 1) * 512], start=(kc == 0), stop=(kc == RC - 1)),
                            reads=[("slot", s), ("yrT", kc, n_)], writes=[("ps", b)])
                    i_ = q * 2 + n_
                    P.op("dve", lambda e, b=b, i_=i_: e.tensor_tensor(out=sgr[i_], in0=sgr[i_], in1=bank(b), op=ALU.mult),
                         reads=[("ps", b), ("sg", 1, i_)], writes=[("sg", 1, i_)])
                    P.op("dve", lambda e, i_=i_, oc=oc, n_=n_: e.tensor_tensor(
                        out=mergedT[:, oc, n_ * 512:(n_ + 1) * 512], in0=m1[i_], in1=sgr[i_], op=ALU.add),
                        reads=[("m1", i_), ("sg", 1, i_)], writes=[("mT", oc, n_)])
        if upto == "mg":
            dump(mergedT[:, 0, :], 128, 1024, [("mT", 0, n_) for n_ in range(2)], is_bf16=True)
            dump(mergedT[:, 9, :], 128, 1024, [("mT", 9, n_) for n_ in range(2)], is_bf16=True)
        tok_out_stage(lambda kc, t: mergedT[:, kc, t * 128:(t + 1) * 128],
                      lambda kc, t: [("mT", kc, t // 4)], 16, wo, h1d, "h1d")
        barrier()
        LB["yfull"] = carve(MOFF, [128, 2048], F32)
        LB["gB"] = carve(MOFF + 8192, [128, 2048], F32)
        LB["bB"] = carve(MOFF + 16384, [128, 2048], F32)
        LB["yfull2"] = carve(MOFF + 24576, [128, 2048], F32)
        LB["yfull2_key"] = ("yfull2",)
        layer_norm_pass(1, h2d if upto != "mg" else dbg_out[3072:4096, :], "h2d", True)
        barrier()
        if upto == "mg":
            final_tags += [("hst_h2d", 0), ("hst_h2d", 1)]
            P.dma("sp", lambda e: e.dma_start(out=out[0:128, :], in_=LB["yfull"]), reads=[("yfull",)], tag="outd")
            final_tags.append("outd")
            P.emit(final_wait_tags=final_tags)
            return nc, P
        sa, xf1 = ffn_phase_bufs()
        ffn(h2d, "h2d", w2a, w2b, 2, out, "out", False, sa)
        final_tags += [("hst_out", 0), ("hst_out", 1)]
        P.emit(final_wait_tags=final_tags)
    return nc, P


def _tile_a(w, c0, ncols, pad):
    K = w.shape[0]
    blk = np.zeros((128, K // 128, pad), np.float32)
    blk[:, :, :ncols] = w[:, c0:c0 + ncols].reshape(K // 128, 128, ncols).transpose(1, 0, 2)
    return blk


def _pack_ffn(w_in, w_out):
    wa = np.empty((FC, 128, 2, 16, 128), np.float32)
    for fc in range(FC):
        wa[fc, :, 0] = _tile_a(w_in, fc * 128, 128, 128)
        wa[fc, :, 1] = _tile_a(w_in, FF + fc * 128, 128, 128)
    wb = np.empty((4, 11, 128, 4, 512), np.float32)
    wo = w_out.reshape(11, 4, 128, 4, 512)
    wb[:] = wo.transpose(3, 0, 2, 1, 4)
    return np.ascontiguousarray(wa.reshape(FC, 128, 4096)), np.ascontiguousarray(wb.reshape(FC, 128, 2048))


def _pack_shared(inp):
    sh = {}
    sh["w1a"], sh["w1b"] = _pack_ffn(inp["ffn1_w_in"][0], inp["ffn1_w_out"][0])
    sh["w2a"], sh["w2b"] = _pack_ffn(inp["ffn2_w_in"][0], inp["ffn2_w_out"][0])
    w_in = inp["w_in"][0]
    sl = w3_slots()
    w3 = np.empty((len(sl), 128, 16, 256), np.float32)
    for i, (c0, nc_) in enumerate(sl):
        w3[i] = _tile_a(w_in, c0, nc_, 256)
    sh["w3"] = w3.reshape(len(sl), 128, 4096)
    wab = np.empty((8, 128, 16, 256), np.float32)
    for i in range(8):
        wab[i] = _tile_a(inp["w_attn_branch"][0], i * 256, 256, 256)
    sh["wab"] = wab.reshape(8, 128, 4096)
    wrb = np.empty((16, 128, 20, 128), np.float32)
    for i in range(16):
        wrb[i] = _tile_a(inp["w_rnn_branch"][0], i * 128, 128, 128)
    sh["wrb"] = wrb.reshape(16, 128, 2560)
    wo = inp["w_out"][0].reshape(4, 4, 128, 4, 512)
    sh["wo"] = np.ascontiguousarray(wo.transpose(3, 0, 2, 1, 4)).reshape(16, 128, 2048)
    pairs = gate_pairs()
    for name, key in (("wga", "lru_wa"), ("wgx", "lru_wx")):
        bd = np.zeros((DRNN, DRNN), np.float32)
        for nb in range(16):
            bd[nb * 160:(nb + 1) * 160, nb * 160:(nb + 1) * 160] = inp[key][0, nb]
        g = np.empty((128, len(pairs), 128), np.float32)
        for k, (j, i) in enumerate(pairs):
            g[:, k, :] = bd[j * 128:(j + 1) * 128, i * 128:(i + 1) * 128]
        sh[name] = g.reshape(128, len(pairs) * 128)
    sh["lnp"] = np.stack([inp["ln1_g"][0], inp["ln1_b"][0], inp["ln2_g"][0], inp["ln2_b"][0],
                          inp["ln3_g"][0], inp["ln3_b"][0]]).astype(np.float32)
    rn = np.empty((128, 8, RC), np.float32)
    for j in range(4):
        rn[:, j, :] = inp["conv_w"][0, j].reshape(RC, 128).T
    rn[:, 4, :] = inp["conv_b"][0].reshape(RC, 128).T
    rn[:, 5, :] = inp["lru_ba"][0].reshape(RC, 128).T
    rn[:, 6, :] = inp["lru_bx"][0].reshape(RC, 128).T
    rn[:, 7, :] = inp["lru_lambda"][0].reshape(RC, 128).T
    sh["rnnp"] = rn.reshape(128, 8 * RC)
    sh["ident"] = np.eye(128, dtype=np.float32)
    invf = np.zeros((128, 24), np.float32)
    invf[:, 0:16] = (500000.0 ** (-(np.arange(16, dtype=np.float32) * 2.0 / 32.0))).astype(np.float32)[None]
    invf[:, 16:24] = (500000.0 ** (-(np.arange(8, dtype=np.float32) * 2.0 / 16.0))).astype(np.float32)[None]
    sh["invf"] = invf
    return sh


def _core_inputs(inp, sh, c):
    b, half = c // 2, c % 2
    x = inp["x"]
    pos = inp["positions"]
    x2 = np.zeros((2048, 2048), np.float32)
    p2 = np.zeros((2, 1024), np.int32)
    if half == 1:
        x2[:1024] = x[b, :1024]
        p2[0] = pos[b, :1024]
    x2[1024:] = x[b, half * 1024:(half + 1) * 1024]
    p2[1] = pos[b, half * 1024:(half + 1) * 1024]
    posi = np.ascontiguousarray(p2.reshape(2, 8, 128).transpose(2, 0, 1).reshape(128, 16))
    q = np.arange(1024)[:, None] + 1024
    k = np.arange(2048)[None, :]
    valid = (k <= q) & (k >= (0 if half == 1 else 1024))
    cb = np.where(valid, 0.0, NEG).astype(np.float32)
    m = dict(sh)
    m.update(x2=x2, posi=posi, cbias=cb, flag=np.full((128, 1), float(half), np.float32))
    return m


_CACHE = {}


def kernel(**inputs):
    inp = {k: np.asarray(v) for k, v in inputs.items()}
    if "nc" not in _CACHE:
        _CACHE["nc"] = build("all")[0]
    nc = _CACHE["nc"]
    sh = _pack_shared(inp)
    in_maps = [_core_inputs(inp, sh, c) for c in range(8)]
    res = run_bass_kernel_spmd(nc, in_maps, core_ids=list(range(8)))
    outp = np.empty((4, 2048, 2048), np.float32)
    for c in range(8):
        b, half = c // 2, c % 2
        outp[b, half * 1024:(half + 1) * 1024] = res.results[c]["out"]
    return outp
```

```python
import contextlib
import os
import math
import numpy as np
import concourse.bass as bass
import concourse.mybir as mybir
from concourse.bass_utils import run_bass_kernel_spmd

F32 = mybir.dt.float32
BF16 = mybir.dt.bfloat16
I32 = mybir.dt.int32
AF = mybir.ActivationFunctionType
ALU = mybir.AluOpType
AX = mybir.AxisListType

D = 2048
DC = 16
T = 1024
NT = 8
FF = 5632
FC = 44
DRNN = 2560
RC = 20
ALPHA = 2.0 ** 0.25
LN_EPS = 1e-5
SCALE = 128 ** -0.5
WI_SCALE = 1.0 / 32.0
NEG = -1.0e30
ENGINES = ("pe", "act", "dve", "pool", "sp")
EPOCH = 12000


class Prog:
    def __init__(self, nc):
        self.nc = nc
        self.ops = []

    def op(self, eng, fn, reads=(), writes=(), phase=True):
        r = tuple(reads) + (("PHASE",) if phase else ())
        self.ops.append(dict(eng=eng, fn=fn, reads=r, writes=tuple(writes), dma=None))

    def dma(self, eng, fn, reads=(), writes=(), tag=None, phase=True):
        r = tuple(reads) + (("PHASE",) if phase else ())
        self.ops.append(dict(eng=eng, fn=fn, reads=r, writes=tuple(writes), dma=tag))

    def emit(self, final_wait_tags=()):
        nc = self.nc
        ops = self.ops
        n = len(ops)
        last_write = {}
        readers = {}
        need = [None] * n
        signal = [False] * n
        for i, o in enumerate(ops):
            d = set()
            for r in o["reads"]:
                lw = last_write.get(r)
                if lw is not None:
                    d.add(lw)
            for w in o["writes"]:
                lw = last_write.get(w)
                if lw is not None:
                    d.add(lw)
                rd = readers.get(w)
                if rd:
                    d.update(rd[0].values())
                    d.update(rd[1])
            d.discard(i)
            lst = []
            for j in d:
                oj = ops[j]
                if oj["dma"] is None:
                    if oj["eng"] == o["eng"] and oj["eng"] == "pe" and o["dma"] is None:
                        continue
                    signal[j] = True
                lst.append(j)
            need[i] = lst
            for r in o["reads"]:
                rd = readers.setdefault(r, ({}, []))
                if o["dma"] is None:
                    rd[0][o["eng"]] = i
                else:
                    rd[1].append(i)
            for w in o["writes"]:
                last_write[w] = i
                readers[w] = ({}, [])
        eng_count = {e: 0 for e in ENGINES}
        tag_count = {}
        semkey = [None] * n
        val = [0] * n
        keys = []
        for i, o in enumerate(ops):
            if o["dma"] is not None:
                t = o["dma"]
                c = tag_count.get(t, 0) + 16
                tag_count[t] = c
                k = ("t", t, c // 32000)
                val[i] = c - (c // 32000) * 32000 if c // 32000 else c
                assert c < 32000, t
                k = ("t", t, 0)
                val[i] = c
            elif signal[i]:
                eng_count[o["eng"]] += 1
                c = eng_count[o["eng"]]
                ep = (c - 1) // EPOCH
                k = ("e", o["eng"], ep)
                val[i] = c - ep * EPOCH
            else:
                continue
            semkey[i] = k
            if k not in keys:
                keys.append(k)
        self.stats = dict(n_ops=n, eng_count=dict(eng_count), n_sems=len(keys))
        with contextlib.ExitStack() as st:
            sems = {}
            for idx, k in enumerate(keys):
                sems[k] = st.enter_context(nc.semaphore("sm%d" % idx))
            block = st.enter_context(nc.Block())
            per_eng = {e: [i for i, o in enumerate(ops) if o["eng"] == e] for e in ENGINES}

            def run(ename, eng):
                known = {}
                for i in per_eng[ename]:
                    o = ops[i]
                    waits = {}
                    for j in need[i]:
                        k = semkey[j]
                        if val[j] > waits.get(k, 0):
                            waits[k] = val[j]
                    for k, v in waits.items():
                        if known.get(k, 0) >= v:
                            continue
                        if k[0] == "e":
                            later = [kk for kk in known if kk[0] == "e" and kk[1] == k[1] and kk[2] > k[2]]
                            if later:
                                continue
                        known[k] = v
                        eng.wait_ge(sems[k], v)
                    ins = o["fn"](eng)
                    if o["dma"] is not None:
                        ins.then_inc(sems[semkey[i]], 16)
                    elif signal[i]:
                        ins.then_inc(sems[semkey[i]], 1)
                if ename == "sp":
                    for t in final_wait_tags:
                        eng.wait_ge(sems[("t", t, 0)], tag_count[t])

            @block.sync
            def _(e):
                run("sp", e)

            @block.scalar
            def _(e):
                run("act", e)

            @block.vector
            def _(e):
                run("dve", e)

            @block.gpsimd
            def _(e):
                run("pool", e)

            @block.tensor
            def _(e):
                run("pe", e)


def w3_slots():
    sl = []
    for s in range(8):
        sl.append((256 * s, 256))
    for s in range(2):
        sl.append((2048 + 256 * s, 256))
    for s in range(2):
        sl.append((2560 + 256 * s, 256))
    for s in range(4):
        sl.append((3072 + 256 * s, 256))
    sl.append((4096, 80))
    for s in range(10):
        sl.append((4176 + 256 * s, 256))
    for s in range(10):
        sl.append((6736 + 256 * s, 256))
    for s in range(16):
        sl.append((9296 + 256 * s, 256))
    return sl


S_Q, S_K, S_V, S_QI, S_KIWI, S_RX, S_RG, S_G = 0, 8, 10, 12, 16, 17, 27, 37


def gate_pairs():
    pairs = []
    for i in range(RC):
        for j in range(RC):
            hit = False
            for nb in range(16):
                lo, hi = nb * 160, (nb + 1) * 160
                if max(lo, j * 128) < min(hi, (j + 1) * 128) and max(lo, i * 128) < min(hi, (i + 1) * 128):
                    hit = True
            if hit:
                pairs.append((j, i))
    return pairs


ARENA_BYTES = 172032
OFF_ACTT = 0
OFF_Q = 32768
OFF_R = 65536
OFF_KT = 65536
OFF_V = 81920
OFF_QIT = 98304
OFF_KIT = 114688
OFF_T = 122880
NSLOT = 3
SLOT_ELEMS = 4096


XCOPY = {}


def build(upto="all"):
    nc = bass.Bass("TRN2", target_bir_lowering=False)
    P = Prog(nc)

    def din(name, shape, dt=F32):
        return nc.dram_tensor(name, shape, dt, kind="ExternalInput").ap()

    x2 = din("x2", [2048, 2048])
    posi = din("posi", [128, 16], I32)
    invf = din("invf", [128, 24])
    ident = din("ident", [128, 128])
    cbias = din("cbias", [1024, 2048])
    flag = din("flag", [128, 1])
    w1a = din("w1a", [FC, 128, 4096])
    w1b = din("w1b", [FC, 128, 2048])
    w2a = din("w2a", [FC, 128, 4096])
    w2b = din("w2b", [FC, 128, 2048])
    w3 = din("w3", [53, 128, 4096])
    wab = din("wab", [8, 128, 4096])
    wrb = din("wrb", [16, 128, 2560])
    wo = din("wo", [16, 128, 2048])
    pairs = gate_pairs()
    NP = len(pairs)
    wga = din("wga", [128, NP * 128])
    wgx = din("wgx", [128, NP * 128])
    lnp = din("lnp", [6, 2048])
    rnnp = din("rnnp", [128, 8 * RC])
    out = nc.dram_tensor("out", [1024, 2048], F32, kind="ExternalOutput").ap()
    ypre = nc.dram_tensor("ypre", [1024, 2048], F32).ap()
    h1d = nc.dram_tensor("h1d", [1024, 2048], F32).ap()
    h2d = nc.dram_tensor("h2d", [1024, 2048], F32).ap()
    kTc_d = nc.dram_tensor("kTc_d", [128, 4096], BF16).ap()
    Vc_d = nc.dram_tensor("Vc_d", [128, 4096], BF16).ap()
    kiTc_d = nc.dram_tensor("kiTc_d", [128, 1024], BF16).ap()
    dbg_out = None
    if upto != "all":
        dbg_out = nc.dram_tensor("dbg", [4096, 2048], F32, kind="ExternalOutput").ap()

    final_tags = []
    with contextlib.ExitStack() as st:
        def sb(name, shape, dt):
            return st.enter_context(nc.sbuf_tensor(name, shape, dt))

        arena = sb("arena", [128, ARENA_BYTES // 4], F32)
        ring = [sb("ring%d" % i, [128, SLOT_ELEMS], BF16) for i in range(NSLOT)]
        ps = st.enter_context(nc.psum_tensor("ps", [128, 4096], F32))
        identb = sb("identb", [128, 128], BF16)
        posf = sb("posf", [128, 16], F32)
        posi_s = sb("posi_s", [128, 16], I32)
        invf_s = sb("invf_s", [128, 24], F32)
        flag_s = sb("flag_s", [128, 1], F32)
        rnnp_s = sb("rnnp_s", [128, 8 * RC], F32)
        c1_s = sb("c1_s", [128, RC], F32)
        c2_s = sb("c2_s", [128, RC], F32)
        hlast = sb("hlast", [128, RC], F32)
        rxhalo = sb("rxhalo", [128, RC, 4], F32)
        stats = sb("stats", [128, NT, 4, 6], F32)
        mv = sb("mv", [128, NT, 2], F32)
        sd = sb("sd", [128, NT], F32)
        rstd = sb("rstd", [128, NT], F32)
        epst = sb("epst", [128, 1], F32)
        witm = sb("witm", [128, NT, 16], F32)
        m8 = sb("m8", [128, 8], F32)
        thr = sb("thr", [128, 1], F32)
        mx = sb("mx", [128, 2], F32)
        rs = sb("rs", [128, 2], F32)
        rinv = sb("rinv", [128, 2], F32)
        bar_t = sb("bar_t", [128, 4], F32)

        def carve(off, shape, dt):
            esz = 2 if dt == BF16 else 4
            nel = 1
            for s_ in shape[1:]:
                nel *= s_
            nb = nel * esz
            assert off % 4 == 0 and nb % 4 == 0 and off + nb <= ARENA_BYTES, (off, nb)
            v = arena[:, off // 4:(off + nb) // 4]
            if dt == BF16:
                v = v.bitcast(BF16)
            elif dt == I32:
                v = v.bitcast(I32)
            if len(shape) == 3:
                v = v.rearrange("p (a b) -> p a b", b=shape[2])
            return v

        def bank(b, n=512, off=0):
            return ps[:, b * 512 + off:b * 512 + off + n]

        def bankbf(b, nb=1):
            return ps[:, b * 512:(b + nb) * 512].bitcast(BF16)

        barrier_n = [0]

        def barrier():
            barrier_n[0] += 1
            P.op("dve", lambda e: e.memset(bar_t[:, 0:1], 0.0), reads=[], writes=["PHASE"], phase=False)

        ring_i = [0]

        def ring_load(src, nel):
            s = ring_i[0] % NSLOT
            ring_i[0] += 1
            k = 1
            while nel // k > 2048 or nel % k:
                k += 1
            dst = ring[s][:, 0:nel].rearrange("p (a b) -> p a b", a=k)
            srcv = src.rearrange("p (a b) -> p a b", a=k)
            P.dma("pool", lambda e: e.dma_start(out=dst, in_=srcv), writes=[("slot", s)],
                  tag=("slot", s), phase=False)
            return s

        P.dma("pool", lambda e: e.dma_start(out=identb[:], in_=ident), writes=["identb"], tag="c_ident")
        P.dma("sp", lambda e: e.dma_start(out=posi_s[:], in_=posi), writes=["posi_s"], tag="c_posi")
        P.dma("sp", lambda e: e.dma_start(out=invf_s[:], in_=invf), writes=["invf_s"], tag="c_invf")
        P.dma("sp", lambda e: e.dma_start(out=flag_s[:], in_=flag), writes=["flag_s"], tag="c_flag")
        P.dma("sp", lambda e: e.dma_start(out=rnnp_s[:], in_=rnnp), writes=["rnnp_s"], tag="c_rnnp")
        P.op("dve", lambda e: e.memset(epst[:], LN_EPS), writes=["epst"])
        P.op("dve", lambda e: e.memset(hlast[:], 0.0), writes=["hlast"])
        P.op("dve", lambda e: e.memset(rxhalo[:], 0.0), writes=["rxhalo"])
        P.op("dve", lambda e: e.tensor_copy(out=posf[:], in_=posi_s[:]), reads=["posi_s"], writes=["posf"])

        LAM = 7 * RC
        P.op("act", lambda e: e.activation(out=c1_s[:], in_=rnnp_s[:, LAM:LAM + RC], func=AF.Exp, scale=-1.0),
             reads=["rnnp_s"], writes=["c1_s"])
        P.op("dve", lambda e: e.tensor_scalar(out=c1_s[:], in0=c1_s[:], scalar1=1.0, scalar2=None, op0=ALU.add),
             reads=["c1_s"], writes=["c1_s"])
        P.op("act", lambda e: e.activation(out=c1_s[:], in_=c1_s[:], func=AF.Ln),
             reads=["c1_s"], writes=["c1_s"])
        P.op("dve", lambda e: e.tensor_scalar(out=c2_s[:], in0=c1_s[:], scalar1=-16.0, scalar2=None, op0=ALU.mult),
             reads=["c1_s"], writes=["c2_s"])
        P.op("dve", lambda e: e.tensor_scalar(out=c1_s[:], in0=c1_s[:], scalar1=-8.0, scalar2=None, op0=ALU.mult),
             reads=["c1_s"], writes=["c1_s"])

        identf = sb("identf", [128, 128], F32)
        onest = sb("onest", [128, 1], F32)
        P.op("dve", lambda e: e.memset(onest[:], 1.0), writes=["onest"])
        nmr = sb("nmr", [128, NT], F32)
        P.dma("sp", lambda e: e.dma_start(out=identf[:], in_=ident), writes=["identf"], tag="c_identf")
        cosq_r = sb("cosq_r", [128, 16, 32], F32)
        sinq_r = sb("sinq_r", [128, 16, 32], F32)
        cosi_r = sb("cosi_r", [128, 16, 32], F32)
        sini_r = sb("sini_r", [128, 16, 32], F32)
        trig_t = carve(OFF_T + 45056, [128, 16, 16], F32)
        trig_k = carve(OFF_T + 46080, [128, 16, 16], F32)
        trig_ki = carve(OFF_T + 47104, [128, 16, 16], I32)

        def trig_table(dst, ncol, col0, shift, nrep):
            tv = trig_t[:, :, 0:ncol]
            kv = trig_k[:, :, 0:ncol]
            kiv = trig_ki[:, :, 0:ncol]
            for gt in range(16):
                P.op("dve", lambda e, gt=gt: e.tensor_scalar(
                    out=trig_t[:, gt, 0:ncol], in0=invf_s[:, col0:col0 + ncol], scalar1=posf[:, gt:gt + 1],
                    scalar2=shift, op0=ALU.mult, op1=ALU.add),
                    reads=["posf", "invf_s", "trig_t"], writes=["trig_t"])
            two_pi = 2.0 * math.pi
            P.op("dve", lambda e: e.tensor_scalar(out=kv, in0=tv, scalar1=1.0 / two_pi, scalar2=None, op0=ALU.mult),
                 reads=["trig_t"], writes=["trig_k"])
            P.op("dve", lambda e: e.tensor_copy(out=kiv, in_=kv), reads=["trig_k"], writes=["trig_ki"])
            P.op("dve", lambda e: e.tensor_copy(out=kv, in_=kiv), reads=["trig_ki"], writes=["trig_k"])
            P.op("dve", lambda e: e.scalar_tensor_tensor(out=tv, in0=kv, scalar=-two_pi, in1=tv,
                                                         op0=ALU.mult, op1=ALU.add),
                 reads=["trig_k", "trig_t"], writes=["trig_t"])
            P.op("dve", lambda e: e.tensor_scalar(out=kv, in0=tv, scalar1=math.pi, scalar2=-two_pi,
                                                  op0=ALU.is_gt, op1=ALU.mult),
                 reads=["trig_t"], writes=["trig_k"])
            P.op("dve", lambda e: e.tensor_tensor(out=tv, in0=tv, in1=kv, op=ALU.add),
                 reads=["trig_t", "trig_k"], writes=["trig_t"])
            P.op("dve", lambda e: e.tensor_scalar(out=kv, in0=tv, scalar1=-math.pi, scalar2=two_pi,
                                                  op0=ALU.is_lt, op1=ALU.mult),
                 reads=["trig_t"], writes=["trig_k"])
            P.op("dve", lambda e: e.tensor_tensor(out=tv, in0=tv, in1=kv, op=ALU.add),
                 reads=["trig_t", "trig_k"], writes=["trig_t"])
            P.op("dve", lambda e: e.tensor_scalar(out=tv, in0=tv, scalar1=3.14159, scalar2=-3.14159,
                                                  op0=ALU.min, op1=ALU.max),
                 reads=["trig_t"], writes=["trig_t"])
            for r_ in range(nrep):
                P.op("act", lambda e, r_=r_: e.activation(out=dst[:, :, r_ * ncol:(r_ + 1) * ncol], in_=tv, func=AF.Sin),
                     reads=["trig_t"], writes=["trig_rep"])

        trig_table(sinq_r, 16, 0, 0.0, 2)
        trig_table(cosq_r, 16, 0, math.pi / 2, 2)
        trig_table(sini_r, 8, 16, 0.0, 4)
        trig_table(cosi_r, 8, 16, math.pi / 2, 4)
        barrier()


        actT = carve(OFF_ACTT, [128, DC, T], BF16)
        bufQ = carve(OFF_Q, [128, 16, T], BF16)
        yrT = carve(OFF_R, [128, RC, T], BF16)
        kT = carve(OFF_KT, [128, 4, 2048], BF16)
        Vt = carve(OFF_V, [128, 16, 512], BF16)
        qiT = carve(OFF_QIT, [128, 8, T], BF16)
        kiT = carve(OFF_KIT, [128, 2048], BF16)
        gT = carve(OFF_Q, [128, FC, T], BF16)

        dump_row = [0]

        def dump(src, nrows_p, ncols, reads, is_bf16=False, src_dram=False):
            if dbg_out is None:
                return
            r0 = dump_row[0]
            dump_row[0] += 128
            tag = "dump%d" % r0
            dst = dbg_out[r0:r0 + nrows_p, 0:ncols]
            if len(src.shape) == 3:
                dst = dst.rearrange("p (a b) -> p a b", b=src.shape[2])
            q = "pool" if is_bf16 else "sp"
            P.dma(q, lambda e: e.dma_start(out=dst, in_=src), reads=reads, writes=[tag], tag=tag)
            final_tags.append(tag)
            return r0

        tr_rr = [0]

        def transpose_cols(src_f32, ncol, dst_fn, src_res, dst_res, banks=(4, 5, 6, 7), eng_pref=None):
            nch = ncol // 128
            c = 0
            while c < nch:
                n = min(4, nch - c)
                b = banks[tr_rr[0] % len(banks)]
                tr_rr[0] += 1
                for i in range(n):
                    P.op("pe", lambda e, i=i, c=c, b=b: e.transpose(
                        bank(b, 128, i * 128), src_f32[:, (c + i) * 128:(c + i + 1) * 128], identf[:]),
                        reads=list(src_res) + ["identf"], writes=[("ps", b)])
                dst = dst_fn(c, n)
                srcv = bank(b, n * 128).rearrange("p (a b) -> p a b", b=128)
                eng = eng_pref or ("act" if tr_rr[0] % 2 else "dve")
                if eng == "act":
                    P.op("act", lambda e, dst=dst, srcv=srcv: e.activation(out=dst, in_=srcv, func=AF.Copy),
                         reads=[("ps", b)], writes=list(dst_res(c, n)))
                else:
                    P.op("dve", lambda e, dst=dst, srcv=srcv: e.tensor_copy(out=dst, in_=srcv),
                         reads=[("ps", b)], writes=list(dst_res(c, n)))
                c += n

        LB = {}

        def set_ln_bufs(off):
            o_ = off
            LB["yfull"] = carve(o_, [128, 2048], F32); o_ += 8192
            LB["gB"] = carve(o_, [128, 2048], F32); o_ += 8192
            LB["bB"] = carve(o_, [128, 2048], F32); o_ += 8192
            LB["rt"] = []
            LB["yt"] = []
            for k in range(2):
                LB["rt"].append(carve(o_, [128, 512], F32)); o_ += 2048
            for k in range(2):
                LB["yt"].append(carve(o_, [128, 512], F32)); o_ += 2048
            return o_

        cnt = dict(sa=0, rt=0, stg=0, pj=0, rb=0)

        def load_xT(row0, xf1):
            xf = [LB["yfull"], xf1]
            for t in range(NT):
                k = t % 2
                P.dma("sp", lambda e, t=t, k=k: e.dma_start(out=xf[k], in_=x2[row0 + t * 128:row0 + (t + 1) * 128, :]),
                      writes=[("xf", k)] + ([("yfull",)] if k == 0 else []), tag=("xf", k))
                transpose_cols(xf[k], 2048, lambda c, n, t=t: actT[:, c:c + n, t * 128:(t + 1) * 128],
                               [("xf", k)] + ([("yfull",)] if k == 0 else []), lambda c, n, t=t: [("actT", cc, t) for cc in range(c, c + n)])

        def tok_out_stage(lhs_fn, lhs_res, nk, wsrc, resid, resid_res):
            nl = nk // 4
            for cb in range(4):
                s4 = 0
                while s4 < nl:
                    na = min(2, nl - s4)
                    s = ring_i[0] % NSLOT
                    ring_i[0] += 1
                    dstv = ring[s][:, 0:na * 2048].rearrange("p (a b) -> p a b", a=na)
                    srcv = wsrc[cb * nl + s4:cb * nl + s4 + na].rearrange("a p n -> p a n")
                    P.dma("pool", lambda e, dstv=dstv, srcv=srcv: e.dma_start(out=dstv, in_=srcv),
                          writes=[("slot", s)], tag=("slot", s), phase=False)
                    for t in range(NT):
                        for a_ in range(na):
                            for j in range(4):
                                kc = (s4 + a_) * 4 + j
                                P.op("pe", lambda e, t=t, kc=kc, j=j, s=s, a_=a_: e.matmul(
                                    bank(t), lhsT=lhs_fn(kc, t),
                                    rhs=ring[s][:, a_ * 2048 + j * 512:a_ * 2048 + (j + 1) * 512],
                                    start=(kc == 0), stop=(kc == nk - 1)),
                                    reads=[("slot", s)] + lhs_res(kc, t), writes=[("ps", t)])
                    s4 += na
                for t in range(NT):
                    k = cnt["rt"] % 2
                    cnt["rt"] += 1
                    rt, yt = LB["rt"], LB["yt"]
                    P.dma("sp", lambda e, k=k, t=t, cb=cb, rt=rt: e.dma_start(
                        out=rt[k], in_=resid[t * 128:(t + 1) * 128, cb * 512:(cb + 1) * 512]),
                        reads=[(resid_res, t)], writes=[("rt", k)], tag=("rt", k))
                    P.op("dve", lambda e, k=k, t=t, rt=rt, yt=yt: e.scalar_tensor_tensor(
                        out=yt[k], in0=rt[k], scalar=ALPHA, in1=bank(t), op0=ALU.mult, op1=ALU.add),
                        reads=[("rt", k), ("ps", t)], writes=[("yt", k)])
                    P.op("dve", lambda e, k=k, t=t, cb=cb, yt=yt: e.bn_stats(out=stats[:, t, cb, :], in_=yt[k]),
                         reads=[("yt", k)], writes=[("stats", t)])
                    P.dma("sp", lambda e, k=k, t=t, cb=cb, yt=yt: e.dma_start(
                        out=ypre[t * 128:(t + 1) * 128, cb * 512:(cb + 1) * 512], in_=yt[k]),
                        reads=[("yt", k)], writes=[("ypre", t)], tag=("yts", k))

        def layer_norm_pass(ln_idx, dst_dram, dst_res, want_T):
            yfs = [LB["yfull"], LB["yfull2"]]
            yks = [("yfull",), LB["yfull2_key"]]
            gB, bB = LB["gB"], LB["bB"]
            P.dma("sp", lambda e: e.dma_start(out=gB, in_=lnp[2 * ln_idx].partition_broadcast(128)),
                  writes=["gB"], tag="gB")
            P.dma("sp", lambda e: e.dma_start(out=bB, in_=lnp[2 * ln_idx + 1].partition_broadcast(128)),
                  writes=["bB"], tag="bB")
            for t in range(NT):
                yf, yk = yfs[t % 2], yks[t % 2]
                P.op("dve", lambda e, t=t: e.bn_aggr(out=mv[:, t, :], in_=stats[:, t, :, :].rearrange("p a b -> p (a b)")),
                     reads=[("stats", t)], writes=[("mv", t)])
                P.op("act", lambda e, t=t: e.activation(out=sd[:, t:t + 1], in_=mv[:, t, 1:2], func=AF.Sqrt,
                                                        bias=epst[:, 0:1], scale=1.0),
                     reads=[("mv", t), "epst"], writes=[("sd", t)])
                P.op("dve", lambda e, t=t: e.reciprocal(out=rstd[:, t:t + 1], in_=sd[:, t:t + 1]),
                     reads=[("sd", t)], writes=[("rstd", t)])
                P.op("dve", lambda e, t=t: e.tensor_scalar(out=nmr[:, t:t + 1], in0=mv[:, t, 0:1], scalar1=rstd[:, t:t + 1],
                                                           scalar2=-1.0, op0=ALU.mult, op1=ALU.mult),
                     reads=[("mv", t), ("rstd", t)], writes=[("nmr", t)])
                P.dma("sp", lambda e, t=t, yf=yf: e.dma_start(out=yf, in_=ypre[t * 128:(t + 1) * 128, :]),
                      reads=[("ypre", t)], writes=[yk], tag=("yfl", t % 2))
                P.op("act", lambda e, t=t, yf=yf: e.activation(out=yf, in_=yf, func=AF.Identity,
                                                               scale=rstd[:, t:t + 1], bias=nmr[:, t:t + 1]),
                     reads=[yk, ("nmr", t), ("rstd", t)], writes=[yk])
                P.op("dve", lambda e, yf=yf: e.tensor_tensor(out=yf, in0=yf, in1=gB, op=ALU.mult),
                     reads=[yk, "gB"], writes=[yk])
                P.op("dve", lambda e, yf=yf: e.tensor_tensor(out=yf, in0=yf, in1=bB, op=ALU.add),
                     reads=[yk, "bB"], writes=[yk])
                if dst_dram is not None:
                    P.dma("sp", lambda e, t=t, yf=yf: e.dma_start(out=dst_dram[t * 128:(t + 1) * 128, :], in_=yf),
                          reads=[yk], writes=[(dst_res, t)], tag=("hst_" + dst_res, t % 2))
                if want_T:
                    transpose_cols(yf, 2048, lambda c, n, t=t: actT[:, c:c + n, t * 128:(t + 1) * 128],
                                   [yk], lambda c, n, t=t: [("actT", cc, t) for cc in range(c, c + n)])

        def ffn(resid, resid_res, wa, wb, ln_idx, dst_dram, dst_res, want_T, sa):
            for fc in range(FC):
                s = ring_load(wa[fc], 4096)
                base = (fc % 2) * 4
                for ab in range(2):
                    for n_ in range(2):
                        b = base + ab * 2 + n_
                        for dc in range(DC):
                            P.op("pe", lambda e, b=b, ab=ab, dc=dc, n_=n_, s=s: e.matmul(
                                bank(b), lhsT=ring[s][:, (ab * 16 + dc) * 128:(ab * 16 + dc + 1) * 128],
                                rhs=actT[:, dc, n_ * 512:(n_ + 1) * 512], start=(dc == 0), stop=(dc == DC - 1)),
                                reads=[("slot", s)] + [("actT", dc, tt) for tt in range(n_ * 4, n_ * 4 + 4)],
                                writes=[("ps", b)])
                for n_ in range(2):
                    k = cnt["sa"] % 2
                    cnt["sa"] += 1
                    ba, bb = base + n_, base + 2 + n_
                    P.op("act", lambda e, k=k, ba=ba: e.activation(out=sa[k], in_=bank(ba), func=AF.Silu),
                         reads=[("ps", ba)], writes=[("sa", k)])
                    P.op("dve", lambda e, k=k, bb=bb, fc=fc, n_=n_: e.scalar_tensor_tensor(
                        out=gT[:, fc, n_ * 512:(n_ + 1) * 512], in0=sa[k], scalar=0.5, in1=bank(bb),
                        op0=ALU.mult, op1=ALU.mult),
                        reads=[("sa", k), ("ps", bb)], writes=[("gT", fc, n_)])
            tok_out_stage(lambda kc, t: gT[:, kc, t * 128:(t + 1) * 128],
                          lambda kc, t: [("gT", kc, t // 4)], FC, wb, resid, resid_res)
            layer_norm_pass(ln_idx, dst_dram, dst_res, want_T)

        def ffn_phase_bufs():
            o_ = set_ln_bufs(OFF_T)
            sa = []
            for k in range(2):
                sa.append(carve(o_, [128, 512], F32)); o_ += 2048
            xf1 = carve(o_, [128, 2048], F32); o_ += 8192
            LB["yfull2"] = xf1
            LB["yfull2_key"] = ("xf", 1)
            return sa, xf1

        pend = [None]

        def proj_tok(slot_idx, ncols, handler, PB):
            s = ring_load(w3[slot_idx], 4096)
            for t in range(NT):
                b = cnt["pj"] % 4
                cnt["pj"] += 1
                for dc in range(DC):
                    P.op("pe", lambda e, b=b, dc=dc, t=t, s=s: e.matmul(
                        bank(b, ncols), lhsT=actT[:, dc, t * 128:(t + 1) * 128],
                        rhs=ring[s][:, dc * 256:dc * 256 + ncols], start=(dc == 0), stop=(dc == DC - 1)),
                        reads=[("slot", s), ("actT", dc, t)], writes=[("ps", b)])
                if pend[0] is not None:
                    pend[0]()
                pend[0] = handler(t, b)
            if pend[0] is not None:
                pend[0]()
                pend[0] = None

        def rope_to_stg(b, stg, k, nh, hd, half, cos_r, sin_r, gt, tmps, ncopy=None):
            ncol = nh * hd
            ncp = ncopy or ncol
            P.op("act", lambda e: e.activation(out=stg[k][:, 0:ncp], in_=bank(b, ncp), func=AF.Copy),
                 reads=[("ps", b)], writes=[("stg", k)])
            if "rope" in os.environ.get("K_SKIP", ""):
                return
            s3 = stg[k][:, 0:ncol].rearrange("p (h d) -> p h d", h=nh)
            t1 = s3[:, :, 0:half]
            t2 = s3[:, :, half:2 * half]
            w = nh * half
            cosb = cos_r[:, gt, 0:w].rearrange("p (h d) -> p h d", h=nh)
            sinb = sin_r[:, gt, 0:w].rearrange("p (h d) -> p h d", h=nh)
            m = [tmps[i][:, 0:w].rearrange("p (h d) -> p h d", h=nh) for i in range(4)]
            for i, (a_, b_) in enumerate(((t1, cosb), (t2, sinb), (t2, cosb), (t1, sinb))):
                P.op("dve", lambda e, i=i, a_=a_, b_=b_: e.tensor_tensor(out=m[i], in0=a_, in1=b_, op=ALU.mult),
                     reads=[("stg", k), "trig_rep"], writes=[("ropetmp", i)])
            P.op("dve", lambda e: e.tensor_tensor(out=t1, in0=m[0], in1=m[1], op=ALU.subtract),
                 reads=[("ropetmp", 0), ("ropetmp", 1)], writes=[("stg", k)])
            P.op("dve", lambda e: e.tensor_tensor(out=t2, in0=m[2], in1=m[3], op=ALU.add),
                 reads=[("ropetmp", 2), ("ropetmp", 3)], writes=[("stg", k)])

        def proj_group(g, PB, kT_dst, V_dst, kiT_dst, key0):
            stg, tmps = PB["stg"], PB["tmps"]

            def nxt():
                k = cnt["stg"] % 2
                cnt["stg"] += 1
                return k

            if g == 1:
                for sq in range(8):
                    def hq(t, b, sq=sq):
                        k = nxt()
                        rope_to_stg(b, stg, k, 2, 128, 16, cosq_r, sinq_r, g * 8 + t, tmps)
                        return lambda: transpose_cols(stg[k], 256,
                                       lambda c, n, t=t: bufQ[:, 2 * sq + c:2 * sq + c + n, t * 128:(t + 1) * 128],
                                       [("stg", k)], lambda c, n, t=t: [("bufQ", 2 * sq + cc, t) for cc in range(c, c + n)])
                    proj_tok(S_Q + sq, 256, hq, PB)
            SK_ = os.environ.get("K_SKIP", "").split(",")
            for sk in range(0 if "nok" in SK_ else 2):
                def hk(t, b, sk=sk):
                    k = nxt()
                    rope_to_stg(b, stg, k, 2, 128, 16, cosq_r, sinq_r, g * 8 + t, tmps)
                    return lambda: transpose_cols(stg[k], 256,
                                   lambda c, n, t=t: kT_dst[:, 2 * sk + c:2 * sk + c + n, key0 + t * 128:key0 + (t + 1) * 128],
                                   [("stg", k)], lambda c, n, t=t: [("kT", g, 2 * sk + cc, t) for cc in range(c, c + n)])
                proj_tok(S_K + sk, 256, hk, PB)
            for sv_ in range(0 if "nov" in SK_ else 2):
                def hv(t, b, sv_=sv_):
                    dstv = V_dst[:, key0 // 128 + t, sv_ * 256:(sv_ + 1) * 256]
                    P.op("act", lambda e: e.activation(out=dstv, in_=bank(b, 256), func=AF.Copy),
                         reads=[("ps", b)], writes=[("V", g, t, sv_)])
                proj_tok(S_V + sv_, 256, hv, PB)
            if g == 1:
                for sq in range(4):
                    def hqi(t, b, sq=sq):
                        k = nxt()
                        rope_to_stg(b, stg, k, 4, 64, 8, cosi_r, sini_r, g * 8 + t, tmps)
                        return lambda: transpose_cols(stg[k], 256,
                                       lambda c, n, t=t: qiT[:, 2 * sq + c:2 * sq + c + n, t * 128:(t + 1) * 128],
                                       [("stg", k)], lambda c, n, t=t: [("qiT", 2 * sq + cc, t) for cc in range(c, c + n)])
                    proj_tok(S_QI + sq, 256, hqi, PB)

            def hki(t, b):
                k = nxt()
                rope_to_stg(b, stg, k, 1, 64, 8, cosi_r, sini_r, g * 8 + t, tmps, ncopy=128)
                if g == 1:
                    P.op("dve", lambda e: e.tensor_scalar(out=witm[:, t, :], in0=stg[k][:, 64:80], scalar1=WI_SCALE,
                                                          scalar2=None, op0=ALU.mult),
                         reads=[("stg", k)], writes=[("witm", t)])
                P.op("dve", lambda e: e.tensor_copy(out=stg[k][:, 64:128], in_=stg[k][:, 0:64]),
                     reads=[("stg", k)], writes=[("stg", k)])
                return lambda: transpose_cols(stg[k], 128,
                               lambda c, n, t=t: kiT_dst[:, key0 + t * 128:key0 + (t + 1) * 128].rearrange("p (a b) -> p a b", a=1),
                               [("stg", k)], lambda c, n, t=t: [("kiT", g, t)])
            if "noki" not in SK_:
                proj_tok(S_KIWI, 128, hki, PB)

        def rnn_group(g, RB_):
            gw = RB_["gw"]
            rxh, xc, xcb = RB_["rxh"], RB_["xc"], RB_["xcb"]
            tl = RB_["tl"]
            CW = lambda j, c: rnnp_s[:, j * RC + c:j * RC + c + 1]
            for n_ in range(2):
                tok = slice(n_ * 512, (n_ + 1) * 512)
                slot_rx = {}
                slot_rg = {}

                def conv(c, n_=n_, tok=tok):
                    if c % 2 == 0:
                        slot_rx[c // 2] = ring_load(w3[S_RX + c // 2], 4096)
                    s = slot_rx[c // 2]
                    b = cnt["rb"] % 4
                    cnt["rb"] += 1
                    for dc in range(DC):
                        P.op("pe", lambda e, dc=dc, b=b, s=s: e.matmul(
                            bank(b), lhsT=ring[s][:, dc * 256 + (c % 2) * 128:dc * 256 + (c % 2) * 128 + 128],
                            rhs=actT[:, dc, tok], start=(dc == 0), stop=(dc == DC - 1)),
                            reads=[("slot", s)] + [("actT", dc, tt) for tt in range(n_ * 4, n_ * 4 + 4)],
                            writes=[("ps", b)])
                    P.op("act", lambda e, b=b: e.activation(out=rxh[:, 4:516], in_=bank(b), func=AF.Copy),
                         reads=[("ps", b)], writes=["rxh"])
                    P.op("dve", lambda e: e.tensor_copy(out=rxh[:, 0:4], in_=rxhalo[:, c, :]),
                         reads=[("rxhalo", c)], writes=["rxh"])
                    xcc = xc[c % 4]
                    P.op("dve", lambda e: e.tensor_scalar(out=xcc, in0=rxh[:, 4:516], scalar1=CW(3, c), scalar2=CW(4, c),
                                                          op0=ALU.mult, op1=ALU.add),
                         reads=["rxh", "rnnp_s"], writes=[("xc", c % 4)])
                    for j in range(3):
                        P.op("dve", lambda e, j=j: e.scalar_tensor_tensor(out=xcc, in0=rxh[:, 1 + j:1 + j + 512], scalar=CW(j, c),
                                                                          in1=xcc, op0=ALU.mult, op1=ALU.add),
                             reads=["rxh", "rnnp_s", ("xc", c % 4)], writes=[("xc", c % 4)])
                    P.op("dve", lambda e: e.tensor_copy(out=rxhalo[:, c, :], in_=rxh[:, 512:516]),
                         reads=["rxh"], writes=[("rxhalo", c)])
                    P.op("act", lambda e: e.activation(out=xcb[c % 4], in_=xcc, func=AF.Copy),
                         reads=[("xc", c % 4)], writes=[("xcb", c % 4)])

                def gates(c, n_=n_, tok=tok):
                    r_, i_, a_, a2_, gx_, h_, u_, sg_, gl_ = tl[0:9]
                    if c % 2:
                        a_, a2_ = tl[9], tl[10]
                    ka, ka2 = ("a_", c % 2), ("a2_", c % 2)
                    bA = cnt["rb"] % 4
                    bX = (cnt["rb"] + 1) % 4
                    cnt["rb"] += 2
                    for gate, bb_ in ((0, bA), (1, bX)):
                        ks = [k for k, (j, i) in enumerate(pairs) if i == c]
                        for q_, k in enumerate(ks):
                            j = pairs[k][0]
                            P.op("pe", lambda e, k=k, j=j, gate=gate, bb_=bb_, q_=q_, ks=ks: e.matmul(
                                bank(bb_), lhsT=gw[gate][:, k, :], rhs=xcb[j % 4],
                                start=(q_ == 0), stop=(q_ == len(ks) - 1)),
                                reads=["gw", ("xcb", j % 4)], writes=[("ps", bb_)])
                    P.op("act", lambda e: e.activation(out=r_, in_=bank(bA), func=AF.Sigmoid, bias=CW(5, c), scale=1.0),
                         reads=[("ps", bA), "rnnp_s"], writes=["r_"])
                    P.op("act", lambda e: e.activation(out=i_, in_=bank(bX), func=AF.Sigmoid, bias=CW(6, c), scale=1.0),
                         reads=[("ps", bX), "rnnp_s"], writes=["i_"])
                    P.op("act", lambda e: e.activation(out=a_, in_=r_, func=AF.Exp, scale=c1_s[:, c:c + 1]),
                         reads=["r_", "c1_s"], writes=[ka])
                    P.op("act", lambda e: e.activation(out=a2_, in_=r_, func=AF.Exp, scale=c2_s[:, c:c + 1]),
                         reads=["r_", "c2_s"], writes=[ka2])
                    P.op("act", lambda e: e.activation(out=a2_, in_=a2_, func=AF.Sqrt, scale=-1.0, bias=onest[:, 0:1]),
                         reads=[ka2, "onest"], writes=[ka2])
                    P.op("dve", lambda e: e.tensor_tensor(out=gx_, in0=i_, in1=xc[c % 4], op=ALU.mult),
                         reads=["i_", ("xc", c % 4)], writes=["gx_"])
                    P.op("dve", lambda e: e.tensor_tensor(out=gx_, in0=gx_, in1=a2_, op=ALU.mult),
                         reads=["gx_", ka2], writes=["gx_"])
                    P.op("dve", lambda e: e.tensor_tensor_scan(out=h_, data0=a_, data1=gx_, initial=hlast[:, c:c + 1],
                                                               op0=ALU.mult, op1=ALU.add),
                         reads=[ka, "gx_", ("hlast", c)], writes=["h_"])
                    P.op("dve", lambda e: e.tensor_copy(out=hlast[:, c:c + 1], in_=h_[:, 511:512]),
                         reads=["h_"], writes=[("hlast", c)])
                    if upto == "pc" and g == 0 and n_ == 0 and c in (0, 1, 5) and "dumps" not in os.environ.get("K_SKIP", ""):
                        dump(xc[c % 4], 128, 512, [("xc", c % 4)])
                        dump(r_, 128, 512, ["r_"])
                        dump(i_, 128, 512, ["i_"])
                        dump(a_, 128, 512, ["a_"])
                        dump(gx_, 128, 512, ["gx_"])
                        dump(h_, 128, 512, ["h_"])
                    if g == 1:
                        if c % 2 == 0:
                            slot_rg[c // 2] = ring_load(w3[S_RG + c // 2], 4096)
                        s = slot_rg[c // 2]
                        bG = cnt["rb"] % 4
                        cnt["rb"] += 1
                        for dc in range(DC):
                            P.op("pe", lambda e, dc=dc, s=s: e.matmul(
                                bank(bG), lhsT=ring[s][:, dc * 256 + (c % 2) * 128:dc * 256 + (c % 2) * 128 + 128],
                                rhs=actT[:, dc, tok], start=(dc == 0), stop=(dc == DC - 1)),
                                reads=[("slot", s)] + [("actT", dc, tt) for tt in range(n_ * 4, n_ * 4 + 4)],
                                writes=[("ps", bG)])
                        P.op("act", lambda e: e.activation(out=u_, in_=bank(bG), func=AF.Square),
                             reads=[("ps", bG)], writes=["u_"])
                        P.op("dve", lambda e: e.tensor_scalar(out=u_, in0=u_, scalar1=0.044715, scalar2=1.0,
                                                              op0=ALU.mult, op1=ALU.add),
                             reads=["u_"], writes=["u_"])
                        P.op("dve", lambda e: e.tensor_tensor(out=u_, in0=u_, in1=bank(bG), op=ALU.mult),
                             reads=["u_", ("ps", bG)], writes=["u_"])
                        P.op("act", lambda e: e.activation(out=sg_, in_=u_, func=AF.Sigmoid, scale=1.5957691216),
                             reads=["u_"], writes=["sg_"])
                        P.op("dve", lambda e: e.tensor_tensor(out=gl_, in0=sg_, in1=bank(bG), op=ALU.mult),
                             reads=["sg_", ("ps", bG)], writes=["gl_"])
                        P.op("dve", lambda e: e.tensor_tensor(out=yrT[:, c, tok], in0=gl_, in1=h_, op=ALU.mult),
                             reads=["gl_", "h_"], writes=[("yrT", c, n_)])

                for c in range(RC + 1):
                    if c < RC:
                        conv(c)
                    if c >= 1 and "nogates" not in os.environ.get("K_SKIP", "").split(","):
                        gates(c - 1)

        def rnn_bufs(off):
            o_ = off
            RB_ = {}
            g0 = carve(o_, [128, NP, 128], BF16); o_ += NP * 256
            g1 = carve(o_, [128, NP, 128], BF16); o_ += NP * 256
            RB_["gw"] = (g0, g1)
            RB_["rxh"] = carve(o_, [128, 516], F32); o_ += 2064
            RB_["xc"] = []
            for k in range(4):
                RB_["xc"].append(carve(o_, [128, 512], F32)); o_ += 2048
            RB_["xcb"] = []
            for k in range(4):
                RB_["xcb"].append(carve(o_, [128, 512], BF16)); o_ += 1024
            RB_["tl"] = []
            for k in range(11):
                RB_["tl"].append(carve(o_, [128, 512], F32)); o_ += 2048
            for gi, (gwt, src) in enumerate(((g0, wga), (g1, wgx))):
                for q_ in range(4):
                    n0 = q_ * 13
                    P.dma("pool", lambda e, gwt=gwt, src=src, n0=n0: e.dma_start(
                        out=gwt.rearrange("p a b -> p (a b)")[:, n0 * 128:(n0 + 13) * 128], in_=src[:, n0 * 128:(n0 + 13) * 128]),
                        writes=["gw"], tag=("gw", gi, q_))
            return RB_

        RNN_OFF = 106496
        sa, xf1 = ffn_phase_bufs()
        load_xT(0, xf1)
        ffn(x2[0:1024, :], "x2c", w1a, w1b, 0, dbg_out[3072:4096, :] if upto in ("ffnc", "pc") else None, "dbgh", True, sa)
        if upto in ("ffnc", "pc"):
            final_tags += [("hst_dbgh", 0), ("hst_dbgh", 1)]
        barrier()
        kTc = carve(OFF_Q, [128, 4, 1024], BF16)
        Vc = carve(OFF_Q + 8192, [128, 8, 512], BF16)
        kiTc = carve(OFF_Q + 16384, [128, 1024], BF16)
        PB = dict(stg=[carve(OFF_Q + 20480, [128, 256], F32), carve(OFF_Q + 21504, [128, 256], F32)],
                  tmps=[carve(OFF_Q + 22528 + 128 * i_, [128, 32], F32) for i_ in range(4)])
        import os
        SKIP = os.environ.get("K_SKIP", "").split(",")
        if "proj" not in SKIP:
            proj_group(0, PB, kTc, Vc, kiTc, 0)
        P.dma("sp", lambda e: e.dma_start(out=kTc_d, in_=kTc.rearrange("p a b -> p (a b)")),
              reads=[("kT", 0, h_, t_) for h_ in range(4) for t_ in range(8)], writes=["kTc_d"], tag="kTc_d")
        P.dma("sp", lambda e: e.dma_start(out=Vc_d, in_=Vc.rearrange("p a b -> p (a b)")),
              reads=[("V", 0, t_, s_) for t_ in range(8) for s_ in range(2)], writes=["Vc_d"], tag="Vc_d")
        P.dma("sp", lambda e: e.dma_start(out=kiTc_d, in_=kiTc),
              reads=[("kiT", 0, t_) for t_ in range(8)], writes=["kiTc_d"], tag="kiTc_d")
        if "rnn" not in SKIP:
            RB_ = rnn_bufs(RNN_OFF)
            rnn_group(0, RB_)
        P.op("dve", lambda e: e.tensor_scalar(out=hlast[:], in0=hlast[:], scalar1=flag_s[:, 0:1], scalar2=None, op0=ALU.mult),
             reads=[("hlast", c_) for c_ in range(RC)] + ["flag_s"], writes=[("hlast", c_) for c_ in range(RC)])
        P.op("dve", lambda e: e.tensor_scalar(out=rxhalo[:].rearrange("p a b -> p (a b)"),
                                              in0=rxhalo[:].rearrange("p a b -> p (a b)"),
                                              scalar1=flag_s[:, 0:1], scalar2=None, op0=ALU.mult),
             reads=[("rxhalo", c_) for c_ in range(RC)] + ["flag_s"], writes=[("rxhalo", c_) for c_ in range(RC)])
        if upto == "pc" and "dumps" not in SKIP:
            dump(kTc.rearrange("p a b -> p (a b)")[:, 0:2048], 128, 2048, [("kT", 0, h_, t_) for h_ in range(4) for t_ in range(8)], is_bf16=True)
            dump(Vc.rearrange("p a b -> p (a b)")[:, 0:2048], 128, 2048, [("V", 0, t_, s_) for t_ in range(8) for s_ in range(2)], is_bf16=True)
            dump(kiTc[:, 0:1024], 128, 1024, [("kiT", 0, t_) for t_ in range(8)], is_bf16=True)
            dump(hlast[:], 128, RC, [("hlast", c_) for c_ in range(RC)])
            dump(rxhalo[:].rearrange("p a b -> p (a b)"), 128, 4 * RC, [("rxhalo", c_) for c_ in range(RC)])
        barrier()
        if upto in ("ffnc", "pc"):
            P.dma("sp", lambda e: e.dma_start(out=out[0:128, :], in_=LB["yfull"]), reads=[("yfull",)], tag="outd")
            final_tags.append("outd")
            P.emit(final_wait_tags=final_tags)
            return nc, P
        sa, xf1 = ffn_phase_bufs()
        load_xT(1024, xf1)
        ffn(x2[1024:2048, :], "x2o", w1a, w1b, 0, h1d, "h1d", True, sa)
        barrier()
        o_ = OFF_T + 45056
        PB = dict(stg=[carve(o_, [128, 256], F32), carve(o_ + 1024, [128, 256], F32)],
                  tmps=[carve(o_ + 2048 + 128 * i_, [128, 32], F32) for i_ in range(4)])
        P.dma("sp", lambda e: e.dma_start(out=kT[:, :, 0:1024], in_=kTc_d.rearrange("p (a b) -> p a b", a=4)),
              reads=["kTc_d"], writes=[("kT", 0, h_, t_) for h_ in range(4) for t_ in range(8)], tag="kTr")
        P.dma("sp", lambda e: e.dma_start(out=Vt[:, 0:8, :], in_=Vc_d.rearrange("p (a b) -> p a b", a=8)),
              reads=["Vc_d"], writes=[("V", 0, t_, s_) for t_ in range(8) for s_ in range(2)], tag="Vr")
        P.dma("sp", lambda e: e.dma_start(out=kiT[:, 0:1024], in_=kiTc_d),
              reads=["kiTc_d"], writes=[("kiT", 0, t_) for t_ in range(8)], tag="kiTr")
        proj_group(1, PB, kT, Vt, kiT, 1024)
        if upto == "po":
            dump(bufQ[:, 0, :], 128, 1024, [("bufQ", 0, t_) for t_ in range(8)], is_bf16=True)
            dump(bufQ[:, 5, :], 128, 1024, [("bufQ", 5, t_) for t_ in range(8)], is_bf16=True)
            dump(kT[:, 1, :], 128, 2048, [("kT", g_, 1, t_) for g_ in range(2) for t_ in range(8)], is_bf16=True)
            dump(qiT[:, 3, :], 128, 1024, [("qiT", 3, t_) for t_ in range(8)], is_bf16=True)
            dump(kiT[:, :], 128, 2048, [("kiT", g_, t_) for g_ in range(2) for t_ in range(8)], is_bf16=True)
            dump(witm[:].rearrange("p a b -> p (a b)"), 128, 128, [("witm", t_) for t_ in range(8)])
            dump(Vt[:, 9, :], 128, 512, [("V", 1, 1, s_) for s_ in range(2)], is_bf16=True)
        barrier()
        h1T_d = nc.dram_tensor("h1T_d", [128, DC * T], BF16).ap()
        P.dma("sp", lambda e: e.dma_start(out=h1T_d, in_=actT.rearrange("p a b -> p (a b)")),
              reads=[("actT", dc_, t_) for dc_ in range(DC) for t_ in range(NT)], writes=["h1T_d"], tag="h1T_st")
        barrier()
        NQB = 8 if upto in ("all", "att", "rn", "mg") else 2
        if upto == "po":
            NQB = 0
        NKs = [1152 + 128 * j for j in range(8)]
        mb = []
        o_ = OFF_ACTT
        for j in range(8):
            mb.append(carve(o_, [128, NKs[j]], BF16)); o_ += NKs[j] * 2
        yh = [carve(o_, [128, 128], F32), carve(o_ + 512, [128, 128], F32)]
        o_ = OFF_T
        accs = []
        for k in range(4):
            accs.append(carve(o_, [128, 2048], F32)); o_ += 8192
        rls = [carve(o_, [128, 2048], F32), carve(o_ + 8192, [128, 2048], F32)]
        S = [ring[0][:, :].bitcast(F32), ring[1][:, :].bitcast(F32)]
        PTs2 = [ring[2][:, 0:2048].rearrange("p (a b) -> p a b", b=128),
                ring[2][:, 2048:4096].rearrange("p (a b) -> p a b", b=128)]
        m8s = [m8, sb("m8b", [128, 8], F32)]
        thrs = [thr, sb("thrb", [128, 1], F32)]
        hcnt = [0]

        def indexer_head(j, h):
            NK = NKs[j]
            nkc = 9 + j
            ai = j % 4
            acc = accs[ai]
            kres = [("kiT", kc // 8, kc % 8) for kc in range(nkc)]
            if h == 0:
                P.dma("sp", lambda e: e.dma_start(out=acc[:, 0:NK], in_=cbias[j * 128:(j + 1) * 128, 0:NK]),
                      writes=[("acc", ai)], tag=("cb", ai))
            pair, hf = h // 2, h % 2
            hp = hcnt[0] % 2
            hcnt[0] += 1
            rl = rls[hp]
            for half in range(2):
                k0 = half * 1024
                k1 = min(NK, k0 + 1024)
                if k1 <= k0:
                    continue
                for kb in range(2):
                    c0 = k0 + kb * 512
                    n = min(512, k1 - c0)
                    if n <= 0:
                        continue
                    P.op("pe", lambda e, kb=kb, n=n, c0=c0: e.matmul(
                        ps[:, kb * 512:kb * 512 + n],
                        lhsT=qiT[hf * 64:(hf + 1) * 64, pair, j * 128:(j + 1) * 128],
                        rhs=kiT[hf * 64:(hf + 1) * 64, c0:c0 + n], start=True, stop=True),
                        reads=[("qiT", pair, j)] + kres, writes=[("ps", kb)])
                P.op("act", lambda e, k0=k0, k1=k1: e.activation(
                    out=rl[:, k0:k1], in_=ps[:, 0:k1 - k0], func=AF.Relu),
                    reads=[("ps", 0), ("ps", 1)], writes=[("rl", hp)])
            if j < 2:
                P.op("dve", lambda e: e.scalar_tensor_tensor(
                    out=acc[:, 0:NK], in0=rl[:, 0:NK], scalar=witm[:, j, h:h + 1], in1=acc[:, 0:NK],
                    op0=ALU.mult, op1=ALU.add),
                    reads=[("rl", hp), ("witm", j), ("acc", ai)], writes=[("acc", ai)])
                return
            P.op("pool", lambda e: e.tensor_scalar(
                out=rl[:, 0:NK], in0=rl[:, 0:NK], scalar1=witm[:, j, h:h + 1], scalar2=0.0,
                op0=ALU.mult, op1=ALU.add),
                reads=[("rl", hp), ("witm", j)], writes=[("rl", hp)])
            P.op("pool", lambda e: e.tensor_tensor(out=acc[:, 0:NK], in0=acc[:, 0:NK], in1=rl[:, 0:NK], op=ALU.add),
                 reads=[("rl", hp), ("acc", ai)], writes=[("acc", ai)])

        def qk(j, h):
            NK = NKs[j]
            nkc = 9 + j
            nb = (NK + 511) // 512
            kvh = h // 4
            for kb in range(nb):
                n = min(512, NK - kb * 512)
                P.op("pe", lambda e, kb=kb, n=n: e.matmul(
                    ps[:, (2 + kb) * 512:(2 + kb) * 512 + n], lhsT=bufQ[:, h, j * 128:(j + 1) * 128],
                    rhs=kT[:, kvh, kb * 512:kb * 512 + n], start=True, stop=True),
                    reads=[("bufQ", h, j)] + [("kT", kc // 8, kvh, kc % 8) for kc in range(nkc)],
                    writes=[("ps", 2 + kb)])

        def a2_head(j, h):
            NK = NKs[j]
            nkc = 9 + j
            nb = (NK + 511) // 512
            kvh = h // 4
            hh = h % 2
            Sm = S[hh]
            PTs = PTs2[hh]
            sk = ("slot", hh)
            if h == 0:
                qk(j, 0)
            P.op("dve", lambda e: e.scalar_tensor_tensor(
                out=Sm[:, 0:NK], in0=ps[:, 1024:1024 + NK], scalar=SCALE, in1=mb[j][:, 0:NK], op0=ALU.mult, op1=ALU.add),
                reads=[("ps", 2 + kb) for kb in range(nb)] + [("mb", j), sk], writes=[("S", hh)])
            if h < 15:
                qk(j, h + 1)
            P.op("dve", lambda e: e.reduce_max(out=mx[:, hh:hh + 1], in_=Sm[:, 0:NK], axis=AX.X, negate=True),
                 reads=[("S", hh), sk], writes=[("mx", hh)])
            P.op("act", lambda e: e.activation(
                out=Sm[:, 0:NK], in_=Sm[:, 0:NK], func=AF.Exp, bias=mx[:, hh:hh + 1], scale=1.0,
                accum_out=rs[:, hh:hh + 1]),
                reads=[("S", hh), ("mx", hh), sk], writes=[("S", hh), ("rs", hh)])
            kc = 0
            while kc < nkc:
                n = min(4, nkc - kc)
                for i in range(n):
                    P.op("pe", lambda e, i=i, kc=kc: e.transpose(
                        bank(6, 128, i * 128), Sm[:, (kc + i) * 128:(kc + i + 1) * 128], identf[:]),
                        reads=[("S", hh), "identf", sk], writes=[("ps", 6)])
                P.op("act", lambda e, kc=kc, n=n: e.activation(
                    out=PTs[:, kc:kc + n, :], in_=bank(6, n * 128).rearrange("p (a b) -> p a b", b=128), func=AF.Copy),
                    reads=[("ps", 6), ("slot", 2)], writes=[("PTs", hh)])
                kc += n
            for kc in range(nkc):
                P.op("pe", lambda e, kc=kc: e.matmul(
                    bank(7, 128), lhsT=PTs[:, kc, :], rhs=Vt[:, kc, kvh * 128:(kvh + 1) * 128],
                    start=(kc == 0), stop=(kc == nkc - 1)),
                    reads=[("PTs", hh), ("V", kc // 8, kc % 8, kvh // 2), ("slot", 2)], writes=[("ps", 7)])
            P.op("dve", lambda e: e.reciprocal(out=rinv[:, hh:hh + 1], in_=rs[:, hh:hh + 1]),
                 reads=[("rs", hh)], writes=[("rinv", hh)])
            P.op("act", lambda e: e.activation(
                out=yh[hh], in_=bank(7, 128), func=AF.Copy, scale=rinv[:, hh:hh + 1]),
                reads=[("ps", 7), ("rinv", hh)], writes=[("yh", hh)])
            P.op("pe", lambda e: e.transpose(bank(6, 128, 0), yh[hh], identf[:]),
                 reads=[("yh", hh), "identf"], writes=[("ps", 6)])
            P.op("act", lambda e: e.activation(out=bufQ[:, h, j * 128:(j + 1) * 128], in_=bank(6, 128), func=AF.Copy),
                 reads=[("ps", 6)], writes=[("bufQ", h, j)])

        def topk_pair(js, fillers):
            for r_ in range(32):
                for c_, j in enumerate(js):
                    NK = NKs[j]
                    acc = accs[j % 4]
                    P.op("dve", lambda e, acc=acc, NK=NK, c_=c_: e.max(out=m8s[c_][:], in_=acc[:, 0:NK]),
                         reads=[("acc", j % 4)], writes=[("m8", c_)])
                    if r_ < 31:
                        P.op("dve", lambda e, acc=acc, NK=NK, c_=c_: e.match_replace(
                            out=acc[:, 0:NK], in_to_replace=m8s[c_][:], in_values=acc[:, 0:NK], imm_value=-3.0e38),
                            reads=[("acc", j % 4), ("m8", c_)], writes=[("acc", j % 4)])
                for f_ in fillers[r_]:
                    f_()
            for c_, j in enumerate(js):
                NK = NKs[j]
                acc = accs[j % 4]
                P.op("dve", lambda e, c_=c_: e.tensor_scalar(out=thrs[c_][:], in0=m8s[c_][:, 7:8], scalar1=-1.0e29,
                                                             scalar2=None, op0=ALU.max),
                     reads=[("m8", c_)], writes=[("thr", c_)])
                P.op("dve", lambda e, acc=acc, NK=NK, j=j: e.tensor_scalar(
                    out=mb[j][:, 0:NK], in0=acc[:, 0:NK], scalar1=-2.0e38, scalar2=NEG, op0=ALU.is_gt, op1=ALU.mult),
                    reads=[("acc", j % 4)], writes=[("mb", j)])
                P.op("dve", lambda e, acc=acc, NK=NK, j=j, c_=c_: e.scalar_tensor_tensor(
                    out=mb[j][:, 0:NK], in0=acc[:, 0:NK], scalar=thrs[c_][:, 0:1], in1=mb[j][:, 0:NK],
                    op0=ALU.is_lt, op1=ALU.mult),
                    reads=[("acc", j % 4), ("thr", c_), ("mb", j)], writes=[("mb", j)])
                if j < 2:
                    P.dma("sp", lambda e, acc=acc, NK=NK, j=j: e.dma_start(
                        out=acc[:, 0:NK], in_=cbias[j * 128:(j + 1) * 128, 0:NK]),
                        reads=[("mb", j)], writes=[("acc", j % 4)], tag=("cb", j % 4))
                    P.op("dve", lambda e, acc=acc, NK=NK, j=j: e.tensor_tensor(
                        out=mb[j][:, 0:NK], in0=mb[j][:, 0:NK], in1=acc[:, 0:NK], op=ALU.add),
                        reads=[("acc", j % 4), ("mb", j)], writes=[("mb", j)])

        npair = NQB // 2
        if npair:
            for h in range(16):
                indexer_head(0, h)
            for h in range(16):
                indexer_head(1, h)
        for p_ in range(npair):
            fillers = [[] for _ in range(32)]
            if p_ + 1 < npair:
                for i_ in range(32):
                    fillers[i_].append(lambda i_=i_, p_=p_: indexer_head(2 * (p_ + 1) + i_ // 16, i_ % 16))
            if p_ >= 1:
                for i_ in range(32):
                    fillers[i_].append(lambda i_=i_, p_=p_: a2_head(2 * (p_ - 1) + i_ // 16, i_ % 16))
            topk_pair((2 * p_, 2 * p_ + 1), fillers)
        if npair:
            for i_ in range(32):
                a2_head(2 * (npair - 1) + i_ // 16, i_ % 16)
        if upto == "att":
            dump(mb[1][:, :], 128, NKs[1], [("mb", 1)], is_bf16=True)
            dump(bufQ[:, :, 128:256], 128, 2048, [("bufQ", h_, 1) for h_ in range(16)], is_bf16=True)
        barrier()
        P.dma("sp", lambda e: e.dma_start(out=actT.rearrange("p a b -> p (a b)"), in_=h1T_d),
              reads=["h1T_d"], writes=[("actT", dc_, t_) for dc_ in range(DC) for t_ in range(NT)], tag="h1T_ld")
        barrier()
        if upto in ("po", "att"):
            P.dma("sp", lambda e: e.dma_start(out=out[0:128, :], in_=LB["yfull"]), reads=[("yfull",)], tag="outd")
            final_tags.append("outd")
            P.emit(final_wait_tags=final_tags)
            return nc, P
        RB_ = rnn_bufs(RNN_OFF)
        rnn_group(1, RB_)
        if upto == "rn":
            dump(yrT[:, 0, :], 128, 1024, [("yrT", 0, n_) for n_ in range(2)], is_bf16=True)
            dump(yrT[:, 7, :], 128, 1024, [("yrT", 7, n_) for n_ in range(2)], is_bf16=True)
            dump(yrT[:, 19, :], 128, 1024, [("yrT", 19, n_) for n_ in range(2)], is_bf16=True)
            dump(bufQ[:, 3, :], 128, 1024, [("bufQ", 3, t_) for t_ in range(8)], is_bf16=True)
            P.dma("sp", lambda e: e.dma_start(out=out[0:128, :], in_=LB["yfull"]), reads=[("yfull",)], tag="outd")
            final_tags.append("outd")
            P.emit(final_wait_tags=final_tags)
            return nc, P
        barrier()
        MOFF = 106496
        mergedT = carve(MOFF, [128, 16, T], BF16)
        o_ = MOFF + 32768
        sga, sgr, m1 = [], [], []
        for lst in (sga, sgr, m1):
            for k in range(4):
                lst.append(carve(o_, [128, 512], F32)); o_ += 2048
        LB["rt"] = [carve(o_, [128, 512], F32), carve(o_ + 2048, [128, 512], F32)]
        LB["yt"] = [carve(o_ + 4096, [128, 512], F32), carve(o_ + 6144, [128, 512], F32)]
        for op_ in range(8):
            for which, sidx, dstl, b0 in ((0, S_G + op_, sga, 0), (1, S_G + 8 + op_, sgr, 4)):
                s = ring_load(w3[sidx], 4096)
                for q in range(2):
                    for n_ in range(2):
                        b = b0 + q * 2 + n_
                        for dc in range(DC):
                            P.op("pe", lambda e, b=b, dc=dc, q=q, n_=n_, s=s: e.matmul(
                                bank(b), lhsT=ring[s][:, dc * 256 + q * 128:dc * 256 + q * 128 + 128],
                                rhs=actT[:, dc, n_ * 512:(n_ + 1) * 512], start=(dc == 0), stop=(dc == DC - 1)),
                                reads=[("slot", s)] + [("actT", dc, tt) for tt in range(n_ * 4, n_ * 4 + 4)],
                                writes=[("ps", b)])
                        P.op("act", lambda e, b=b, d_=dstl[q * 2 + n_]: e.activation(out=d_, in_=bank(b), func=AF.Sigmoid),
                             reads=[("ps", b)], writes=[("sg", which, q * 2 + n_)])
            s = ring_load(wab[op_], 4096)
            for q in range(2):
                for n_ in range(2):
                    b = q * 2 + n_
                    for kc in range(16):
                        P.op("pe", lambda e, b=b, kc=kc, q=q, n_=n_, s=s: e.matmul(
                            bank(b), lhsT=ring[s][:, kc * 256 + q * 128:kc * 256 + q * 128 + 128],
                            rhs=bufQ[:, kc, n_ * 512:(n_ + 1) * 512], start=(kc == 0), stop=(kc == 15)),
                            reads=[("slot", s)] + [("bufQ", kc, tt) for tt in range(n_ * 4, n_ * 4 + 4)],
                            writes=[("ps", b)])
                    P.op("dve", lambda e, b=b, i_=q * 2 + n_: e.tensor_tensor(out=m1[i_], in0=sga[i_], in1=bank(b), op=ALU.mult),
                         reads=[("ps", b), ("sg", 0, q * 2 + n_)], writes=[("m1", q * 2 + n_)])
            for q in range(2):
                oc = 2 * op_ + q
                s = ring_load(wrb[oc], 2560)
                for n_ in range(2):
                    b = 4 + q * 2 + n_
                    for kc in range(RC):
                        P.op("pe", lambda e, b=b, kc=kc, n_=n_, s=s: e.matmul(
                            bank(b), lhsT=ring[s][:, kc * 128:(kc + 1) * 128],
                            rhs=yrT[:, kc, n_ * 512:(n_ + 1) * 512], start=(kc == 0), stop=(kc == RC - 1)),
                            reads=[("slot", s), ("yrT", kc, n_)], writes=[("ps", b)])
                    i_ = q * 2 + n_
                    P.op("dve", lambda e, b=b, i_=i_: e.tensor_tensor(out=sgr[i_], in0=sgr[i_], in1=bank(b), op=ALU.mult),
                         reads=[("ps", b), ("sg", 1, i_)], writes=[("sg", 1, i_)])
                    P.op("dve", lambda e, i_=i_, oc=oc, n_=n_: e.tensor_tensor(
                        out=mergedT[:, oc, n_ * 512:(n_ + 1) * 512], in0=m1[i_], in1=sgr[i_], op=ALU.add),
                        reads=[("m1", i_), ("sg", 1, i_)], writes=[("mT", oc, n_)])
        if upto == "mg":
            dump(mergedT[:, 0, :], 128, 1024, [("mT", 0, n_) for n_ in range(2)], is_bf16=True)
            dump(mergedT[:, 9, :], 128, 1024, [("mT", 9, n_) for n_ in range(2)], is_bf16=True)
        tok_out_stage(lambda kc, t: mergedT[:, kc, t * 128:(t + 1) * 128],
                      lambda kc, t: [("mT", kc, t // 4)], 16, wo, h1d, "h1d")
        barrier()
        LB["yfull"] = carve(MOFF, [128, 2048], F32)
        LB["gB"] = carve(MOFF + 8192, [128, 2048], F32)
        LB["bB"] = carve(MOFF + 16384, [128, 2048], F32)
        LB["yfull2"] = carve(MOFF + 24576, [128, 2048], F32)
        LB["yfull2_key"] = ("yfull2",)
        layer_norm_pass(1, h2d if upto != "mg" else dbg_out[3072:4096, :], "h2d", True)
        barrier()
        if upto == "mg":
            final_tags += [("hst_h2d", 0), ("hst_h2d", 1)]
            P.dma("sp", lambda e: e.dma_start(out=out[0:128, :], in_=LB["yfull"]), reads=[("yfull",)], tag="outd")
            final_tags.append("outd")
            P.emit(final_wait_tags=final_tags)
            return nc, P
        sa, xf1 = ffn_phase_bufs()
        ffn(h2d, "h2d", w2a, w2b, 2, out, "out", False, sa)
        final_tags += [("hst_out", 0), ("hst_out", 1)]
        P.emit(final_wait_tags=final_tags)
    return nc, P


def _tile_a(w, c0, ncols, pad):
    K = w.shape[0]
    blk = np.zeros((128, K // 128, pad), np.float32)
    blk[:, :, :ncols] = w[:, c0:c0 + ncols].reshape(K // 128, 128, ncols).transpose(1, 0, 2)
    return blk


def _pack_ffn(w_in, w_out):
    wa = np.empty((FC, 128, 2, 16, 128), np.float32)
    for fc in range(FC):
        wa[fc, :, 0] = _tile_a(w_in, fc * 128, 128, 128)
        wa[fc, :, 1] = _tile_a(w_in, FF + fc * 128, 128, 128)
    wb = np.empty((4, 11, 128, 4, 512), np.float32)
    wo = w_out.reshape(11, 4, 128, 4, 512)
    wb[:] = wo.transpose(3, 0, 2, 1, 4)
    return np.ascontiguousarray(wa.reshape(FC, 128, 4096)), np.ascontiguousarray(wb.reshape(FC, 128, 2048))


def _pack_shared(inp):
    sh = {}
    sh["w1a"], sh["w1b"] = _pack_ffn(inp["ffn1_w_in"][0], inp["ffn1_w_out"][0])
    sh["w2a"], sh["w2b"] = _pack_ffn(inp["ffn2_w_in"][0], inp["ffn2_w_out"][0])
    w_in = inp["w_in"][0]
    sl = w3_slots()
    w3 = np.empty((len(sl), 128, 16, 256), np.float32)
    for i, (c0, nc_) in enumerate(sl):
        w3[i] = _tile_a(w_in, c0, nc_, 256)
    sh["w3"] = w3.reshape(len(sl), 128, 4096)
    wab = np.empty((8, 128, 16, 256), np.float32)
    for i in range(8):
        wab[i] = _tile_a(inp["w_attn_branch"][0], i * 256, 256, 256)
    sh["wab"] = wab.reshape(8, 128, 4096)
    wrb = np.empty((16, 128, 20, 128), np.float32)
    for i in range(16):
        wrb[i] = _tile_a(inp["w_rnn_branch"][0], i * 128, 128, 128)
    sh["wrb"] = wrb.reshape(16, 128, 2560)
    wo = inp["w_out"][0].reshape(4, 4, 128, 4, 512)
    sh["wo"] = np.ascontiguousarray(wo.transpose(3, 0, 2, 1, 4)).reshape(16, 128, 2048)
    pairs = gate_pairs()
    for name, key in (("wga", "lru_wa"), ("wgx", "lru_wx")):
        bd = np.zeros((DRNN, DRNN), np.float32)
        for nb in range(16):
            bd[nb * 160:(nb + 1) * 160, nb * 160:(nb + 1) * 160] = inp[key][0, nb]
        g = np.empty((128, len(pairs), 128), np.float32)
        for k, (j, i) in enumerate(pairs):
            g[:, k, :] = bd[j * 128:(j + 1) * 128, i * 128:(i + 1) * 128]
        sh[name] = g.reshape(128, len(pairs) * 128)
    sh["lnp"] = np.stack([inp["ln1_g"][0], inp["ln1_b"][0], inp["ln2_g"][0], inp["ln2_b"][0],
                          inp["ln3_g"][0], inp["ln3_b"][0]]).astype(np.float32)
    rn = np.empty((128, 8, RC), np.float32)
    for j in range(4):
        rn[:, j, :] = inp["conv_w"][0, j].reshape(RC, 128).T
    rn[:, 4, :] = inp["conv_b"][0].reshape(RC, 128).T
    rn[:, 5, :] = inp["lru_ba"][0].reshape(RC, 128).T
    rn[:, 6, :] = inp["lru_bx"][0].reshape(RC, 128).T
    rn[:, 7, :] = inp["lru_lambda"][0].reshape(RC, 128).T
    sh["rnnp"] = rn.reshape(128, 8 * RC)
    sh["ident"] = np.eye(128, dtype=np.float32)
    invf = np.zeros((128, 24), np.float32)
    invf[:, 0:16] = (500000.0 ** (-(np.arange(16, dtype=np.float32) * 2.0 / 32.0))).astype(np.float32)[None]
    invf[:, 16:24] = (500000.0 ** (-(np.arange(8, dtype=np.float32) * 2.0 / 16.0))).astype(np.float32)[None]
    sh["invf"] = invf
    return sh


def _core_inputs(inp, sh, c):
    b, half = c // 2, c % 2
    x = inp["x"]
    pos = inp["positions"]
    x2 = np.zeros((2048, 2048), np.float32)
    p2 = np.zeros((2, 1024), np.int32)
    if half == 1:
        x2[:1024] = x[b, :1024]
        p2[0] = pos[b, :1024]
    x2[1024:] = x[b, half * 1024:(half + 1) * 1024]
    p2[1] = pos[b, half * 1024:(half + 1) * 1024]
    posi = np.ascontiguousarray(p2.reshape(2, 8, 128).transpose(2, 0, 1).reshape(128, 16))
    q = np.arange(1024)[:, None] + 1024
    k = np.arange(2048)[None, :]
    valid = (k <= q) & (k >= (0 if half == 1 else 1024))
    cb = np.where(valid, 0.0, NEG).astype(np.float32)
    m = dict(sh)
    m.update(x2=x2, posi=posi, cbias=cb, flag=np.full((128, 1), float(half), np.float32))
    return m


_CACHE = {}


def kernel(**inputs):
    inp = {k: np.asarray(v) for k, v in inputs.items()}
    if "nc" not in _CACHE:
        _CACHE["nc"] = build("all")[0]
    nc = _CACHE["nc"]
    sh = _pack_shared(inp)
    in_maps = [_core_inputs(inp, sh, c) for c in range(8)]
    res = run_bass_kernel_spmd(nc, in_maps, core_ids=list(range(8)))
    outp = np.empty((4, 2048, 2048), np.float32)
    for c in range(8):
        b, half = c // 2, c % 2
        outp[b, half * 1024:(half + 1) * 1024] = res.results[c]["out"]
    return outp
```

```python
import contextlib
import os
import math
import numpy as np
import concourse.bass as bass
import concourse.mybir as mybir
from concourse.bass_utils import run_bass_kernel_spmd

F32 = mybir.dt.float32
BF16 = mybir.dt.bfloat16
I32 = mybir.dt.int32
AF = mybir.ActivationFunctionType
ALU = mybir.AluOpType
AX = mybir.AxisListType

D = 2048
DC = 16
T = 1024
NT = 8
FF = 5632
FC = 44
DRNN = 2560
RC = 20
ALPHA = 2.0 ** 0.25
LN_EPS = 1e-5
SCALE = 128 ** -0.5
WI_SCALE = 1.0 / 32.0
NEG = -1.0e30
ENGINES = ("pe", "act", "dve", "pool", "sp")
EPOCH = 12000


class Prog:
    def __init__(self, nc):
        self.nc = nc
        self.ops = []

    def op(self, eng, fn, reads=(), writes=(), phase=True):
        r = tuple(reads) + (("PHASE",) if phase else ())
        self.ops.append(dict(eng=eng, fn=fn, reads=r, writes=tuple(writes), dma=None))

    def dma(self, eng, fn, reads=(), writes=(), tag=None, phase=True):
        r = tuple(reads) + (("PHASE",) if phase else ())
        self.ops.append(dict(eng=eng, fn=fn, reads=r, writes=tuple(writes), dma=tag))

    def emit(self, final_wait_tags=()):
        nc = self.nc
        ops = self.ops
        n = len(ops)
        last_write = {}
        readers = {}
        need = [None] * n
        signal = [False] * n
        for i, o in enumerate(ops):
            d = set()
            for r in o["reads"]:
                lw = last_write.get(r)
                if lw is not None:
                    d.add(lw)
            for w in o["writes"]:
                lw = last_write.get(w)
                if lw is not None:
                    d.add(lw)
                rd = readers.get(w)
                if rd:
                    d.update(rd[0].values())
                    d.update(rd[1])
            d.discard(i)
            lst = []
            for j in d:
                oj = ops[j]
                if oj["dma"] is None:
                    if oj["eng"] == o["eng"] and oj["eng"] == "pe" and o["dma"] is None:
                        continue
                    signal[j] = True
                lst.append(j)
            need[i] = lst
            for r in o["reads"]:
                rd = readers.setdefault(r, ({}, []))
                if o["dma"] is None:
                    rd[0][o["eng"]] = i
                else:
                    rd[1].append(i)
            for w in o["writes"]:
                last_write[w] = i
                readers[w] = ({}, [])
        eng_count = {e: 0 for e in ENGINES}
        tag_count = {}
        semkey = [None] * n
        val = [0] * n
        keys = []
        for i, o in enumerate(ops):
            if o["dma"] is not None:
                t = o["dma"]
                c = tag_count.get(t, 0) + 16
                tag_count[t] = c
                k = ("t", t, c // 32000)
                val[i] = c - (c // 32000) * 32000 if c // 32000 else c
                assert c < 32000, t
                k = ("t", t, 0)
                val[i] = c
            elif signal[i]:
                eng_count[o["eng"]] += 1
                c = eng_count[o["eng"]]
                ep = (c - 1) // EPOCH
                k = ("e", o["eng"], ep)
                val[i] = c - ep * EPOCH
            else:
                continue
            semkey[i] = k
            if k not in keys:
                keys.append(k)
        self.stats = dict(n_ops=n, eng_count=dict(eng_count), n_sems=len(keys))
        with contextlib.ExitStack() as st:
            sems = {}
            for idx, k in enumerate(keys):
                sems[k] = st.enter_context(nc.semaphore("sm%d" % idx))
            block = st.enter_context(nc.Block())
            per_eng = {e: [i for i, o in enumerate(ops) if o["eng"] == e] for e in ENGINES}

            def run(ename, eng):
                known = {}
                for i in per_eng[ename]:
                    o = ops[i]
                    waits = {}
                    for j in need[i]:
                        k = semkey[j]
                        if val[j] > waits.get(k, 0):
                            waits[k] = val[j]
                    for k, v in waits.items():
                        if known.get(k, 0) >= v:
                            continue
                        if k[0] == "e":
                            later = [kk for kk in known if kk[0] == "e" and kk[1] == k[1] and kk[2] > k[2]]
                            if later:
                                continue
                        known[k] = v
                        eng.wait_ge(sems[k], v)
                    ins = o["fn"](eng)
                    if o["dma"] is not None:
                        ins.then_inc(sems[semkey[i]], 16)
                    elif signal[i]:
                        ins.then_inc(sems[semkey[i]], 1)
                if ename == "sp":
                    for t in final_wait_tags:
                        eng.wait_ge(sems[("t", t, 0)], tag_count[t])

            @block.sync
            def _(e):
                run("sp", e)

            @block.scalar
            def _(e):
                run("act", e)

            @block.vector
            def _(e):
                run("dve", e)

            @block.gpsimd
            def _(e):
                run("pool", e)

            @block.tensor
            def _(e):
                run("pe", e)


def w3_slots():
    sl = []
    for s in range(8):
        sl.append((256 * s, 256))
    for s in range(2):
        sl.append((2048 + 256 * s, 256))
    for s in range(2):
        sl.append((2560 + 256 * s, 256))
    for s in range(4):
        sl.append((3072 + 256 * s, 256))
    sl.append((4096, 80))
    for s in range(10):
        sl.append((4176 + 256 * s, 256))
    for s in range(10):
        sl.append((6736 + 256 * s, 256))
    for s in range(16):
        sl.append((9296 + 256 * s, 256))
    return sl


S_Q, S_K, S_V, S_QI, S_KIWI, S_RX, S_RG, S_G = 0, 8, 10, 12, 16, 17, 27, 37


def gate_pairs():
    pairs = []
    for i in range(RC):
        for j in range(RC):
            hit = False
            for nb in range(16):
                lo, hi = nb * 160, (nb + 1) * 160
                if max(lo, j * 128) < min(hi, (j + 1) * 128) and max(lo, i * 128) < min(hi, (i + 1) * 128):
                    hit = True
            if hit:
                pairs.append((j, i))
    return pairs


ARENA_BYTES = 172032
OFF_ACTT = 0
OFF_Q = 32768
OFF_R = 65536
OFF_KT = 65536
OFF_V = 81920
OFF_QIT = 98304
OFF_KIT = 114688
OFF_T = 122880
NSLOT = 3
SLOT_ELEMS = 4096


XCOPY = {}


def build(upto="all"):
    nc = bass.Bass("TRN2", target_bir_lowering=False)
    P = Prog(nc)

    def din(name, shape, dt=F32):
        return nc.dram_tensor(name, shape, dt, kind="ExternalInput").ap()

    x2 = din("x2", [2048, 2048])
    posi = din("posi", [128, 16], I32)
    invf = din("invf", [128, 24])
    ident = din("ident", [128, 128])
    cbias = din("cbias", [1024, 2048])
    flag = din("flag", [128, 1])
    w1a = din("w1a", [FC, 128, 4096])
    w1b = din("w1b", [FC, 128, 2048])
    w2a = din("w2a", [FC, 128, 4096])
    w2b = din("w2b", [FC, 128, 2048])
    w3 = din("w3", [53, 128, 4096])
    wab = din("wab", [8, 128, 4096])
    wrb = din("wrb", [16, 128, 2560])
    wo = din("wo", [16, 128, 2048])
    pairs = gate_pairs()
    NP = len(pairs)
    wga = din("wga", [128, NP * 128])
    wgx = din("wgx", [128, NP * 128])
    lnp = din("lnp", [6, 2048])
    rnnp = din("rnnp", [128, 8 * RC])
    out = nc.dram_tensor("out", [1024, 2048], F32, kind="ExternalOutput").ap()
    ypre = nc.dram_tensor("ypre", [1024, 2048], F32).ap()
    h1d = nc.dram_tensor("h1d", [1024, 2048], F32).ap()
    h2d = nc.dram_tensor("h2d", [1024, 2048], F32).ap()
    kTc_d = nc.dram_tensor("kTc_d", [128, 4096], BF16).ap()
    Vc_d = nc.dram_tensor("Vc_d", [128, 4096], BF16).ap()
    kiTc_d = nc.dram_tensor("kiTc_d", [128, 1024], BF16).ap()
    dbg_out = None
    if upto != "all":
        dbg_out = nc.dram_tensor("dbg", [4096, 2048], F32, kind="ExternalOutput").ap()

    final_tags = []
    with contextlib.ExitStack() as st:
        def sb(name, shape, dt):
            return st.enter_context(nc.sbuf_tensor(name, shape, dt))

        arena = sb("arena", [128, ARENA_BYTES // 4], F32)
        ring = [sb("ring%d" % i, [128, SLOT_ELEMS], BF16) for i in range(NSLOT)]
        ps = st.enter_context(nc.psum_tensor("ps", [128, 4096], F32))
        identb = sb("identb", [128, 128], BF16)
        posf = sb("posf", [128, 16], F32)
        posi_s = sb("posi_s", [128, 16], I32)
        invf_s = sb("invf_s", [128, 24], F32)
        flag_s = sb("flag_s", [128, 1], F32)
        rnnp_s = sb("rnnp_s", [128, 8 * RC], F32)
        c1_s = sb("c1_s", [128, RC], F32)
        c2_s = sb("c2_s", [128, RC], F32)
        hlast = sb("hlast", [128, RC], F32)
        rxhalo = sb("rxhalo", [128, RC, 4], F32)
        stats = sb("stats", [128, NT, 4, 6], F32)
        mv = sb("mv", [128, NT, 2], F32)
        sd = sb("sd", [128, NT], F32)
        rstd = sb("rstd", [128, NT], F32)
        epst = sb("epst", [128, 1], F32)
        witm = sb("witm", [128, NT, 16], F32)
        m8 = sb("m8", [128, 8], F32)
        thr = sb("thr", [128, 1], F32)
        mx = sb("mx", [128, 2], F32)
        rs = sb("rs", [128, 2], F32)
        rinv = sb("rinv", [128, 2], F32)
        bar_t = sb("bar_t", [128, 4], F32)

        def carve(off, shape, dt):
            esz = 2 if dt == BF16 else 4
            nel = 1
            for s_ in shape[1:]:
                nel *= s_
            nb = nel * esz
            assert off % 4 == 0 and nb % 4 == 0 and off + nb <= ARENA_BYTES, (off, nb)
            v = arena[:, off // 4:(off + nb) // 4]
            if dt == BF16:
                v = v.bitcast(BF16)
            elif dt == I32:
                v = v.bitcast(I32)
            if len(shape) == 3:
                v = v.rearrange("p (a b) -> p a b", b=shape[2])
            return v

        def bank(b, n=512, off=0):
            return ps[:, b * 512 + off:b * 512 + off + n]

        def bankbf(b, nb=1):
            return ps[:, b * 512:(b + nb) * 512].bitcast(BF16)

        barrier_n = [0]

        def barrier():
            barrier_n[0] += 1
            P.op("dve", lambda e: e.memset(bar_t[:, 0:1], 0.0), reads=[], writes=["PHASE"], phase=False)

        ring_i = [0]

        def ring_load(src, nel):
            s = ring_i[0] % NSLOT
            ring_i[0] += 1
            k = 1
            while nel // k > 2048 or nel % k:
                k += 1
            dst = ring[s][:, 0:nel].rearrange("p (a b) -> p a b", a=k)
            srcv = src.rearrange("p (a b) -> p a b", a=k)
            P.dma("pool", lambda e: e.dma_start(out=dst, in_=srcv), writes=[("slot", s)],
                  tag=("slot", s), phase=False)
            return s

        P.dma("pool", lambda e: e.dma_start(out=identb[:], in_=ident), writes=["identb"], tag="c_ident")
        P.dma("sp", lambda e: e.dma_start(out=posi_s[:], in_=posi), writes=["posi_s"], tag="c_posi")
        P.dma("sp", lambda e: e.dma_start(out=invf_s[:], in_=invf), writes=["invf_s"], tag="c_invf")
        P.dma("sp", lambda e: e.dma_start(out=flag_s[:], in_=flag), writes=["flag_s"], tag="c_flag")
        P.dma("sp", lambda e: e.dma_start(out=rnnp_s[:], in_=rnnp), writes=["rnnp_s"], tag="c_rnnp")
        P.op("dve", lambda e: e.memset(epst[:], LN_EPS), writes=["epst"])
        P.op("dve", lambda e: e.memset(hlast[:], 0.0), writes=["hlast"])
        P.op("dve", lambda e: e.memset(rxhalo[:], 0.0), writes=["rxhalo"])
        P.op("dve", lambda e: e.tensor_copy(out=posf[:], in_=posi_s[:]), reads=["posi_s"], writes=["posf"])

        LAM = 7 * RC
        P.op("act", lambda e: e.activation(out=c1_s[:], in_=rnnp_s[:, LAM:LAM + RC], func=AF.Exp, scale=-1.0),
             reads=["rnnp_s"], writes=["c1_s"])
        P.op("dve", lambda e: e.tensor_scalar(out=c1_s[:], in0=c1_s[:], scalar1=1.0, scalar2=None, op0=ALU.add),
             reads=["c1_s"], writes=["c1_s"])
        P.op("act", lambda e: e.activation(out=c1_s[:], in_=c1_s[:], func=AF.Ln),
             reads=["c1_s"], writes=["c1_s"])
        P.op("dve", lambda e: e.tensor_scalar(out=c2_s[:], in0=c1_s[:], scalar1=-16.0, scalar2=None, op0=ALU.mult),
             reads=["c1_s"], writes=["c2_s"])
        P.op("dve", lambda e: e.tensor_scalar(out=c1_s[:], in0=c1_s[:], scalar1=-8.0, scalar2=None, op0=ALU.mult),
             reads=["c1_s"], writes=["c1_s"])

        identf = sb("identf", [128, 128], F32)
        onest = sb("onest", [128, 1], F32)
        P.op("dve", lambda e: e.memset(onest[:], 1.0), writes=["onest"])
        nmr = sb("nmr", [128, NT], F32)
        P.dma("sp", lambda e: e.dma_start(out=identf[:], in_=ident), writes=["identf"], tag="c_identf")
        cosq_r = sb("cosq_r", [128, 16, 32], F32)
        sinq_r = sb("sinq_r", [128, 16, 32], F32)
        cosi_r = sb("cosi_r", [128, 16, 32], F32)
        sini_r = sb("sini_r", [128, 16, 32], F32)
        trig_t = carve(OFF_T + 45056, [128, 16, 16], F32)
        trig_k = carve(OFF_T + 46080, [128, 16, 16], F32)
        trig_ki = carve(OFF_T + 47104, [128, 16, 16], I32)

        def trig_table(dst, ncol, col0, shift, nrep):
            tv = trig_t[:, :, 0:ncol]
            kv = trig_k[:, :, 0:ncol]
            kiv = trig_ki[:, :, 0:ncol]
            for gt in range(16):
                P.op("dve", lambda e, gt=gt: e.tensor_scalar(
                    out=trig_t[:, gt, 0:ncol], in0=invf_s[:, col0:col0 + ncol], scalar1=posf[:, gt:gt + 1],
                    scalar2=shift, op0=ALU.mult, op1=ALU.add),
                    reads=["posf", "invf_s", "trig_t"], writes=["trig_t"])
            two_pi = 2.0 * math.pi
            P.op("dve", lambda e: e.tensor_scalar(out=kv, in0=tv, scalar1=1.0 / two_pi, scalar2=None, op0=ALU.mult),
                 reads=["trig_t"], writes=["trig_k"])
            P.op("dve", lambda e: e.tensor_copy(out=kiv, in_=kv), reads=["trig_k"], writes=["trig_ki"])
            P.op("dve", lambda e: e.tensor_copy(out=kv, in_=kiv), reads=["trig_ki"], writes=["trig_k"])
            P.op("dve", lambda e: e.scalar_tensor_tensor(out=tv, in0=kv, scalar=-two_pi, in1=tv,
                                                         op0=ALU.mult, op1=ALU.add),
                 reads=["trig_k", "trig_t"], writes=["trig_t"])
            P.op("dve", lambda e: e.tensor_scalar(out=kv, in0=tv, scalar1=math.pi, scalar2=-two_pi,
                                                  op0=ALU.is_gt, op1=ALU.mult),
                 reads=["trig_t"], writes=["trig_k"])
            P.op("dve", lambda e: e.tensor_tensor(out=tv, in0=tv, in1=kv, op=ALU.add),
                 reads=["trig_t", "trig_k"], writes=["trig_t"])
            P.op("dve", lambda e: e.tensor_scalar(out=kv, in0=tv, scalar1=-math.pi, scalar2=two_pi,
                                                  op0=ALU.is_lt, op1=ALU.mult),
                 reads=["trig_t"], writes=["trig_k"])
            P.op("dve", lambda e: e.tensor_tensor(out=tv, in0=tv, in1=kv, op=ALU.add),
                 reads=["trig_t", "trig_k"], writes=["trig_t"])
            P.op("dve", lambda e: e.tensor_scalar(out=tv, in0=tv, scalar1=3.14159, scalar2=-3.14159,
                                                  op0=ALU.min, op1=ALU.max),
                 reads=["trig_t"], writes=["trig_t"])
            for r_ in range(nrep):
                P.op("act", lambda e, r_=r_: e.activation(out=dst[:, :, r_ * ncol:(r_ + 1) * ncol], in_=tv, func=AF.Sin),
                     reads=["trig_t"], writes=["trig_rep"])

        trig_table(sinq_r, 16, 0, 0.0, 2)
        trig_table(cosq_r, 16, 0, math.pi / 2, 2)
        trig_table(sini_r, 8, 16, 0.0, 4)
        trig_table(cosi_r, 8, 16, math.pi / 2, 4)
        barrier()


        actT = carve(OFF_ACTT, [128, DC, T], BF16)
        bufQ = carve(OFF_Q, [128, 16, T], BF16)
        yrT = carve(OFF_R, [128, RC, T], BF16)
        kT = carve(OFF_KT, [128, 4, 2048], BF16)
        Vt = carve(OFF_V, [128, 16, 512], BF16)
        qiT = carve(OFF_QIT, [128, 8, T], BF16)
        kiT = carve(OFF_KIT, [128, 2048], BF16)
        gT = carve(OFF_Q, [128, FC, T], BF16)

        dump_row = [0]

        def dump(src, nrows_p, ncols, reads, is_bf16=False, src_dram=False):
            if dbg_out is None:
                return
            r0 = dump_row[0]
            dump_row[0] += 128
            tag = "dump%d" % r0
            dst = dbg_out[r0:r0 + nrows_p, 0:ncols]
            if len(src.shape) == 3:
                dst = dst.rearrange("p (a b) -> p a b", b=src.shape[2])
            q = "pool" if is_bf16 else "sp"
            P.dma(q, lambda e: e.dma_start(out=dst, in_=src), reads=reads, writes=[tag], tag=tag)
            final_tags.append(tag)
            return r0

        tr_rr = [0]

        def transpose_cols(src_f32, ncol, dst_fn, src_res, dst_res, banks=(4, 5, 6, 7), eng_pref=None):
            nch = ncol // 128
            c = 0
            while c < nch:
                n = min(4, nch - c)
                b = banks[tr_rr[0] % len(banks)]
                tr_rr[0] += 1
                for i in range(n):
                    P.op("pe", lambda e, i=i, c=c, b=b: e.transpose(
                        bank(b, 128, i * 128), src_f32[:, (c + i) * 128:(c + i + 1) * 128], identf[:]),
                        reads=list(src_res) + ["identf"], writes=[("ps", b)])
                dst = dst_fn(c, n)
                srcv = bank(b, n * 128).rearrange("p (a b) -> p a b", b=128)
                eng = eng_pref or ("act" if tr_rr[0] % 2 else "dve")
                if eng == "act":
                    P.op("act", lambda e, dst=dst, srcv=srcv: e.activation(out=dst, in_=srcv, func=AF.Copy),
                         reads=[("ps", b)], writes=list(dst_res(c, n)))
                else:
                    P.op("dve", lambda e, dst=dst, srcv=srcv: e.tensor_copy(out=dst, in_=srcv),
                         reads=[("ps", b)], writes=list(dst_res(c, n)))
                c += n

        LB = {}

        def set_ln_bufs(off):
            o_ = off
            LB["yfull"] = carve(o_, [128, 2048], F32); o_ += 8192
            LB["gB"] = carve(o_, [128, 2048], F32); o_ += 8192
            LB["bB"] = carve(o_, [128, 2048], F32); o_ += 8192
            LB["rt"] = []
            LB["yt"] = []
            for k in range(2):
                LB["rt"].append(carve(o_, [128, 512], F32)); o_ += 2048
            for k in range(2):
                LB["yt"].append(carve(o_, [128, 512], F32)); o_ += 2048
            return o_

        cnt = dict(sa=0, rt=0, stg=0, pj=0, rb=0)

        def load_xT(row0, xf1):
            xf = [LB["yfull"], xf1]
            for t in range(NT):
                k = t % 2
                P.dma("sp", lambda e, t=t, k=k: e.dma_start(out=xf[k], in_=x2[row0 + t * 128:row0 + (t + 1) * 128, :]),
                      writes=[("xf", k)] + ([("yfull",)] if k == 0 else []), tag=("xf", k))
                transpose_cols(xf[k], 2048, lambda c, n, t=t: actT[:, c:c + n, t * 128:(t + 1) * 128],
                               [("xf", k)] + ([("yfull",)] if k == 0 else []), lambda c, n, t=t: [("actT", cc, t) for cc in range(c, c + n)])

        def tok_out_stage(lhs_fn, lhs_res, nk, wsrc, resid, resid_res):
            nl = nk // 4
            for cb in range(4):
                s4 = 0
                while s4 < nl:
                    na = min(2, nl - s4)
                    s = ring_i[0] % NSLOT
                    ring_i[0] += 1
                    dstv = ring[s][:, 0:na * 2048].rearrange("p (a b) -> p a b", a=na)
                    srcv = wsrc[cb * nl + s4:cb * nl + s4 + na].rearrange("a p n -> p a n")
                    P.dma("pool", lambda e, dstv=dstv, srcv=srcv: e.dma_start(out=dstv, in_=srcv),
                          writes=[("slot", s)], tag=("slot", s), phase=False)
                    for t in range(NT):
                        for a_ in range(na):
                            for j in range(4):
                                kc = (s4 + a_) * 4 + j
                                P.op("pe", lambda e, t=t, kc=kc, j=j, s=s, a_=a_: e.matmul(
                                    bank(t), lhsT=lhs_fn(kc, t),
                                    rhs=ring[s][:, a_ * 2048 + j * 512:a_ * 2048 + (j + 1) * 512],
                                    start=(kc == 0), stop=(kc == nk - 1)),
                                    reads=[("slot", s)] + lhs_res(kc, t), writes=[("ps", t)])
                    s4 += na
                for t in range(NT):
                    k = cnt["rt"] % 2
                    cnt["rt"] += 1
                    rt, yt = LB["rt"], LB["yt"]
                    P.dma("sp", lambda e, k=k, t=t, cb=cb, rt=rt: e.dma_start(
                        out=rt[k], in_=resid[t * 128:(t + 1) * 128, cb * 512:(cb + 1) * 512]),
                        reads=[(resid_res, t)], writes=[("rt", k)], tag=("rt", k))
                    P.op("dve", lambda e, k=k, t=t, rt=rt, yt=yt: e.scalar_tensor_tensor(
                        out=yt[k], in0=rt[k], scalar=ALPHA, in1=bank(t), op0=ALU.mult, op1=ALU.add),
                        reads=[("rt", k), ("ps", t)], writes=[("yt", k)])
                    P.op("dve", lambda e, k=k, t=t, cb=cb, yt=yt: e.bn_stats(out=stats[:, t, cb, :], in_=yt[k]),
                         reads=[("yt", k)], writes=[("stats", t)])
                    P.dma("sp", lambda e, k=k, t=t, cb=cb, yt=yt: e.dma_start(
                        out=ypre[t * 128:(t + 1) * 128, cb * 512:(cb + 1) * 512], in_=yt[k]),
                        reads=[("yt", k)], writes=[("ypre", t)], tag=("yts", k))

        def layer_norm_pass(ln_idx, dst_dram, dst_res, want_T):
            yfs = [LB["yfull"], LB["yfull2"]]
            yks = [("yfull",), LB["yfull2_key"]]
            gB, bB = LB["gB"], LB["bB"]
            P.dma("sp", lambda e: e.dma_start(out=gB, in_=lnp[2 * ln_idx].partition_broadcast(128)),
                  writes=["gB"], tag="gB")
            P.dma("sp", lambda e: e.dma_start(out=bB, in_=lnp[2 * ln_idx + 1].partition_broadcast(128)),
                  writes=["bB"], tag="bB")
            for t in range(NT):
                yf, yk = yfs[t % 2], yks[t % 2]
                P.op("dve", lambda e, t=t: e.bn_aggr(out=mv[:, t, :], in_=stats[:, t, :, :].rearrange("p a b -> p (a b)")),
                     reads=[("stats", t)], writes=[("mv", t)])
                P.op("act", lambda e, t=t: e.activation(out=sd[:, t:t + 1], in_=mv[:, t, 1:2], func=AF.Sqrt,
                                                        bias=epst[:, 0:1], scale=1.0),
                     reads=[("mv", t), "epst"], writes=[("sd", t)])
                P.op("dve", lambda e, t=t: e.reciprocal(out=rstd[:, t:t + 1], in_=sd[:, t:t + 1]),
                     reads=[("sd", t)], writes=[("rstd", t)])
                P.op("dve", lambda e, t=t: e.tensor_scalar(out=nmr[:, t:t + 1], in0=mv[:, t, 0:1], scalar1=rstd[:, t:t + 1],
                                                           scalar2=-1.0, op0=ALU.mult, op1=ALU.mult),
                     reads=[("mv", t), ("rstd", t)], writes=[("nmr", t)])
                P.dma("sp", lambda e, t=t, yf=yf: e.dma_start(out=yf, in_=ypre[t * 128:(t + 1) * 128, :]),
                      reads=[("ypre", t)], writes=[yk], tag=("yfl", t % 2))
                P.op("act", lambda e, t=t, yf=yf: e.activation(out=yf, in_=yf, func=AF.Identity,
                                                               scale=rstd[:, t:t + 1], bias=nmr[:, t:t + 1]),
                     reads=[yk, ("nmr", t), ("rstd", t)], writes=[yk])
                P.op("dve", lambda e, yf=yf: e.tensor_tensor(out=yf, in0=yf, in1=gB, op=ALU.mult),
                     reads=[yk, "gB"], writes=[yk])
                P.op("dve", lambda e, yf=yf: e.tensor_tensor(out=yf, in0=yf, in1=bB, op=ALU.add),
                     reads=[yk, "bB"], writes=[yk])
                if dst_dram is not None:
                    P.dma("sp", lambda e, t=t, yf=yf: e.dma_start(out=dst_dram[t * 128:(t + 1) * 128, :], in_=yf),
                          reads=[yk], writes=[(dst_res, t)], tag=("hst_" + dst_res, t % 2))
                if want_T:
                    transpose_cols(yf, 2048, lambda c, n, t=t: actT[:, c:c + n, t * 128:(t + 1) * 128],
                                   [yk], lambda c, n, t=t: [("actT", cc, t) for cc in range(c, c + n)])

        def ffn(resid, resid_res, wa, wb, ln_idx, dst_dram, dst_res, want_T, sa):
            for fc in range(FC):
                s = ring_load(wa[fc], 4096)
                base = (fc % 2) * 4
                for ab in range(2):
                    for n_ in range(2):
                        b = base + ab * 2 + n_
                        for dc in range(DC):
                            P.op("pe", lambda e, b=b, ab=ab, dc=dc, n_=n_, s=s: e.matmul(
                                bank(b), lhsT=ring[s][:, (ab * 16 + dc) * 128:(ab * 16 + dc + 1) * 128],
                                rhs=actT[:, dc, n_ * 512:(n_ + 1) * 512], start=(dc == 0), stop=(dc == DC - 1)),
                                reads=[("slot", s)] + [("actT", dc, tt) for tt in range(n_ * 4, n_ * 4 + 4)],
                                writes=[("ps", b)])
                for n_ in range(2):
                    k = cnt["sa"] % 2
                    cnt["sa"] += 1
                    ba, bb = base + n_, base + 2 + n_
                    P.op("act", lambda e, k=k, ba=ba: e.activation(out=sa[k], in_=bank(ba), func=AF.Silu),
                         reads=[("ps", ba)], writes=[("sa", k)])
                    P.op("dve", lambda e, k=k, bb=bb, fc=fc, n_=n_: e.scalar_tensor_tensor(
                        out=gT[:, fc, n_ * 512:(n_ + 1) * 512], in0=sa[k], scalar=0.5, in1=bank(bb),
                        op0=ALU.mult, op1=ALU.mult),
                        reads=[("sa", k), ("ps", bb)], writes=[("gT", fc, n_)])
            tok_out_stage(lambda kc, t: gT[:, kc, t * 128:(t + 1) * 128],
                          lambda kc, t: [("gT", kc, t // 4)], FC, wb, resid, resid_res)
            layer_norm_pass(ln_idx, dst_dram, dst_res, want_T)

        def ffn_phase_bufs():
            o_ = set_ln_bufs(OFF_T)
            sa = []
            for k in range(2):
                sa.append(carve(o_, [128, 512], F32)); o_ += 2048
            xf1 = carve(o_, [128, 2048], F32); o_ += 8192
            LB["yfull2"] = xf1
            LB["yfull2_key"] = ("xf", 1)
            return sa, xf1

        pend = [None]

        def proj_tok(slot_idx, ncols, handler, PB):
            s = ring_load(w3[slot_idx], 4096)
            for t in range(NT):
                b = cnt["pj"] % 4
                cnt["pj"] += 1
                for dc in range(DC):
                    P.op("pe", lambda e, b=b, dc=dc, t=t, s=s: e.matmul(
                        bank(b, ncols), lhsT=actT[:, dc, t * 128:(t + 1) * 128],
                        rhs=ring[s][:, dc * 256:dc * 256 + ncols], start=(dc == 0), stop=(dc == DC - 1)),
                        reads=[("slot", s), ("actT", dc, t)], writes=[("ps", b)])
                if pend[0] is not None:
                    pend[0]()
                pend[0] = handler(t, b)
            if pend[0] is not None:
                pend[0]()
                pend[0] = None

        def rope_to_stg(b, stg, k, nh, hd, half, cos_r, sin_r, gt, tmps, ncopy=None):
            ncol = nh * hd
            ncp = ncopy or ncol
            P.op("act", lambda e: e.activation(out=stg[k][:, 0:ncp], in_=bank(b, ncp), func=AF.Copy),
                 reads=[("ps", b)], writes=[("stg", k)])
            if "rope" in os.environ.get("K_SKIP", ""):
                return
            s3 = stg[k][:, 0:ncol].rearrange("p (h d) -> p h d", h=nh)
            t1 = s3[:, :, 0:half]
            t2 = s3[:, :, half:2 * half]
            w = nh * half
            cosb = cos_r[:, gt, 0:w].rearrange("p (h d) -> p h d", h=nh)
            sinb = sin_r[:, gt, 0:w].rearrange("p (h d) -> p h d", h=nh)
            m = [tmps[i][:, 0:w].rearrange("p (h d) -> p h d", h=nh) for i in range(4)]
            for i, (a_, b_) in enumerate(((t1, cosb), (t2, sinb), (t2, cosb), (t1, sinb))):
                P.op("dve", lambda e, i=i, a_=a_, b_=b_: e.tensor_tensor(out=m[i], in0=a_, in1=b_, op=ALU.mult),
                     reads=[("stg", k), "trig_rep"], writes=[("ropetmp", i)])
            P.op("dve", lambda e: e.tensor_tensor(out=t1, in0=m[0], in1=m[1], op=ALU.subtract),
                 reads=[("ropetmp", 0), ("ropetmp", 1)], writes=[("stg", k)])
            P.op("dve", lambda e: e.tensor_tensor(out=t2, in0=m[2], in1=m[3], op=ALU.add),
                 reads=[("ropetmp", 2), ("ropetmp", 3)], writes=[("stg", k)])

        def proj_group(g, PB, kT_dst, V_dst, kiT_dst, key0):
            stg, tmps = PB["stg"], PB["tmps"]

            def nxt():
                k = cnt["stg"] % 2
                cnt["stg"] += 1
                return k

            if g == 1:
                for sq in range(8):
                    def hq(t, b, sq=sq):
                        k = nxt()
                        rope_to_stg(b, stg, k, 2, 128, 16, cosq_r, sinq_r, g * 8 + t, tmps)
                        return lambda: transpose_cols(stg[k], 256,
                                       lambda c, n, t=t: bufQ[:, 2 * sq + c:2 * sq + c + n, t * 128:(t + 1) * 128],
                                       [("stg", k)], lambda c, n, t=t: [("bufQ", 2 * sq + cc, t) for cc in range(c, c + n)])
                    proj_tok(S_Q + sq, 256, hq, PB)
            SK_ = os.environ.get("K_SKIP", "").split(",")
            for sk in range(0 if "nok" in SK_ else 2):
                def hk(t, b, sk=sk):
                    k = nxt()
                    rope_to_stg(b, stg, k, 2, 128, 16, cosq_r, sinq_r, g * 8 + t, tmps)
                    return lambda: transpose_cols(stg[k], 256,
                                   lambda c, n, t=t: kT_dst[:, 2 * sk + c:2 * sk + c + n, key0 + t * 128:key0 + (t + 1) * 128],
                                   [("stg", k)], lambda c, n, t=t: [("kT", g, 2 * sk + cc, t) for cc in range(c, c + n)])
                proj_tok(S_K + sk, 256, hk, PB)
            for sv_ in range(0 if "nov" in SK_ else 2):
                def hv(t, b, sv_=sv_):
                    dstv = V_dst[:, key0 // 128 + t, sv_ * 256:(sv_ + 1) * 256]
                    P.op("act", lambda e: e.activation(out=dstv, in_=bank(b, 256), func=AF.Copy),
                         reads=[("ps", b)], writes=[("V", g, t, sv_)])
                proj_tok(S_V + sv_, 256, hv, PB)
            if g == 1:
                for sq in range(4):
                    def hqi(t, b, sq=sq):
                        k = nxt()
                        rope_to_stg(b, stg, k, 4, 64, 8, cosi_r, sini_r, g * 8 + t, tmps)
                        return lambda: transpose_cols(stg[k], 256,
                                       lambda c, n, t=t: qiT[:, 2 * sq + c:2 * sq + c + n, t * 128:(t + 1) * 128],
                                       [("stg", k)], lambda c, n, t=t: [("qiT", 2 * sq + cc, t) for cc in range(c, c + n)])
                    proj_tok(S_QI + sq, 256, hqi, PB)

            def hki(t, b):
                k = nxt()
                rope_to_stg(b, stg, k, 1, 64, 8, cosi_r, sini_r, g * 8 + t, tmps, ncopy=128)
                if g == 1:
                    P.op("dve", lambda e: e.tensor_scalar(out=witm[:, t, :], in0=stg[k][:, 64:80], scalar1=WI_SCALE,
                                                          scalar2=None, op0=ALU.mult),
                         reads=[("stg", k)], writes=[("witm", t)])
                P.op("dve", lambda e: e.tensor_copy(out=stg[k][:, 64:128], in_=stg[k][:, 0:64]),
                     reads=[("stg", k)], writes=[("stg", k)])
                return lambda: transpose_cols(stg[k], 128,
                               lambda c, n, t=t: kiT_dst[:, key0 + t * 128:key0 + (t + 1) * 128].rearrange("p (a b) -> p a b", a=1),
                               [("stg", k)], lambda c, n, t=t: [("kiT", g, t)])
            if "noki" not in SK_:
                proj_tok(S_KIWI, 128, hki, PB)

        def rnn_group(g, RB_):
            gw = RB_["gw"]
            rxh, xc, xcb = RB_["rxh"], RB_["xc"], RB_["xcb"]
            tl = RB_["tl"]
            CW = lambda j, c: rnnp_s[:, j * RC + c:j * RC + c + 1]
            for n_ in range(2):
                tok = slice(n_ * 512, (n_ + 1) * 512)
                slot_rx = {}
                slot_rg = {}

                def conv(c, n_=n_, tok=tok):
                    if c % 2 == 0:
                        slot_rx[c // 2] = ring_load(w3[S_RX + c // 2], 4096)
                    s = slot_rx[c // 2]
                    b = cnt["rb"] % 4
                    cnt["rb"] += 1
                    for dc in range(DC):
                        P.op("pe", lambda e, dc=dc, b=b, s=s: e.matmul(
                            bank(b), lhsT=ring[s][:, dc * 256 + (c % 2) * 128:dc * 256 + (c % 2) * 128 + 128],
                            rhs=actT[:, dc, tok], start=(dc == 0), stop=(dc == DC - 1)),
                            reads=[("slot", s)] + [("actT", dc, tt) for tt in range(n_ * 4, n_ * 4 + 4)],
                            writes=[("ps", b)])
                    P.op("act", lambda e, b=b: e.activation(out=rxh[:, 4:516], in_=bank(b), func=AF.Copy),
                         reads=[("ps", b)], writes=["rxh"])
                    P.op("dve", lambda e: e.tensor_copy(out=rxh[:, 0:4], in_=rxhalo[:, c, :]),
                         reads=[("rxhalo", c)], writes=["rxh"])
                    xcc = xc[c % 4]
                    P.op("dve", lambda e: e.tensor_scalar(out=xcc, in0=rxh[:, 4:516], scalar1=CW(3, c), scalar2=CW(4, c),
                                                          op0=ALU.mult, op1=ALU.add),
                         reads=["rxh", "rnnp_s"], writes=[("xc", c % 4)])
                    for j in range(3):
                        P.op("dve", lambda e, j=j: e.scalar_tensor_tensor(out=xcc, in0=rxh[:, 1 + j:1 + j + 512], scalar=CW(j, c),
                                                                          in1=xcc, op0=ALU.mult, op1=ALU.add),
                             reads=["rxh", "rnnp_s", ("xc", c % 4)], writes=[("xc", c % 4)])
                    P.op("dve", lambda e: e.tensor_copy(out=rxhalo[:, c, :], in_=rxh[:, 512:516]),
                         reads=["rxh"], writes=[("rxhalo", c)])
                    P.op("act", lambda e: e.activation(out=xcb[c % 4], in_=xcc, func=AF.Copy),
                         reads=[("xc", c % 4)], writes=[("xcb", c % 4)])

                def gates(c, n_=n_, tok=tok):
                    r_, i_, a_, a2_, gx_, h_, u_, sg_, gl_ = tl[0:9]
                    sfx = ""
                    if c % 2:
                        a_, a2_ = tl[9], tl[10]
                        if RB_.get("tl2") is not None:
                            r_, i_, gx_, h_ = RB_["tl2"]
                            sfx = "b"
                    ka, ka2 = ("a_", c % 2), ("a2_", c % 2)
                    kr, ki_, kgx, kh = "r_" + sfx, "i_" + sfx, "gx_" + sfx, "h_" + sfx
                    bA = cnt["rb"] % 4
                    bX = (cnt["rb"] + 1) % 4
                    cnt["rb"] += 2
                    for gate, bb_ in ((0, bA), (1, bX)):
                        ks = [k for k, (j, i) in enumerate(pairs) if i == c]
                        for q_, k in enumerate(ks):
                            j = pairs[k][0]
                            P.op("pe", lambda e, k=k, j=j, gate=gate, bb_=bb_, q_=q_, ks=ks: e.matmul(
                                bank(bb_), lhsT=gw[gate][:, k, :], rhs=xcb[j % 4],
                                start=(q_ == 0), stop=(q_ == len(ks) - 1)),
                                reads=["gw", ("xcb", j % 4)], writes=[("ps", bb_)])
                    P.op("act", lambda e: e.activation(out=r_, in_=bank(bA), func=AF.Sigmoid, bias=CW(5, c), scale=1.0),
                         reads=[("ps", bA), "rnnp_s"], writes=[kr])
                    P.op("act", lambda e: e.activation(out=i_, in_=bank(bX), func=AF.Sigmoid, bias=CW(6, c), scale=1.0),
                         reads=[("ps", bX), "rnnp_s"], writes=[ki_])
                    P.op("act", lambda e: e.activation(out=a_, in_=r_, func=AF.Exp, scale=c1_s[:, c:c + 1]),
                         reads=[kr, "c1_s"], writes=[ka])
                    P.op("act", lambda e: e.activation(out=a2_, in_=r_, func=AF.Exp, scale=c2_s[:, c:c + 1]),
                         reads=[kr, "c2_s"], writes=[ka2])
                    P.op("act", lambda e: e.activation(out=a2_, in_=a2_, func=AF.Sqrt, scale=-1.0, bias=onest[:, 0:1]),
                         reads=[ka2, "onest"], writes=[ka2])
                    P.op("dve", lambda e: e.tensor_tensor(out=gx_, in0=i_, in1=xc[c % 4], op=ALU.mult),
                         reads=[ki_, ("xc", c % 4)], writes=[kgx])
                    P.op("dve", lambda e: e.tensor_tensor(out=gx_, in0=gx_, in1=a2_, op=ALU.mult),
                         reads=[kgx, ka2], writes=[kgx])
                    P.op("dve", lambda e: e.tensor_tensor_scan(out=h_, data0=a_, data1=gx_, initial=hlast[:, c:c + 1],
                                                               op0=ALU.mult, op1=ALU.add),
                         reads=[ka, kgx, ("hlast", c)], writes=[kh])
                    P.op("dve", lambda e: e.tensor_copy(out=hlast[:, c:c + 1], in_=h_[:, 511:512]),
                         reads=[kh], writes=[("hlast", c)])
                    if upto == "pc" and g == 0 and n_ == 0 and c in (0, 1, 5) and "dumps" not in os.environ.get("K_SKIP", ""):
                        dump(xc[c % 4], 128, 512, [("xc", c % 4)])
                        dump(r_, 128, 512, [kr])
                        dump(i_, 128, 512, [ki_])
                        dump(a_, 128, 512, ["a_"])
                        dump(gx_, 128, 512, [kgx])
                        dump(h_, 128, 512, [kh])
                    if g == 1:
                        if c % 2 == 0:
                            slot_rg[c // 2] = ring_load(w3[S_RG + c // 2], 4096)
                        s = slot_rg[c // 2]
                        bG = cnt["rb"] % 4
                        cnt["rb"] += 1
                        for dc in range(DC):
                            P.op("pe", lambda e, dc=dc, s=s: e.matmul(
                                bank(bG), lhsT=ring[s][:, dc * 256 + (c % 2) * 128:dc * 256 + (c % 2) * 128 + 128],
                                rhs=actT[:, dc, tok], start=(dc == 0), stop=(dc == DC - 1)),
                                reads=[("slot", s)] + [("actT", dc, tt) for tt in range(n_ * 4, n_ * 4 + 4)],
                                writes=[("ps", bG)])
                        P.op("act", lambda e: e.activation(out=u_, in_=bank(bG), func=AF.Square),
                             reads=[("ps", bG)], writes=["u_"])
                        P.op("dve", lambda e: e.tensor_scalar(out=u_, in0=u_, scalar1=0.044715, scalar2=1.0,
                                                              op0=ALU.mult, op1=ALU.add),
                             reads=["u_"], writes=["u_"])
                        P.op("dve", lambda e: e.tensor_tensor(out=u_, in0=u_, in1=bank(bG), op=ALU.mult),
                             reads=["u_", ("ps", bG)], writes=["u_"])
                        P.op("act", lambda e: e.activation(out=sg_, in_=u_, func=AF.Sigmoid, scale=1.5957691216),
                             reads=["u_"], writes=["sg_"])
                        P.op("dve", lambda e: e.tensor_tensor(out=gl_, in0=sg_, in1=bank(bG), op=ALU.mult),
                             reads=["sg_", ("ps", bG)], writes=["gl_"])
                        P.op("dve", lambda e: e.tensor_tensor(out=yrT[:, c, tok], in0=gl_, in1=h_, op=ALU.mult),
                             reads=["gl_", kh], writes=[("yrT", c, n_)])

                for c in range(RC + 1):
                    if c < RC:
                        conv(c)
                    if c >= 1 and "nogates" not in os.environ.get("K_SKIP", "").split(","):
                        gates(c - 1)

        def rnn_bufs(off, extra_off=None):
            o_ = off
            RB_ = {}
            g0 = carve(o_, [128, NP, 128], BF16); o_ += NP * 256
            g1 = carve(o_, [128, NP, 128], BF16); o_ += NP * 256
            RB_["gw"] = (g0, g1)
            RB_["rxh"] = carve(o_, [128, 516], F32); o_ += 2064
            RB_["xc"] = []
            for k in range(4):
                RB_["xc"].append(carve(o_, [128, 512], F32)); o_ += 2048
            RB_["xcb"] = []
            for k in range(4):
                RB_["xcb"].append(carve(o_, [128, 512], BF16)); o_ += 1024
            RB_["tl"] = []
            for k in range(11):
                RB_["tl"].append(carve(o_, [128, 512], F32)); o_ += 2048
            RB_["tl2"] = None
            if extra_off is not None:
                RB_["tl2"] = [carve(extra_off + 2048 * k_, [128, 512], F32) for k_ in range(4)]
            for gi, (gwt, src) in enumerate(((g0, wga), (g1, wgx))):
                for q_ in range(4):
                    n0 = q_ * 13
                    P.dma("pool", lambda e, gwt=gwt, src=src, n0=n0: e.dma_start(
                        out=gwt.rearrange("p a b -> p (a b)")[:, n0 * 128:(n0 + 13) * 128], in_=src[:, n0 * 128:(n0 + 13) * 128]),
                        writes=["gw"], tag=("gw", gi, q_))
            return RB_

        RNN_OFF = 106496
        sa, xf1 = ffn_phase_bufs()
        load_xT(0, xf1)
        ffn(x2[0:1024, :], "x2c", w1a, w1b, 0, dbg_out[3072:4096, :] if upto in ("ffnc", "pc") else None, "dbgh", True, sa)
        if upto in ("ffnc", "pc"):
            final_tags += [("hst_dbgh", 0), ("hst_dbgh", 1)]
        barrier()
        kTc = carve(OFF_Q, [128, 4, 1024], BF16)
        Vc = carve(OFF_Q + 8192, [128, 8, 512], BF16)
        kiTc = carve(OFF_Q + 16384, [128, 1024], BF16)
        PB = dict(stg=[carve(OFF_Q + 20480, [128, 256], F32), carve(OFF_Q + 21504, [128, 256], F32)],
                  tmps=[carve(OFF_Q + 22528 + 128 * i_, [128, 32], F32) for i_ in range(4)])
        import os
        SKIP = os.environ.get("K_SKIP", "").split(",")
        if "proj" not in SKIP:
            proj_group(0, PB, kTc, Vc, kiTc, 0)
        P.dma("sp", lambda e: e.dma_start(out=kTc_d, in_=kTc.rearrange("p a b -> p (a b)")),
              reads=[("kT", 0, h_, t_) for h_ in range(4) for t_ in range(8)], writes=["kTc_d"], tag="kTc_d")
        P.dma("sp", lambda e: e.dma_start(out=Vc_d, in_=Vc.rearrange("p a b -> p (a b)")),
              reads=[("V", 0, t_, s_) for t_ in range(8) for s_ in range(2)], writes=["Vc_d"], tag="Vc_d")
        P.dma("sp", lambda e: e.dma_start(out=kiTc_d, in_=kiTc),
              reads=[("kiT", 0, t_) for t_ in range(8)], writes=["kiTc_d"], tag="kiTc_d")
        if "rnn" not in SKIP:
            RB_ = rnn_bufs(RNN_OFF, extra_off=OFF_Q + 24576)
            rnn_group(0, RB_)
        P.op("dve", lambda e: e.tensor_scalar(out=hlast[:], in0=hlast[:], scalar1=flag_s[:, 0:1], scalar2=None, op0=ALU.mult),
             reads=[("hlast", c_) for c_ in range(RC)] + ["flag_s"], writes=[("hlast", c_) for c_ in range(RC)])
        P.op("dve", lambda e: e.tensor_scalar(out=rxhalo[:].rearrange("p a b -> p (a b)"),
                                              in0=rxhalo[:].rearrange("p a b -> p (a b)"),
                                              scalar1=flag_s[:, 0:1], scalar2=None, op0=ALU.mult),
             reads=[("rxhalo", c_) for c_ in range(RC)] + ["flag_s"], writes=[("rxhalo", c_) for c_ in range(RC)])
        if upto == "pc" and "dumps" not in SKIP:
            dump(kTc.rearrange("p a b -> p (a b)")[:, 0:2048], 128, 2048, [("kT", 0, h_, t_) for h_ in range(4) for t_ in range(8)], is_bf16=True)
            dump(Vc.rearrange("p a b -> p (a b)")[:, 0:2048], 128, 2048, [("V", 0, t_, s_) for t_ in range(8) for s_ in range(2)], is_bf16=True)
            dump(kiTc[:, 0:1024], 128, 1024, [("kiT", 0, t_) for t_ in range(8)], is_bf16=True)
            dump(hlast[:], 128, RC, [("hlast", c_) for c_ in range(RC)])
            dump(rxhalo[:].rearrange("p a b -> p (a b)"), 128, 4 * RC, [("rxhalo", c_) for c_ in range(RC)])
        barrier()
        if upto in ("ffnc", "pc"):
            P.dma("sp", lambda e: e.dma_start(out=out[0:128, :], in_=LB["yfull"]), reads=[("yfull",)], tag="outd")
            final_tags.append("outd")
            P.emit(final_wait_tags=final_tags)
            return nc, P
        sa, xf1 = ffn_phase_bufs()
        load_xT(1024, xf1)
        ffn(x2[1024:2048, :], "x2o", w1a, w1b, 0, h1d, "h1d", True, sa)
        barrier()
        o_ = OFF_T + 45056
        PB = dict(stg=[carve(o_, [128, 256], F32), carve(o_ + 1024, [128, 256], F32)],
                  tmps=[carve(o_ + 2048 + 128 * i_, [128, 32], F32) for i_ in range(4)])
        P.dma("sp", lambda e: e.dma_start(out=kT[:, :, 0:1024], in_=kTc_d.rearrange("p (a b) -> p a b", a=4)),
              reads=["kTc_d"], writes=[("kT", 0, h_, t_) for h_ in range(4) for t_ in range(8)], tag="kTr")
        P.dma("sp", lambda e: e.dma_start(out=Vt[:, 0:8, :], in_=Vc_d.rearrange("p (a b) -> p a b", a=8)),
              reads=["Vc_d"], writes=[("V", 0, t_, s_) for t_ in range(8) for s_ in range(2)], tag="Vr")
        P.dma("sp", lambda e: e.dma_start(out=kiT[:, 0:1024], in_=kiTc_d),
              reads=["kiTc_d"], writes=[("kiT", 0, t_) for t_ in range(8)], tag="kiTr")
        proj_group(1, PB, kT, Vt, kiT, 1024)
        if upto == "po":
            dump(bufQ[:, 0, :], 128, 1024, [("bufQ", 0, t_) for t_ in range(8)], is_bf16=True)
            dump(bufQ[:, 5, :], 128, 1024, [("bufQ", 5, t_) for t_ in range(8)], is_bf16=True)
            dump(kT[:, 1, :], 128, 2048, [("kT", g_, 1, t_) for g_ in range(2) for t_ in range(8)], is_bf16=True)
            dump(qiT[:, 3, :], 128, 1024, [("qiT", 3, t_) for t_ in range(8)], is_bf16=True)
            dump(kiT[:, :], 128, 2048, [("kiT", g_, t_) for g_ in range(2) for t_ in range(8)], is_bf16=True)
            dump(witm[:].rearrange("p a b -> p (a b)"), 128, 128, [("witm", t_) for t_ in range(8)])
            dump(Vt[:, 9, :], 128, 512, [("V", 1, 1, s_) for s_ in range(2)], is_bf16=True)
        barrier()
        h1T_d = nc.dram_tensor("h1T_d", [128, DC * T], BF16).ap()
        P.dma("sp", lambda e: e.dma_start(out=h1T_d, in_=actT.rearrange("p a b -> p (a b)")),
              reads=[("actT", dc_, t_) for dc_ in range(DC) for t_ in range(NT)], writes=["h1T_d"], tag="h1T_st")
        barrier()
        NQB = 8 if upto in ("all", "att", "rn", "mg") else 2
        if upto == "po":
            NQB = 0
        NKs = [1152 + 128 * j for j in range(8)]
        mb = []
        o_ = OFF_ACTT
        for j in range(8):
            mb.append(carve(o_, [128, NKs[j]], BF16)); o_ += NKs[j] * 2
        yh = [carve(o_, [128, 128], F32), carve(o_ + 512, [128, 128], F32)]
        o_ = OFF_T
        accs = []
        for k in range(4):
            accs.append(carve(o_, [128, 2048], F32)); o_ += 8192
        rls = [carve(o_, [128, 2048], F32), carve(o_ + 8192, [128, 2048], F32)]
        S = [ring[0][:, :].bitcast(F32), ring[1][:, :].bitcast(F32)]
        PTs2 = [ring[2][:, 0:2048].rearrange("p (a b) -> p a b", b=128),
                ring[2][:, 2048:4096].rearrange("p (a b) -> p a b", b=128)]
        m8s = [m8, sb("m8b", [128, 8], F32)]
        thrs = [thr, sb("thrb", [128, 1], F32)]
        hcnt = [0]

        def indexer_head(j, h):
            NK = NKs[j]
            nkc = 9 + j
            ai = j % 4
            acc = accs[ai]
            kres = [("kiT", kc // 8, kc % 8) for kc in range(nkc)]
            if h == 0:
                P.dma("sp", lambda e: e.dma_start(out=acc[:, 0:NK], in_=cbias[j * 128:(j + 1) * 128, 0:NK]),
                      writes=[("acc", ai)], tag=("cb", ai))
            pair, hf = h // 2, h % 2
            hp = hcnt[0] % 2
            hcnt[0] += 1
            rl = rls[hp]
            for half in range(2):
                k0 = half * 1024
                k1 = min(NK, k0 + 1024)
                if k1 <= k0:
                    continue
                for kb in range(2):
                    c0 = k0 + kb * 512
                    n = min(512, k1 - c0)
                    if n <= 0:
                        continue
                    P.op("pe", lambda e, kb=kb, n=n, c0=c0: e.matmul(
                        ps[:, kb * 512:kb * 512 + n],
                        lhsT=qiT[hf * 64:(hf + 1) * 64, pair, j * 128:(j + 1) * 128],
                        rhs=kiT[hf * 64:(hf + 1) * 64, c0:c0 + n], start=True, stop=True),
                        reads=[("qiT", pair, j)] + kres, writes=[("ps", kb)])
                P.op("act", lambda e, k0=k0, k1=k1: e.activation(
                    out=rl[:, k0:k1], in_=ps[:, 0:k1 - k0], func=AF.Relu),
                    reads=[("ps", 0), ("ps", 1)], writes=[("rl", hp)])
            if j < 2:
                P.op("dve", lambda e: e.scalar_tensor_tensor(
                    out=acc[:, 0:NK], in0=rl[:, 0:NK], scalar=witm[:, j, h:h + 1], in1=acc[:, 0:NK],
                    op0=ALU.mult, op1=ALU.add),
                    reads=[("rl", hp), ("witm", j), ("acc", ai)], writes=[("acc", ai)])
                return
            P.op("pool", lambda e: e.tensor_scalar(
                out=rl[:, 0:NK], in0=rl[:, 0:NK], scalar1=witm[:, j, h:h + 1], scalar2=0.0,
                op0=ALU.mult, op1=ALU.add),
                reads=[("rl", hp), ("witm", j)], writes=[("rl", hp)])
            P.op("pool", lambda e: e.tensor_tensor(out=acc[:, 0:NK], in0=acc[:, 0:NK], in1=rl[:, 0:NK], op=ALU.add),
                 reads=[("rl", hp), ("acc", ai)], writes=[("acc", ai)])

        def qk(j, h):
            NK = NKs[j]
            nkc = 9 + j
            nb = (NK + 511) // 512
            kvh = h // 4
            for kb in range(nb):
                n = min(512, NK - kb * 512)
                P.op("pe", lambda e, kb=kb, n=n: e.matmul(
                    ps[:, (2 + kb) * 512:(2 + kb) * 512 + n], lhsT=bufQ[:, h, j * 128:(j + 1) * 128],
                    rhs=kT[:, kvh, kb * 512:kb * 512 + n], start=True, stop=True),
                    reads=[("bufQ", h, j)] + [("kT", kc // 8, kvh, kc % 8) for kc in range(nkc)],
                    writes=[("ps", 2 + kb)])

        def a2_head(j, h):
            NK = NKs[j]
            nkc = 9 + j
            nb = (NK + 511) // 512
            kvh = h // 4
            hh = h % 2
            Sm = S[hh]
            PTs = PTs2[hh]
            sk = ("slot", hh)
            if h == 0:
                qk(j, 0)
            P.op("dve", lambda e: e.scalar_tensor_tensor(
                out=Sm[:, 0:NK], in0=ps[:, 1024:1024 + NK], scalar=SCALE, in1=mb[j][:, 0:NK], op0=ALU.mult, op1=ALU.add),
                reads=[("ps", 2 + kb) for kb in range(nb)] + [("mb", j), sk], writes=[("S", hh)])
            if h < 15:
                qk(j, h + 1)
            P.op("dve", lambda e: e.reduce_max(out=mx[:, hh:hh + 1], in_=Sm[:, 0:NK], axis=AX.X, negate=True),
                 reads=[("S", hh), sk], writes=[("mx", hh)])
            P.op("act", lambda e: e.activation(
                out=Sm[:, 0:NK], in_=Sm[:, 0:NK], func=AF.Exp, bias=mx[:, hh:hh + 1], scale=1.0,
                accum_out=rs[:, hh:hh + 1]),
                reads=[("S", hh), ("mx", hh), sk], writes=[("S", hh), ("rs", hh)])
            kc = 0
            while kc < nkc:
                n = min(4, nkc - kc)
                for i in range(n):
                    P.op("pe", lambda e, i=i, kc=kc: e.transpose(
                        bank(6, 128, i * 128), Sm[:, (kc + i) * 128:(kc + i + 1) * 128], identf[:]),
                        reads=[("S", hh), "identf", sk], writes=[("ps", 6)])
                P.op("act", lambda e, kc=kc, n=n: e.activation(
                    out=PTs[:, kc:kc + n, :], in_=bank(6, n * 128).rearrange("p (a b) -> p a b", b=128), func=AF.Copy),
                    reads=[("ps", 6), ("slot", 2)], writes=[("PTs", hh)])
                kc += n
            for kc in range(nkc):
                P.op("pe", lambda e, kc=kc: e.matmul(
                    bank(7, 128), lhsT=PTs[:, kc, :], rhs=Vt[:, kc, kvh * 128:(kvh + 1) * 128],
                    start=(kc == 0), stop=(kc == nkc - 1)),
                    reads=[("PTs", hh), ("V", kc // 8, kc % 8, kvh // 2), ("slot", 2)], writes=[("ps", 7)])
            P.op("dve", lambda e: e.reciprocal(out=rinv[:, hh:hh + 1], in_=rs[:, hh:hh + 1]),
                 reads=[("rs", hh)], writes=[("rinv", hh)])
            P.op("act", lambda e: e.activation(
                out=yh[hh], in_=bank(7, 128), func=AF.Copy, scale=rinv[:, hh:hh + 1]),
                reads=[("ps", 7), ("rinv", hh)], writes=[("yh", hh)])
            P.op("pe", lambda e: e.transpose(bank(6, 128, 0), yh[hh], identf[:]),
                 reads=[("yh", hh), "identf"], writes=[("ps", 6)])
            P.op("act", lambda e: e.activation(out=bufQ[:, h, j * 128:(j + 1) * 128], in_=bank(6, 128), func=AF.Copy),
                 reads=[("ps", 6)], writes=[("bufQ", h, j)])

        def topk_pair(js, fillers):
            for r_ in range(32):
                for c_, j in enumerate(js):
                    NK = NKs[j]
                    acc = accs[j % 4]
                    P.op("dve", lambda e, acc=acc, NK=NK, c_=c_: e.max(out=m8s[c_][:], in_=acc[:, 0:NK]),
                         reads=[("acc", j % 4)], writes=[("m8", c_)])
                    if r_ < 31:
                        P.op("dve", lambda e, acc=acc, NK=NK, c_=c_: e.match_replace(
                            out=acc[:, 0:NK], in_to_replace=m8s[c_][:], in_values=acc[:, 0:NK], imm_value=-3.0e38),
                            reads=[("acc", j % 4), ("m8", c_)], writes=[("acc", j % 4)])
                for f_ in fillers[r_]:
                    f_()
            for c_, j in enumerate(js):
                NK = NKs[j]
                acc = accs[j % 4]
                P.op("dve", lambda e, c_=c_: e.tensor_scalar(out=thrs[c_][:], in0=m8s[c_][:, 7:8], scalar1=-1.0e29,
                                                             scalar2=None, op0=ALU.max),
                     reads=[("m8", c_)], writes=[("thr", c_)])
                P.op("dve", lambda e, acc=acc, NK=NK, j=j: e.tensor_scalar(
                    out=mb[j][:, 0:NK], in0=acc[:, 0:NK], scalar1=-2.0e38, scalar2=NEG, op0=ALU.is_gt, op1=ALU.mult),
                    reads=[("acc", j % 4)], writes=[("mb", j)])
                P.op("dve", lambda e, acc=acc, NK=NK, j=j, c_=c_: e.scalar_tensor_tensor(
                    out=mb[j][:, 0:NK], in0=acc[:, 0:NK], scalar=thrs[c_][:, 0:1], in1=mb[j][:, 0:NK],
                    op0=ALU.is_lt, op1=ALU.mult),
                    reads=[("acc", j % 4), ("thr", c_), ("mb", j)], writes=[("mb", j)])
                if j < 2:
                    P.dma("sp", lambda e, acc=acc, NK=NK, j=j: e.dma_start(
                        out=acc[:, 0:NK], in_=cbias[j * 128:(j + 1) * 128, 0:NK]),
                        reads=[("mb", j)], writes=[("acc", j % 4)], tag=("cb", j % 4))
                    P.op("dve", lambda e, acc=acc, NK=NK, j=j: e.tensor_tensor(
                        out=mb[j][:, 0:NK], in0=mb[j][:, 0:NK], in1=acc[:, 0:NK], op=ALU.add),
                        reads=[("acc", j % 4), ("mb", j)], writes=[("mb", j)])

        npair = NQB // 2
        if npair:
            for h in range(16):
                indexer_head(0, h)
            for h in range(16):
                indexer_head(1, h)
        for p_ in range(npair):
            fillers = [[] for _ in range(32)]
            if p_ + 1 < npair:
                for i_ in range(32):
                    fillers[i_].append(lambda i_=i_, p_=p_: indexer_head(2 * (p_ + 1) + i_ // 16, i_ % 16))
            if p_ >= 1:
                for i_ in range(32):
                    fillers[i_].append(lambda i_=i_, p_=p_: a2_head(2 * (p_ - 1) + i_ // 16, i_ % 16))
            topk_pair((2 * p_, 2 * p_ + 1), fillers)
        if npair:
            for i_ in range(32):
                a2_head(2 * (npair - 1) + i_ // 16, i_ % 16)
        if upto == "att":
            dump(mb[1][:, :], 128, NKs[1], [("mb", 1)], is_bf16=True)
            dump(bufQ[:, :, 128:256], 128, 2048, [("bufQ", h_, 1) for h_ in range(16)], is_bf16=True)
        barrier()
        P.dma("sp", lambda e: e.dma_start(out=actT.rearrange("p a b -> p (a b)"), in_=h1T_d),
              reads=["h1T_d"], writes=[("actT", dc_, t_) for dc_ in range(DC) for t_ in range(NT)], tag="h1T_ld")
        barrier()
        if upto in ("po", "att"):
            P.dma("sp", lambda e: e.dma_start(out=out[0:128, :], in_=LB["yfull"]), reads=[("yfull",)], tag="outd")
            final_tags.append("outd")
            P.emit(final_wait_tags=final_tags)
            return nc, P
        RB_ = rnn_bufs(RNN_OFF)
        rnn_group(1, RB_)
        if upto == "rn":
            dump(yrT[:, 0, :], 128, 1024, [("yrT", 0, n_) for n_ in range(2)], is_bf16=True)
            dump(yrT[:, 7, :], 128, 1024, [("yrT", 7, n_) for n_ in range(2)], is_bf16=True)
            dump(yrT[:, 19, :], 128, 1024, [("yrT", 19, n_) for n_ in range(2)], is_bf16=True)
            dump(bufQ[:, 3, :], 128, 1024, [("bufQ", 3, t_) for t_ in range(8)], is_bf16=True)
            P.dma("sp", lambda e: e.dma_start(out=out[0:128, :], in_=LB["yfull"]), reads=[("yfull",)], tag="outd")
            final_tags.append("outd")
            P.emit(final_wait_tags=final_tags)
            return nc, P
        barrier()
        MOFF = 106496
        mergedT = carve(MOFF, [128, 16, T], BF16)
        o_ = MOFF + 32768
        sga, sgr, m1 = [], [], []
        for lst in (sga, sgr, m1):
            for k in range(4):
                lst.append(carve(o_, [128, 512], F32)); o_ += 2048
        LB["rt"] = [carve(o_, [128, 512], F32), carve(o_ + 2048, [128, 512], F32)]
        LB["yt"] = [carve(o_ + 4096, [128, 512], F32), carve(o_ + 6144, [128, 512], F32)]
        for op_ in range(8):
            for which, sidx, dstl, b0 in ((0, S_G + op_, sga, 0), (1, S_G + 8 + op_, sgr, 4)):
                s = ring_load(w3[sidx], 4096)
                for q in range(2):
                    for n_ in range(2):
                        b = b0 + q * 2 + n_
                        for dc in range(DC):
                            P.op("pe", lambda e, b=b, dc=dc, q=q, n_=n_, s=s: e.matmul(
                                bank(b), lhsT=ring[s][:, dc * 256 + q * 128:dc * 256 + q * 128 + 128],
                                rhs=actT[:, dc, n_ * 512:(n_ + 1) * 512], start=(dc == 0), stop=(dc == DC - 1)),
                                reads=[("slot", s)] + [("actT", dc, tt) for tt in range(n_ * 4, n_ * 4 + 4)],
                                writes=[("ps", b)])
                        P.op("act", lambda e, b=b, d_=dstl[q * 2 + n_]: e.activation(out=d_, in_=bank(b), func=AF.Sigmoid),
                             reads=[("ps", b)], writes=[("sg", which, q * 2 + n_)])
            s = ring_load(wab[op_], 4096)
            for q in range(2):
                for n_ in range(2):
                    b = q * 2 + n_
                    for kc in range(16):
                        P.op("pe", lambda e, b=b, kc=kc, q=q, n_=n_, s=s: e.matmul(
                            bank(b), lhsT=ring[s][:, kc * 256 + q * 128:kc * 256 + q * 128 + 128],
                            rhs=bufQ[:, kc, n_ * 512:(n_ + 1) * 512], start=(kc == 0), stop=(kc == 15)),
                            reads=[("slot", s)] + [("bufQ", kc, tt) for tt in range(n_ * 4, n_ * 4 + 4)],
                            writes=[("ps", b)])
                    P.op("dve", lambda e, b=b, i_=q * 2 + n_: e.tensor_tensor(out=m1[i_], in0=sga[i_], in1=bank(b), op=ALU.mult),
                         reads=[("ps", b), ("sg", 0, q * 2 + n_)], writes=[("m1", q * 2 + n_)])
            for q in range(2):
                oc = 2 * op_ + q
                s = ring_load(wrb[oc], 2560)
                for n_ in range(2):
                    b = 4 + q * 2 + n_
                    for kc in range(RC):
                        P.op("pe", lambda e, b=b, kc=kc, n_=n_, s=s: e.matmul(
                            bank(b), lhsT=ring[s][:, kc * 128:(kc + 1) * 128],
                            rhs=yrT[:, kc, n_ * 512:(n_ + 1) * 512], start=(kc == 0), stop=(kc == RC - 1)),
                            reads=[("slot", s), ("yrT", kc, n_)], writes=[("ps", b)])
                    i_ = q * 2 + n_
                    P.op("dve", lambda e, b=b, i_=i_: e.tensor_tensor(out=sgr[i_], in0=sgr[i_], in1=bank(b), op=ALU.mult),
                         reads=[("ps", b), ("sg", 1, i_)], writes=[("sg", 1, i_)])
                    P.op("dve", lambda e, i_=i_, oc=oc, n_=n_: e.tensor_tensor(
                        out=mergedT[:, oc, n_ * 512:(n_ + 1) * 512], in0=m1[i_], in1=sgr[i_], op=ALU.add),
                        reads=[("m1", i_), ("sg", 1, i_)], writes=[("mT", oc, n_)])
        if upto == "mg":
            dump(mergedT[:, 0, :], 128, 1024, [("mT", 0, n_) for n_ in range(2)], is_bf16=True)
            dump(mergedT[:, 9, :], 128, 1024, [("mT", 9, n_) for n_ in range(2)], is_bf16=True)
        tok_out_stage(lambda kc, t: mergedT[:, kc, t * 128:(t + 1) * 128],
                      lambda kc, t: [("mT", kc, t // 4)], 16, wo, h1d, "h1d")
        barrier()
        LB["yfull"] = carve(MOFF, [128, 2048], F32)
        LB["gB"] = carve(MOFF + 8192, [128, 2048], F32)
        LB["bB"] = carve(MOFF + 16384, [128, 2048], F32)
        LB["yfull2"] = carve(MOFF + 24576, [128, 2048], F32)
        LB["yfull2_key"] = ("yfull2",)
        layer_norm_pass(1, h2d if upto != "mg" else dbg_out[3072:4096, :], "h2d", True)
        barrier()
        if upto == "mg":
            final_tags += [("hst_h2d", 0), ("hst_h2d", 1)]
            P.dma("sp", lambda e: e.dma_start(out=out[0:128, :], in_=LB["yfull"]), reads=[("yfull",)], tag="outd")
            final_tags.append("outd")
            P.emit(final_wait_tags=final_tags)
            return nc, P
        sa, xf1 = ffn_phase_bufs()
        ffn(h2d, "h2d", w2a, w2b, 2, out, "out", False, sa)
        final_tags += [("hst_out", 0), ("hst_out", 1)]
        P.emit(final_wait_tags=final_tags)
    return nc, P


def _tile_a(w, c0, ncols, pad):
    K = w.shape[0]
    blk = np.zeros((128, K // 128, pad), np.float32)
    blk[:, :, :ncols] = w[:, c0:c0 + ncols].reshape(K // 128, 128, ncols).transpose(1, 0, 2)
    return blk


def _pack_ffn(w_in, w_out):
    wa = np.empty((FC, 128, 2, 16, 128), np.float32)
    for fc in range(FC):
        wa[fc, :, 0] = _tile_a(w_in, fc * 128, 128, 128)
        wa[fc, :, 1] = _tile_a(w_in, FF + fc * 128, 128, 128)
    wb = np.empty((4, 11, 128, 4, 512), np.float32)
    wo = w_out.reshape(11, 4, 128, 4, 512)
    wb[:] = wo.transpose(3, 0, 2, 1, 4)
    return np.ascontiguousarray(wa.reshape(FC, 128, 4096)), np.ascontiguousarray(wb.reshape(FC, 128, 2048))


def _pack_shared(inp):
    sh = {}
    sh["w1a"], sh["w1b"] = _pack_ffn(inp["ffn1_w_in"][0], inp["ffn1_w_out"][0])
    sh["w2a"], sh["w2b"] = _pack_ffn(inp["ffn2_w_in"][0], inp["ffn2_w_out"][0])
    w_in = inp["w_in"][0]
    sl = w3_slots()
    w3 = np.empty((len(sl), 128, 16, 256), np.float32)
    for i, (c0, nc_) in enumerate(sl):
        w3[i] = _tile_a(w_in, c0, nc_, 256)
    sh["w3"] = w3.reshape(len(sl), 128, 4096)
    wab = np.empty((8, 128, 16, 256), np.float32)
    for i in range(8):
        wab[i] = _tile_a(inp["w_attn_branch"][0], i * 256, 256, 256)
    sh["wab"] = wab.reshape(8, 128, 4096)
    wrb = np.empty((16, 128, 20, 128), np.float32)
    for i in range(16):
        wrb[i] = _tile_a(inp["w_rnn_branch"][0], i * 128, 128, 128)
    sh["wrb"] = wrb.reshape(16, 128, 2560)
    wo = inp["w_out"][0].reshape(4, 4, 128, 4, 512)
    sh["wo"] = np.ascontiguousarray(wo.transpose(3, 0, 2, 1, 4)).reshape(16, 128, 2048)
    pairs = gate_pairs()
    for name, key in (("wga", "lru_wa"), ("wgx", "lru_wx")):
        bd = np.zeros((DRNN, DRNN), np.float32)
        for nb in range(16):
            bd[nb * 160:(nb + 1) * 160, nb * 160:(nb + 1) * 160] = inp[key][0, nb]
        g = np.empty((128, len(pairs), 128), np.float32)
        for k, (j, i) in enumerate(pairs):
            g[:, k, :] = bd[j * 128:(j + 1) * 128, i * 128:(i + 1) * 128]
        sh[name] = g.reshape(128, len(pairs) * 128)
    sh["lnp"] = np.stack([inp["ln1_g"][0], inp["ln1_b"][0], inp["ln2_g"][0], inp["ln2_b"][0],
                          inp["ln3_g"][0], inp["ln3_b"][0]]).astype(np.float32)
    rn = np.empty((128, 8, RC), np.float32)
    for j in range(4):
        rn[:, j, :] = inp["conv_w"][0, j].reshape(RC, 128).T
    rn[:, 4, :] = inp["conv_b"][0].reshape(RC, 128).T
    rn[:, 5, :] = inp["lru_ba"][0].reshape(RC, 128).T
    rn[:, 6, :] = inp["lru_bx"][0].reshape(RC, 128).T
    rn[:, 7, :] = inp["lru_lambda"][0].reshape(RC, 128).T
    sh["rnnp"] = rn.reshape(128, 8 * RC)
    sh["ident"] = np.eye(128, dtype=np.float32)
    invf = np.zeros((128, 24), np.float32)
    invf[:, 0:16] = (500000.0 ** (-(np.arange(16, dtype=np.float32) * 2.0 / 32.0))).astype(np.float32)[None]
    invf[:, 16:24] = (500000.0 ** (-(np.arange(8, dtype=np.float32) * 2.0 / 16.0))).astype(np.float32)[None]
    sh["invf"] = invf
    return sh


def _core_inputs(inp, sh, c):
    b, half = c // 2, c % 2
    x = inp["x"]
    pos = inp["positions"]
    x2 = np.zeros((2048, 2048), np.float32)
    p2 = np.zeros((2, 1024), np.int32)
    if half == 1:
        x2[:1024] = x[b, :1024]
        p2[0] = pos[b, :1024]
    x2[1024:] = x[b, half * 1024:(half + 1) * 1024]
    p2[1] = pos[b, half * 1024:(half + 1) * 1024]
    posi = np.ascontiguousarray(p2.reshape(2, 8, 128).transpose(2, 0, 1).reshape(128, 16))
    q = np.arange(1024)[:, None] + 1024
    k = np.arange(2048)[None, :]
    valid = (k <= q) & (k >= (0 if half == 1 else 1024))
    cb = np.where(valid, 0.0, NEG).astype(np.float32)
    m = dict(sh)
    m.update(x2=x2, posi=posi, cbias=cb, flag=np.full((128, 1), float(half), np.float32))
    return m


_CACHE = {}


def kernel(**inputs):
    inp = {k: np.asarray(v) for k, v in inputs.items()}
    if "nc" not in _CACHE:
        _CACHE["nc"] = build("all")[0]
    nc = _CACHE["nc"]
    sh = _pack_shared(inp)
    in_maps = [_core_inputs(inp, sh, c) for c in range(8)]
    res = run_bass_kernel_spmd(nc, in_maps, core_ids=list(range(8)))
    outp = np.empty((4, 2048, 2048), np.float32)
    for c in range(8):
        b, half = c // 2, c % 2
        outp[b, half * 1024:(half + 1) * 1024] = res.results[c]["out"]
    return outp
```

```python
import contextlib
import os
import math
import numpy as np
import concourse.bass as bass
import concourse.mybir as mybir
from concourse.bass_utils import run_bass_kernel_spmd

F32 = mybir.dt.float32
BF16 = mybir.dt.bfloat16
I32 = mybir.dt.int32
AF = mybir.ActivationFunctionType
ALU = mybir.AluOpType
AX = mybir.AxisListType

D = 2048
DC = 16
T = 1024
NT = 8
FF = 5632
FC = 44
DRNN = 2560
RC = 20
ALPHA = 2.0 ** 0.25
LN_EPS = 1e-5
SCALE = 128 ** -0.5
WI_SCALE = 1.0 / 32.0
NEG = -1.0e30
ENGINES = ("pe", "act", "dve", "pool", "sp")
EPOCH = 12000


class Prog:
    def __init__(self, nc):
        self.nc = nc
        self.ops = []

    def op(self, eng, fn, reads=(), writes=(), phase=True):
        r = tuple(reads) + (("PHASE",) if phase else ())
        self.ops.append(dict(eng=eng, fn=fn, reads=r, writes=tuple(writes), dma=None))

    def dma(self, eng, fn, reads=(), writes=(), tag=None, phase=True):
        r = tuple(reads) + (("PHASE",) if phase else ())
        self.ops.append(dict(eng=eng, fn=fn, reads=r, writes=tuple(writes), dma=tag))

    def emit(self, final_wait_tags=()):
        nc = self.nc
        ops = self.ops
        n = len(ops)
        last_write = {}
        readers = {}
        need = [None] * n
        signal = [False] * n
        for i, o in enumerate(ops):
            d = set()
            for r in o["reads"]:
                lw = last_write.get(r)
                if lw is not None:
                    d.add(lw)
            for w in o["writes"]:
                lw = last_write.get(w)
                if lw is not None:
                    d.add(lw)
                rd = readers.get(w)
                if rd:
                    d.update(rd[0].values())
                    d.update(rd[1])
            d.discard(i)
            lst = []
            for j in d:
                oj = ops[j]
                if oj["dma"] is None:
                    if oj["eng"] == o["eng"] and oj["eng"] == "pe" and o["dma"] is None:
                        continue
                    signal[j] = True
                lst.append(j)
            need[i] = lst
            for r in o["reads"]:
                rd = readers.setdefault(r, ({}, []))
                if o["dma"] is None:
                    rd[0][o["eng"]] = i
                else:
                    rd[1].append(i)
            for w in o["writes"]:
                last_write[w] = i
                readers[w] = ({}, [])
        eng_count = {e: 0 for e in ENGINES}
        tag_count = {}
        semkey = [None] * n
        val = [0] * n
        keys = []
        for i, o in enumerate(ops):
            if o["dma"] is not None:
                t = o["dma"]
                c = tag_count.get(t, 0) + 16
                tag_count[t] = c
                k = ("t", t, c // 32000)
                val[i] = c - (c // 32000) * 32000 if c // 32000 else c
                assert c < 32000, t
                k = ("t", t, 0)
                val[i] = c
            elif signal[i]:
                eng_count[o["eng"]] += 1
                c = eng_count[o["eng"]]
                ep = (c - 1) // EPOCH
                k = ("e", o["eng"], ep)
                val[i] = c - ep * EPOCH
            else:
                continue
            semkey[i] = k
            if k not in keys:
                keys.append(k)
        self.stats = dict(n_ops=n, eng_count=dict(eng_count), n_sems=len(keys))
        with contextlib.ExitStack() as st:
            sems = {}
            for idx, k in enumerate(keys):
                sems[k] = st.enter_context(nc.semaphore("sm%d" % idx))
            block = st.enter_context(nc.Block())
            per_eng = {e: [i for i, o in enumerate(ops) if o["eng"] == e] for e in ENGINES}

            def run(ename, eng):
                known = {}
                for i in per_eng[ename]:
                    o = ops[i]
                    waits = {}
                    for j in need[i]:
                        k = semkey[j]
                        if val[j] > waits.get(k, 0):
                            waits[k] = val[j]
                    for k, v in waits.items():
                        if known.get(k, 0) >= v:
                            continue
                        if k[0] == "e":
                            later = [kk for kk in known if kk[0] == "e" and kk[1] == k[1] and kk[2] > k[2]]
                            if later:
                                continue
                        known[k] = v
                        eng.wait_ge(sems[k], v)
                    ins = o["fn"](eng)
                    if o["dma"] is not None:
                        ins.then_inc(sems[semkey[i]], 16)
                    elif signal[i]:
                        ins.then_inc(sems[semkey[i]], 1)
                if ename == "sp":
                    for t in final_wait_tags:
                        eng.wait_ge(sems[("t", t, 0)], tag_count[t])

            @block.sync
            def _(e):
                run("sp", e)

            @block.scalar
            def _(e):
                run("act", e)

            @block.vector
            def _(e):
                run("dve", e)

            @block.gpsimd
            def _(e):
                run("pool", e)

            @block.tensor
            def _(e):
                run("pe", e)


def w3_slots():
    sl = []
    for s in range(8):
        sl.append((256 * s, 256))
    for s in range(2):
        sl.append((2048 + 256 * s, 256))
    for s in range(2):
        sl.append((2560 + 256 * s, 256))
    for s in range(4):
        sl.append((3072 + 256 * s, 256))
    sl.append((4096, 80))
    for s in range(10):
        sl.append((4176 + 256 * s, 256))
    for s in range(10):
        sl.append((6736 + 256 * s, 256))
    for s in range(16):
        sl.append((9296 + 256 * s, 256))
    return sl


S_Q, S_K, S_V, S_QI, S_KIWI, S_RX, S_RG, S_G = 0, 8, 10, 12, 16, 17, 27, 37


def gate_pairs():
    pairs = []
    for i in range(RC):
        for j in range(RC):
            hit = False
            for nb in range(16):
                lo, hi = nb * 160, (nb + 1) * 160
                if max(lo, j * 128) < min(hi, (j + 1) * 128) and max(lo, i * 128) < min(hi, (i + 1) * 128):
                    hit = True
            if hit:
                pairs.append((j, i))
    return pairs


ARENA_BYTES = 172032
OFF_ACTT = 0
OFF_Q = 32768
OFF_R = 65536
OFF_KT = 65536
OFF_V = 81920
OFF_QIT = 98304
OFF_KIT = 114688
OFF_T = 122880
NSLOT = 3
SLOT_ELEMS = 4096


XCOPY = {}


def build(upto="all"):
    nc = bass.Bass("TRN2", target_bir_lowering=False)
    P = Prog(nc)

    def din(name, shape, dt=F32):
        return nc.dram_tensor(name, shape, dt, kind="ExternalInput").ap()

    x2 = din("x2", [2048, 2048])
    posi = din("posi", [128, 16], I32)
    invf = din("invf", [128, 24])
    ident = din("ident", [128, 128])
    cbias = din("cbias", [1024, 2048])
    flag = din("flag", [128, 1])
    w1a = din("w1a", [FC, 128, 4096])
    w1b = din("w1b", [FC, 128, 2048])
    w2a = din("w2a", [FC, 128, 4096])
    w2b = din("w2b", [FC, 128, 2048])
    w3 = din("w3", [53, 128, 4096])
    wab = din("wab", [8, 128, 4096])
    wrb = din("wrb", [16, 128, 2560])
    wo = din("wo", [16, 128, 2048])
    pairs = gate_pairs()
    NP = len(pairs)
    wga = din("wga", [128, NP * 128])
    wgx = din("wgx", [128, NP * 128])
    lnp = din("lnp", [6, 2048])
    rnnp = din("rnnp", [128, 8 * RC])
    out = nc.dram_tensor("out", [1024, 2048], F32, kind="ExternalOutput").ap()
    ypre = nc.dram_tensor("ypre", [1024, 2048], F32).ap()
    h1d = nc.dram_tensor("h1d", [1024, 2048], F32).ap()
    h2d = nc.dram_tensor("h2d", [1024, 2048], F32).ap()
    kTc_d = nc.dram_tensor("kTc_d", [128, 4096], BF16).ap()
    Vc_d = nc.dram_tensor("Vc_d", [128, 4096], BF16).ap()
    kiTc_d = nc.dram_tensor("kiTc_d", [128, 1024], BF16).ap()
    dbg_out = None
    if upto != "all":
        dbg_out = nc.dram_tensor("dbg", [4096, 2048], F32, kind="ExternalOutput").ap()

    final_tags = []
    with contextlib.ExitStack() as st:
        def sb(name, shape, dt):
            return st.enter_context(nc.sbuf_tensor(name, shape, dt))

        arena = sb("arena", [128, ARENA_BYTES // 4], F32)
        ring = [sb("ring%d" % i, [128, SLOT_ELEMS], BF16) for i in range(NSLOT)]
        ps = st.enter_context(nc.psum_tensor("ps", [128, 4096], F32))
        identb = sb("identb", [128, 128], BF16)
        posf = sb("posf", [128, 16], F32)
        posi_s = sb("posi_s", [128, 16], I32)
        invf_s = sb("invf_s", [128, 24], F32)
        flag_s = sb("flag_s", [128, 1], F32)
        rnnp_s = sb("rnnp_s", [128, 8 * RC], F32)
        c1_s = sb("c1_s", [128, RC], F32)
        c2_s = sb("c2_s", [128, RC], F32)
        hlast = sb("hlast", [128, RC], F32)
        rxhalo = sb("rxhalo", [128, RC, 4], F32)
        stats = sb("stats", [128, NT, 4, 6], F32)
        mv = sb("mv", [128, NT, 2], F32)
        sd = sb("sd", [128, NT], F32)
        rstd = sb("rstd", [128, NT], F32)
        epst = sb("epst", [128, 1], F32)
        witm = sb("witm", [128, NT, 16], F32)
        m8 = sb("m8", [128, 8], F32)
        thr = sb("thr", [128, 1], F32)
        mx = sb("mx", [128, 2], F32)
        rs = sb("rs", [128, 2], F32)
        rinv = sb("rinv", [128, 2], F32)
        bar_t = sb("bar_t", [128, 4], F32)

        def carve(off, shape, dt):
            esz = 2 if dt == BF16 else 4
            nel = 1
            for s_ in shape[1:]:
                nel *= s_
            nb = nel * esz
            assert off % 4 == 0 and nb % 4 == 0 and off + nb <= ARENA_BYTES, (off, nb)
            v = arena[:, off // 4:(off + nb) // 4]
            if dt == BF16:
                v = v.bitcast(BF16)
            elif dt == I32:
                v = v.bitcast(I32)
            if len(shape) == 3:
                v = v.rearrange("p (a b) -> p a b", b=shape[2])
            return v

        def bank(b, n=512, off=0):
            return ps[:, b * 512 + off:b * 512 + off + n]

        def bankbf(b, nb=1):
            return ps[:, b * 512:(b + nb) * 512].bitcast(BF16)

        barrier_n = [0]

        def barrier():
            barrier_n[0] += 1
            P.op("dve", lambda e: e.memset(bar_t[:, 0:1], 0.0), reads=[], writes=["PHASE"], phase=False)

        ring_i = [0]

        def ring_load(src, nel):
            s = ring_i[0] % NSLOT
            ring_i[0] += 1
            k = 1
            while nel // k > 2048 or nel % k:
                k += 1
            dst = ring[s][:, 0:nel].rearrange("p (a b) -> p a b", a=k)
            srcv = src.rearrange("p (a b) -> p a b", a=k)
            P.dma("pool", lambda e: e.dma_start(out=dst, in_=srcv), writes=[("slot", s)],
                  tag=("slot", s), phase=False)
            return s

        P.dma("pool", lambda e: e.dma_start(out=identb[:], in_=ident), writes=["identb"], tag="c_ident")
        P.dma("sp", lambda e: e.dma_start(out=posi_s[:], in_=posi), writes=["posi_s"], tag="c_posi")
        P.dma("sp", lambda e: e.dma_start(out=invf_s[:], in_=invf), writes=["invf_s"], tag="c_invf")
        P.dma("sp", lambda e: e.dma_start(out=flag_s[:], in_=flag), writes=["flag_s"], tag="c_flag")
        P.dma("sp", lambda e: e.dma_start(out=rnnp_s[:], in_=rnnp), writes=["rnnp_s"], tag="c_rnnp")
        P.op("dve", lambda e: e.memset(epst[:], LN_EPS), writes=["epst"])
        P.op("dve", lambda e: e.memset(hlast[:], 0.0), writes=["hlast"])
        P.op("dve", lambda e: e.memset(rxhalo[:], 0.0), writes=["rxhalo"])
        P.op("dve", lambda e: e.tensor_copy(out=posf[:], in_=posi_s[:]), reads=["posi_s"], writes=["posf"])

        LAM = 7 * RC
        P.op("act", lambda e: e.activation(out=c1_s[:], in_=rnnp_s[:, LAM:LAM + RC], func=AF.Exp, scale=-1.0),
             reads=["rnnp_s"], writes=["c1_s"])
        P.op("dve", lambda e: e.tensor_scalar(out=c1_s[:], in0=c1_s[:], scalar1=1.0, scalar2=None, op0=ALU.add),
             reads=["c1_s"], writes=["c1_s"])
        P.op("act", lambda e: e.activation(out=c1_s[:], in_=c1_s[:], func=AF.Ln),
             reads=["c1_s"], writes=["c1_s"])
        P.op("dve", lambda e: e.tensor_scalar(out=c2_s[:], in0=c1_s[:], scalar1=-16.0, scalar2=None, op0=ALU.mult),
             reads=["c1_s"], writes=["c2_s"])
        P.op("dve", lambda e: e.tensor_scalar(out=c1_s[:], in0=c1_s[:], scalar1=-8.0, scalar2=None, op0=ALU.mult),
             reads=["c1_s"], writes=["c1_s"])

        identf = sb("identf", [128, 128], F32)
        onest = sb("onest", [128, 1], F32)
        P.op("dve", lambda e: e.memset(onest[:], 1.0), writes=["onest"])
        nmr = sb("nmr", [128, NT], F32)
        P.dma("sp", lambda e: e.dma_start(out=identf[:], in_=ident), writes=["identf"], tag="c_identf")
        cosq_r = sb("cosq_r", [128, 16, 32], F32)
        sinq_r = sb("sinq_r", [128, 16, 32], F32)
        cosi_r = sb("cosi_r", [128, 16, 32], F32)
        sini_r = sb("sini_r", [128, 16, 32], F32)
        trig_t = carve(OFF_T + 45056, [128, 16, 16], F32)
        trig_k = carve(OFF_T + 46080, [128, 16, 16], F32)
        trig_ki = carve(OFF_T + 47104, [128, 16, 16], I32)

        def trig_table(dst, ncol, col0, shift, nrep):
            tv = trig_t[:, :, 0:ncol]
            kv = trig_k[:, :, 0:ncol]
            kiv = trig_ki[:, :, 0:ncol]
            for gt in range(16):
                P.op("dve", lambda e, gt=gt: e.tensor_scalar(
                    out=trig_t[:, gt, 0:ncol], in0=invf_s[:, col0:col0 + ncol], scalar1=posf[:, gt:gt + 1],
                    scalar2=shift, op0=ALU.mult, op1=ALU.add),
                    reads=["posf", "invf_s", "trig_t"], writes=["trig_t"])
            two_pi = 2.0 * math.pi
            P.op("dve", lambda e: e.tensor_scalar(out=kv, in0=tv, scalar1=1.0 / two_pi, scalar2=None, op0=ALU.mult),
                 reads=["trig_t"], writes=["trig_k"])
            P.op("dve", lambda e: e.tensor_copy(out=kiv, in_=kv), reads=["trig_k"], writes=["trig_ki"])
            P.op("dve", lambda e: e.tensor_copy(out=kv, in_=kiv), reads=["trig_ki"], writes=["trig_k"])
            P.op("dve", lambda e: e.scalar_tensor_tensor(out=tv, in0=kv, scalar=-two_pi, in1=tv,
                                                         op0=ALU.mult, op1=ALU.add),
                 reads=["trig_k", "trig_t"], writes=["trig_t"])
            P.op("dve", lambda e: e.tensor_scalar(out=kv, in0=tv, scalar1=math.pi, scalar2=-two_pi,
                                                  op0=ALU.is_gt, op1=ALU.mult),
                 reads=["trig_t"], writes=["trig_k"])
            P.op("dve", lambda e: e.tensor_tensor(out=tv, in0=tv, in1=kv, op=ALU.add),
                 reads=["trig_t", "trig_k"], writes=["trig_t"])
            P.op("dve", lambda e: e.tensor_scalar(out=kv, in0=tv, scalar1=-math.pi, scalar2=two_pi,
                                                  op0=ALU.is_lt, op1=ALU.mult),
                 reads=["trig_t"], writes=["trig_k"])
            P.op("dve", lambda e: e.tensor_tensor(out=tv, in0=tv, in1=kv, op=ALU.add),
                 reads=["trig_t", "trig_k"], writes=["trig_t"])
            P.op("dve", lambda e: e.tensor_scalar(out=tv, in0=tv, scalar1=3.14159, scalar2=-3.14159,
                                                  op0=ALU.min, op1=ALU.max),
                 reads=["trig_t"], writes=["trig_t"])
            for r_ in range(nrep):
                P.op("act", lambda e, r_=r_: e.activation(out=dst[:, :, r_ * ncol:(r_ + 1) * ncol], in_=tv, func=AF.Sin),
                     reads=["trig_t"], writes=["trig_rep"])

        trig_table(sinq_r, 16, 0, 0.0, 2)
        trig_table(cosq_r, 16, 0, math.pi / 2, 2)
        trig_table(sini_r, 8, 16, 0.0, 4)
        trig_table(cosi_r, 8, 16, math.pi / 2, 4)
        barrier()


        actT = carve(OFF_ACTT, [128, DC, T], BF16)
        bufQ = carve(OFF_Q, [128, 16, T], BF16)
        yrT = carve(OFF_R, [128, RC, T], BF16)
        kT = carve(OFF_KT, [128, 4, 2048], BF16)
        Vt = carve(OFF_V, [128, 16, 512], BF16)
        qiT = carve(OFF_QIT, [128, 8, T], BF16)
        kiT = carve(OFF_KIT, [128, 2048], BF16)
        gT = carve(OFF_Q, [128, FC, T], BF16)

        dump_row = [0]

        def dump(src, nrows_p, ncols, reads, is_bf16=False, src_dram=False):
            if dbg_out is None:
                return
            r0 = dump_row[0]
            dump_row[0] += 128
            tag = "dump%d" % r0
            dst = dbg_out[r0:r0 + nrows_p, 0:ncols]
            if len(src.shape) == 3:
                dst = dst.rearrange("p (a b) -> p a b", b=src.shape[2])
            q = "pool" if is_bf16 else "sp"
            P.dma(q, lambda e: e.dma_start(out=dst, in_=src), reads=reads, writes=[tag], tag=tag)
            final_tags.append(tag)
            return r0

        tr_rr = [0]

        def transpose_cols(src_f32, ncol, dst_fn, src_res, dst_res, banks=(4, 5, 6, 7), eng_pref=None):
            nch = ncol // 128
            c = 0
            while c < nch:
                n = min(4, nch - c)
                b = banks[tr_rr[0] % len(banks)]
                tr_rr[0] += 1
                for i in range(n):
                    P.op("pe", lambda e, i=i, c=c, b=b: e.transpose(
                        bank(b, 128, i * 128), src_f32[:, (c + i) * 128:(c + i + 1) * 128], identf[:]),
                        reads=list(src_res) + ["identf"], writes=[("ps", b)])
                dst = dst_fn(c, n)
                srcv = bank(b, n * 128).rearrange("p (a b) -> p a b", b=128)
                eng = eng_pref or ("act" if tr_rr[0] % 2 else "dve")
                if eng == "act":
                    P.op("act", lambda e, dst=dst, srcv=srcv: e.activation(out=dst, in_=srcv, func=AF.Copy),
                         reads=[("ps", b)], writes=list(dst_res(c, n)))
                else:
                    P.op("dve", lambda e, dst=dst, srcv=srcv: e.tensor_copy(out=dst, in_=srcv),
                         reads=[("ps", b)], writes=list(dst_res(c, n)))
                c += n

        LB = {}

        def set_ln_bufs(off):
            o_ = off
            LB["yfull"] = carve(o_, [128, 2048], F32); o_ += 8192
            LB["gB"] = carve(o_, [128, 2048], F32); o_ += 8192
            LB["bB"] = carve(o_, [128, 2048], F32); o_ += 8192
            LB["rt"] = []
            LB["yt"] = []
            for k in range(2):
                LB["rt"].append(carve(o_, [128, 512], F32)); o_ += 2048
            for k in range(2):
                LB["yt"].append(carve(o_, [128, 512], F32)); o_ += 2048
            return o_

        cnt = dict(sa=0, rt=0, stg=0, pj=0, rb=0)

        def load_xT(row0, xf1):
            xf = [LB["yfull"], xf1]
            for t in range(NT):
                k = t % 2
                P.dma("sp", lambda e, t=t, k=k: e.dma_start(out=xf[k], in_=x2[row0 + t * 128:row0 + (t + 1) * 128, :]),
                      writes=[("xf", k)] + ([("yfull",)] if k == 0 else []), tag=("xf", k))
                transpose_cols(xf[k], 2048, lambda c, n, t=t: actT[:, c:c + n, t * 128:(t + 1) * 128],
                               [("xf", k)] + ([("yfull",)] if k == 0 else []), lambda c, n, t=t: [("actT", cc, t) for cc in range(c, c + n)])

        def tok_out_stage(lhs_fn, lhs_res, nk, wsrc, resid, resid_res):
            nl = nk // 4
            for cb in range(4):
                s4 = 0
                while s4 < nl:
                    na = min(2, nl - s4)
                    s = ring_i[0] % NSLOT
                    ring_i[0] += 1
                    dstv = ring[s][:, 0:na * 2048].rearrange("p (a b) -> p a b", a=na)
                    srcv = wsrc[cb * nl + s4:cb * nl + s4 + na].rearrange("a p n -> p a n")
                    P.dma("pool", lambda e, dstv=dstv, srcv=srcv: e.dma_start(out=dstv, in_=srcv),
                          writes=[("slot", s)], tag=("slot", s), phase=False)
                    for t in range(NT):
                        for a_ in range(na):
                            for j in range(4):
                                kc = (s4 + a_) * 4 + j
                                P.op("pe", lambda e, t=t, kc=kc, j=j, s=s, a_=a_: e.matmul(
                                    bank(t), lhsT=lhs_fn(kc, t),
                                    rhs=ring[s][:, a_ * 2048 + j * 512:a_ * 2048 + (j + 1) * 512],
                                    start=(kc == 0), stop=(kc == nk - 1)),
                                    reads=[("slot", s)] + lhs_res(kc, t), writes=[("ps", t)])
                    s4 += na
                for t in range(NT):
                    k = cnt["rt"] % 2
                    cnt["rt"] += 1
                    rt, yt = LB["rt"], LB["yt"]
                    P.dma("sp", lambda e, k=k, t=t, cb=cb, rt=rt: e.dma_start(
                        out=rt[k], in_=resid[t * 128:(t + 1) * 128, cb * 512:(cb + 1) * 512]),
                        reads=[(resid_res, t)], writes=[("rt", k)], tag=("rt", k))
                    P.op("dve", lambda e, k=k, t=t, rt=rt, yt=yt: e.scalar_tensor_tensor(
                        out=yt[k], in0=rt[k], scalar=ALPHA, in1=bank(t), op0=ALU.mult, op1=ALU.add),
                        reads=[("rt", k), ("ps", t)], writes=[("yt", k)])
                    P.op("dve", lambda e, k=k, t=t, cb=cb, yt=yt: e.bn_stats(out=stats[:, t, cb, :], in_=yt[k]),
                         reads=[("yt", k)], writes=[("stats", t)])
                    P.dma("act", lambda e, k=k, t=t, cb=cb, yt=yt: e.dma_start(
                        out=ypre[t * 128:(t + 1) * 128, cb * 512:(cb + 1) * 512], in_=yt[k]),
                        reads=[("yt", k)], writes=[("ypre", t)], tag=("yts", k))

        def layer_norm_pass(ln_idx, dst_dram, dst_res, want_T):
            yfs = [LB["yfull"], LB["yfull2"]]
            yks = [("yfull",), LB["yfull2_key"]]
            gB, bB = LB["gB"], LB["bB"]
            P.dma("sp", lambda e: e.dma_start(out=gB, in_=lnp[2 * ln_idx].partition_broadcast(128)),
                  writes=["gB"], tag="gB")
            P.dma("sp", lambda e: e.dma_start(out=bB, in_=lnp[2 * ln_idx + 1].partition_broadcast(128)),
                  writes=["bB"], tag="bB")
            for t in range(NT):
                yf, yk = yfs[t % 2], yks[t % 2]
                P.op("dve", lambda e, t=t: e.bn_aggr(out=mv[:, t, :], in_=stats[:, t, :, :].rearrange("p a b -> p (a b)")),
                     reads=[("stats", t)], writes=[("mv", t)])
                P.op("act", lambda e, t=t: e.activation(out=sd[:, t:t + 1], in_=mv[:, t, 1:2], func=AF.Sqrt,
                                                        bias=epst[:, 0:1], scale=1.0),
                     reads=[("mv", t), "epst"], writes=[("sd", t)])
                P.op("dve", lambda e, t=t: e.reciprocal(out=rstd[:, t:t + 1], in_=sd[:, t:t + 1]),
                     reads=[("sd", t)], writes=[("rstd", t)])
                P.op("dve", lambda e, t=t: e.tensor_scalar(out=nmr[:, t:t + 1], in0=mv[:, t, 0:1], scalar1=rstd[:, t:t + 1],
                                                           scalar2=-1.0, op0=ALU.mult, op1=ALU.mult),
                     reads=[("mv", t), ("rstd", t)], writes=[("nmr", t)])
                P.dma("sp", lambda e, t=t, yf=yf: e.dma_start(out=yf, in_=ypre[t * 128:(t + 1) * 128, :]),
                      reads=[("ypre", t)], writes=[yk], tag=("yfl", t % 2))
                P.op("act", lambda e, t=t, yf=yf: e.activation(out=yf, in_=yf, func=AF.Identity,
                                                               scale=rstd[:, t:t + 1], bias=nmr[:, t:t + 1]),
                     reads=[yk, ("nmr", t), ("rstd", t)], writes=[yk])
                P.op("dve", lambda e, yf=yf: e.tensor_tensor(out=yf, in0=yf, in1=gB, op=ALU.mult),
                     reads=[yk, "gB"], writes=[yk])
                P.op("dve", lambda e, yf=yf: e.tensor_tensor(out=yf, in0=yf, in1=bB, op=ALU.add),
                     reads=[yk, "bB"], writes=[yk])
                if dst_dram is not None:
                    P.dma("sp", lambda e, t=t, yf=yf: e.dma_start(out=dst_dram[t * 128:(t + 1) * 128, :], in_=yf),
                          reads=[yk], writes=[(dst_res, t)], tag=("hst_" + dst_res, t % 2))
                if want_T:
                    transpose_cols(yf, 2048, lambda c, n, t=t: actT[:, c:c + n, t * 128:(t + 1) * 128],
                                   [yk], lambda c, n, t=t: [("actT", cc, t) for cc in range(c, c + n)])

        def ffn(resid, resid_res, wa, wb, ln_idx, dst_dram, dst_res, want_T, sa):
            for fc in range(FC):
                s = ring_load(wa[fc], 4096)
                base = (fc % 2) * 4
                for ab in range(2):
                    for n_ in range(2):
                        b = base + ab * 2 + n_
                        for dc in range(DC):
                            P.op("pe", lambda e, b=b, ab=ab, dc=dc, n_=n_, s=s: e.matmul(
                                bank(b), lhsT=ring[s][:, (ab * 16 + dc) * 128:(ab * 16 + dc + 1) * 128],
                                rhs=actT[:, dc, n_ * 512:(n_ + 1) * 512], start=(dc == 0), stop=(dc == DC - 1)),
                                reads=[("slot", s)] + [("actT", dc, tt) for tt in range(n_ * 4, n_ * 4 + 4)],
                                writes=[("ps", b)])
                for n_ in range(2):
                    k = cnt["sa"] % 2
                    cnt["sa"] += 1
                    ba, bb = base + n_, base + 2 + n_
                    P.op("act", lambda e, k=k, ba=ba: e.activation(out=sa[k], in_=bank(ba), func=AF.Silu),
                         reads=[("ps", ba)], writes=[("sa", k)])
                    P.op("dve", lambda e, k=k, bb=bb, fc=fc, n_=n_: e.scalar_tensor_tensor(
                        out=gT[:, fc, n_ * 512:(n_ + 1) * 512], in0=sa[k], scalar=0.5, in1=bank(bb),
                        op0=ALU.mult, op1=ALU.mult),
                        reads=[("sa", k), ("ps", bb)], writes=[("gT", fc, n_)])
            tok_out_stage(lambda kc, t: gT[:, kc, t * 128:(t + 1) * 128],
                          lambda kc, t: [("gT", kc, t // 4)], FC, wb, resid, resid_res)
            layer_norm_pass(ln_idx, dst_dram, dst_res, want_T)

        def ffn_phase_bufs():
            o_ = set_ln_bufs(OFF_T)
            sa = []
            for k in range(2):
                sa.append(carve(o_, [128, 512], F32)); o_ += 2048
            xf1 = carve(o_, [128, 2048], F32); o_ += 8192
            LB["yfull2"] = xf1
            LB["yfull2_key"] = ("xf", 1)
            return sa, xf1

        pend = [None]

        def proj_tok(slot_idx, ncols, handler, PB):
            s = ring_load(w3[slot_idx], 4096)
            for t in range(NT):
                b = cnt["pj"] % 4
                cnt["pj"] += 1
                for dc in range(DC):
                    P.op("pe", lambda e, b=b, dc=dc, t=t, s=s: e.matmul(
                        bank(b, ncols), lhsT=actT[:, dc, t * 128:(t + 1) * 128],
                        rhs=ring[s][:, dc * 256:dc * 256 + ncols], start=(dc == 0), stop=(dc == DC - 1)),
                        reads=[("slot", s), ("actT", dc, t)], writes=[("ps", b)])
                if pend[0] is not None:
                    pend[0]()
                pend[0] = handler(t, b)
            if pend[0] is not None:
                pend[0]()
                pend[0] = None

        def rope_to_stg(b, stg, k, nh, hd, half, cos_r, sin_r, gt, tmps, ncopy=None):
            ncol = nh * hd
            ncp = ncopy or ncol
            P.op("act", lambda e: e.activation(out=stg[k][:, 0:ncp], in_=bank(b, ncp), func=AF.Copy),
                 reads=[("ps", b)], writes=[("stg", k)])
            if "rope" in os.environ.get("K_SKIP", ""):
                return
            s3 = stg[k][:, 0:ncol].rearrange("p (h d) -> p h d", h=nh)
            t1 = s3[:, :, 0:half]
            t2 = s3[:, :, half:2 * half]
            w = nh * half
            cosb = cos_r[:, gt, 0:w].rearrange("p (h d) -> p h d", h=nh)
            sinb = sin_r[:, gt, 0:w].rearrange("p (h d) -> p h d", h=nh)
            m = [tmps[i][:, 0:w].rearrange("p (h d) -> p h d", h=nh) for i in range(4)]
            for i, (a_, b_) in enumerate(((t1, cosb), (t2, sinb), (t2, cosb), (t1, sinb))):
                P.op("dve", lambda e, i=i, a_=a_, b_=b_: e.tensor_tensor(out=m[i], in0=a_, in1=b_, op=ALU.mult),
                     reads=[("stg", k), "trig_rep"], writes=[("ropetmp", i)])
            P.op("dve", lambda e: e.tensor_tensor(out=t1, in0=m[0], in1=m[1], op=ALU.subtract),
                 reads=[("ropetmp", 0), ("ropetmp", 1)], writes=[("stg", k)])
            P.op("dve", lambda e: e.tensor_tensor(out=t2, in0=m[2], in1=m[3], op=ALU.add),
                 reads=[("ropetmp", 2), ("ropetmp", 3)], writes=[("stg", k)])

        def proj_group(g, PB, kT_dst, V_dst, kiT_dst, key0):
            stg, tmps = PB["stg"], PB["tmps"]

            def nxt():
                k = cnt["stg"] % 2
                cnt["stg"] += 1
                return k

            if g == 1:
                for sq in range(8):
                    def hq(t, b, sq=sq):
                        k = nxt()
                        rope_to_stg(b, stg, k, 2, 128, 16, cosq_r, sinq_r, g * 8 + t, tmps)
                        return lambda: transpose_cols(stg[k], 256,
                                       lambda c, n, t=t: bufQ[:, 2 * sq + c:2 * sq + c + n, t * 128:(t + 1) * 128],
                                       [("stg", k)], lambda c, n, t=t: [("bufQ", 2 * sq + cc, t) for cc in range(c, c + n)])
                    proj_tok(S_Q + sq, 256, hq, PB)
            SK_ = os.environ.get("K_SKIP", "").split(",")
            for sk in range(0 if "nok" in SK_ else 2):
                def hk(t, b, sk=sk):
                    k = nxt()
                    rope_to_stg(b, stg, k, 2, 128, 16, cosq_r, sinq_r, g * 8 + t, tmps)
                    return lambda: transpose_cols(stg[k], 256,
                                   lambda c, n, t=t: kT_dst[:, 2 * sk + c:2 * sk + c + n, key0 + t * 128:key0 + (t + 1) * 128],
                                   [("stg", k)], lambda c, n, t=t: [("kT", g, 2 * sk + cc, t) for cc in range(c, c + n)])
                proj_tok(S_K + sk, 256, hk, PB)
            for sv_ in range(0 if "nov" in SK_ else 2):
                def hv(t, b, sv_=sv_):
                    dstv = V_dst[:, key0 // 128 + t, sv_ * 256:(sv_ + 1) * 256]
                    P.op("act", lambda e: e.activation(out=dstv, in_=bank(b, 256), func=AF.Copy),
                         reads=[("ps", b)], writes=[("V", g, t, sv_)])
                proj_tok(S_V + sv_, 256, hv, PB)
            if g == 1:
                for sq in range(4):
                    def hqi(t, b, sq=sq):
                        k = nxt()
                        rope_to_stg(b, stg, k, 4, 64, 8, cosi_r, sini_r, g * 8 + t, tmps)
                        return lambda: transpose_cols(stg[k], 256,
                                       lambda c, n, t=t: qiT[:, 2 * sq + c:2 * sq + c + n, t * 128:(t + 1) * 128],
                                       [("stg", k)], lambda c, n, t=t: [("qiT", 2 * sq + cc, t) for cc in range(c, c + n)])
                    proj_tok(S_QI + sq, 256, hqi, PB)

            def hki(t, b):
                k = nxt()
                rope_to_stg(b, stg, k, 1, 64, 8, cosi_r, sini_r, g * 8 + t, tmps, ncopy=128)
                if g == 1:
                    P.op("dve", lambda e: e.tensor_scalar(out=witm[:, t, :], in0=stg[k][:, 64:80], scalar1=WI_SCALE,
                                                          scalar2=None, op0=ALU.mult),
                         reads=[("stg", k)], writes=[("witm", t)])
                P.op("dve", lambda e: e.tensor_copy(out=stg[k][:, 64:128], in_=stg[k][:, 0:64]),
                     reads=[("stg", k)], writes=[("stg", k)])
                return lambda: transpose_cols(stg[k], 128,
                               lambda c, n, t=t: kiT_dst[:, key0 + t * 128:key0 + (t + 1) * 128].rearrange("p (a b) -> p a b", a=1),
                               [("stg", k)], lambda c, n, t=t: [("kiT", g, t)])
            if "noki" not in SK_:
                proj_tok(S_KIWI, 128, hki, PB)

        def rnn_group(g, RB_):
            gw = RB_["gw"]
            rxh, xc, xcb = RB_["rxh"], RB_["xc"], RB_["xcb"]
            tl = RB_["tl"]
            CW = lambda j, c: rnnp_s[:, j * RC + c:j * RC + c + 1]
            for n_ in range(2):
                tok = slice(n_ * 512, (n_ + 1) * 512)
                slot_rx = {}
                slot_rg = {}

                def conv(c, n_=n_, tok=tok):
                    if c % 2 == 0:
                        slot_rx[c // 2] = ring_load(w3[S_RX + c // 2], 4096)
                    s = slot_rx[c // 2]
                    b = cnt["rb"] % 4
                    cnt["rb"] += 1
                    for dc in range(DC):
                        P.op("pe", lambda e, dc=dc, b=b, s=s: e.matmul(
                            bank(b), lhsT=ring[s][:, dc * 256 + (c % 2) * 128:dc * 256 + (c % 2) * 128 + 128],
                            rhs=actT[:, dc, tok], start=(dc == 0), stop=(dc == DC - 1)),
                            reads=[("slot", s)] + [("actT", dc, tt) for tt in range(n_ * 4, n_ * 4 + 4)],
                            writes=[("ps", b)])
                    P.op("act", lambda e, b=b: e.activation(out=rxh[:, 4:516], in_=bank(b), func=AF.Copy),
                         reads=[("ps", b)], writes=["rxh"])
                    P.op("dve", lambda e: e.tensor_copy(out=rxh[:, 0:4], in_=rxhalo[:, c, :]),
                         reads=[("rxhalo", c)], writes=["rxh"])
                    xcc = xc[c % 4]
                    P.op("dve", lambda e: e.tensor_scalar(out=xcc, in0=rxh[:, 4:516], scalar1=CW(3, c), scalar2=CW(4, c),
                                                          op0=ALU.mult, op1=ALU.add),
                         reads=["rxh", "rnnp_s"], writes=[("xc", c % 4)])
                    for j in range(3):
                        P.op("dve", lambda e, j=j: e.scalar_tensor_tensor(out=xcc, in0=rxh[:, 1 + j:1 + j + 512], scalar=CW(j, c),
                                                                          in1=xcc, op0=ALU.mult, op1=ALU.add),
                             reads=["rxh", "rnnp_s", ("xc", c % 4)], writes=[("xc", c % 4)])
                    P.op("dve", lambda e: e.tensor_copy(out=rxhalo[:, c, :], in_=rxh[:, 512:516]),
                         reads=["rxh"], writes=[("rxhalo", c)])
                    P.op("act", lambda e: e.activation(out=xcb[c % 4], in_=xcc, func=AF.Copy),
                         reads=[("xc", c % 4)], writes=[("xcb", c % 4)])

                def gates(c, n_=n_, tok=tok):
                    r_, i_, a_, a2_, gx_, h_, u_, sg_, gl_ = tl[0:9]
                    if c % 2:
                        a_, a2_ = tl[9], tl[10]
                    ka, ka2 = ("a_", c % 2), ("a2_", c % 2)
                    bA = cnt["rb"] % 4
                    bX = (cnt["rb"] + 1) % 4
                    cnt["rb"] += 2
                    for gate, bb_ in ((0, bA), (1, bX)):
                        ks = [k for k, (j, i) in enumerate(pairs) if i == c]
                        for q_, k in enumerate(ks):
                            j = pairs[k][0]
                            P.op("pe", lambda e, k=k, j=j, gate=gate, bb_=bb_, q_=q_, ks=ks: e.matmul(
                                bank(bb_), lhsT=gw[gate][:, k, :], rhs=xcb[j % 4],
                                start=(q_ == 0), stop=(q_ == len(ks) - 1)),
                                reads=["gw", ("xcb", j % 4)], writes=[("ps", bb_)])
                    P.op("act", lambda e: e.activation(out=r_, in_=bank(bA), func=AF.Sigmoid, bias=CW(5, c), scale=1.0),
                         reads=[("ps", bA), "rnnp_s"], writes=["r_"])
                    P.op("act", lambda e: e.activation(out=i_, in_=bank(bX), func=AF.Sigmoid, bias=CW(6, c), scale=1.0),
                         reads=[("ps", bX), "rnnp_s"], writes=["i_"])
                    P.op("act", lambda e: e.activation(out=a_, in_=r_, func=AF.Exp, scale=c1_s[:, c:c + 1]),
                         reads=["r_", "c1_s"], writes=[ka])
                    P.op("act", lambda e: e.activation(out=a2_, in_=r_, func=AF.Exp, scale=c2_s[:, c:c + 1]),
                         reads=["r_", "c2_s"], writes=[ka2])
                    P.op("act", lambda e: e.activation(out=a2_, in_=a2_, func=AF.Sqrt, scale=-1.0, bias=onest[:, 0:1]),
                         reads=[ka2, "onest"], writes=[ka2])
                    P.op("dve", lambda e: e.tensor_tensor(out=gx_, in0=i_, in1=xc[c % 4], op=ALU.mult),
                         reads=["i_", ("xc", c % 4)], writes=["gx_"])
                    P.op("dve", lambda e: e.tensor_tensor(out=gx_, in0=gx_, in1=a2_, op=ALU.mult),
                         reads=["gx_", ka2], writes=["gx_"])
                    P.op("dve", lambda e: e.tensor_tensor_scan(out=h_, data0=a_, data1=gx_, initial=hlast[:, c:c + 1],
                                                               op0=ALU.mult, op1=ALU.add),
                         reads=[ka, "gx_", ("hlast", c)], writes=["h_"])
                    P.op("dve", lambda e: e.tensor_copy(out=hlast[:, c:c + 1], in_=h_[:, 511:512]),
                         reads=["h_"], writes=[("hlast", c)])
                    if upto == "pc" and g == 0 and n_ == 0 and c in (0, 1, 5) and "dumps" not in os.environ.get("K_SKIP", ""):
                        dump(xc[c % 4], 128, 512, [("xc", c % 4)])
                        dump(r_, 128, 512, ["r_"])
                        dump(i_, 128, 512, ["i_"])
                        dump(a_, 128, 512, ["a_"])
                        dump(gx_, 128, 512, ["gx_"])
                        dump(h_, 128, 512, ["h_"])
                    if g == 1:
                        if c % 2 == 0:
                            slot_rg[c // 2] = ring_load(w3[S_RG + c // 2], 4096)
                        s = slot_rg[c // 2]
                        bG = cnt["rb"] % 4
                        cnt["rb"] += 1
                        for dc in range(DC):
                            P.op("pe", lambda e, dc=dc, s=s: e.matmul(
                                bank(bG), lhsT=ring[s][:, dc * 256 + (c % 2) * 128:dc * 256 + (c % 2) * 128 + 128],
                                rhs=actT[:, dc, tok], start=(dc == 0), stop=(dc == DC - 1)),
                                reads=[("slot", s)] + [("actT", dc, tt) for tt in range(n_ * 4, n_ * 4 + 4)],
                                writes=[("ps", bG)])
                        P.op("act", lambda e: e.activation(out=u_, in_=bank(bG), func=AF.Square),
                             reads=[("ps", bG)], writes=["u_"])
                        P.op("dve", lambda e: e.tensor_scalar(out=u_, in0=u_, scalar1=0.044715, scalar2=1.0,
                                                              op0=ALU.mult, op1=ALU.add),
                             reads=["u_"], writes=["u_"])
                        P.op("dve", lambda e: e.tensor_tensor(out=u_, in0=u_, in1=bank(bG), op=ALU.mult),
                             reads=["u_", ("ps", bG)], writes=["u_"])
                        P.op("act", lambda e: e.activation(out=sg_, in_=u_, func=AF.Sigmoid, scale=1.5957691216),
                             reads=["u_"], writes=["sg_"])
                        P.op("dve", lambda e: e.tensor_tensor(out=gl_, in0=sg_, in1=bank(bG), op=ALU.mult),
                             reads=["sg_", ("ps", bG)], writes=["gl_"])
                        P.op("dve", lambda e: e.tensor_tensor(out=yrT[:, c, tok], in0=gl_, in1=h_, op=ALU.mult),
                             reads=["gl_", "h_"], writes=[("yrT", c, n_)])

                for c in range(RC + 1):
                    if c < RC:
                        conv(c)
                    if c >= 1 and "nogates" not in os.environ.get("K_SKIP", "").split(","):
                        gates(c - 1)

        def rnn_bufs(off):
            o_ = off
            RB_ = {}
            g0 = carve(o_, [128, NP, 128], BF16); o_ += NP * 256
            g1 = carve(o_, [128, NP, 128], BF16); o_ += NP * 256
            RB_["gw"] = (g0, g1)
            RB_["rxh"] = carve(o_, [128, 516], F32); o_ += 2064
            RB_["xc"] = []
            for k in range(4):
                RB_["xc"].append(carve(o_, [128, 512], F32)); o_ += 2048
            RB_["xcb"] = []
            for k in range(4):
                RB_["xcb"].append(carve(o_, [128, 512], BF16)); o_ += 1024
            RB_["tl"] = []
            for k in range(11):
                RB_["tl"].append(carve(o_, [128, 512], F32)); o_ += 2048
            for gi, (gwt, src) in enumerate(((g0, wga), (g1, wgx))):
                for q_ in range(4):
                    n0 = q_ * 13
                    P.dma("pool", lambda e, gwt=gwt, src=src, n0=n0: e.dma_start(
                        out=gwt.rearrange("p a b -> p (a b)")[:, n0 * 128:(n0 + 13) * 128], in_=src[:, n0 * 128:(n0 + 13) * 128]),
                        writes=["gw"], tag=("gw", gi, q_))
            return RB_

        RNN_OFF = 106496
        sa, xf1 = ffn_phase_bufs()
        load_xT(0, xf1)
        ffn(x2[0:1024, :], "x2c", w1a, w1b, 0, dbg_out[3072:4096, :] if upto in ("ffnc", "pc") else None, "dbgh", True, sa)
        if upto in ("ffnc", "pc"):
            final_tags += [("hst_dbgh", 0), ("hst_dbgh", 1)]
        barrier()
        kTc = carve(OFF_Q, [128, 4, 1024], BF16)
        Vc = carve(OFF_Q + 8192, [128, 8, 512], BF16)
        kiTc = carve(OFF_Q + 16384, [128, 1024], BF16)
        PB = dict(stg=[carve(OFF_Q + 20480, [128, 256], F32), carve(OFF_Q + 21504, [128, 256], F32)],
                  tmps=[carve(OFF_Q + 22528 + 128 * i_, [128, 32], F32) for i_ in range(4)])
        import os
        SKIP = os.environ.get("K_SKIP", "").split(",")
        if "proj" not in SKIP:
            proj_group(0, PB, kTc, Vc, kiTc, 0)
        P.dma("sp", lambda e: e.dma_start(out=kTc_d, in_=kTc.rearrange("p a b -> p (a b)")),
              reads=[("kT", 0, h_, t_) for h_ in range(4) for t_ in range(8)], writes=["kTc_d"], tag="kTc_d")
        P.dma("sp", lambda e: e.dma_start(out=Vc_d, in_=Vc.rearrange("p a b -> p (a b)")),
              reads=[("V", 0, t_, s_) for t_ in range(8) for s_ in range(2)], writes=["Vc_d"], tag="Vc_d")
        P.dma("sp", lambda e: e.dma_start(out=kiTc_d, in_=kiTc),
              reads=[("kiT", 0, t_) for t_ in range(8)], writes=["kiTc_d"], tag="kiTc_d")
        if "rnn" not in SKIP:
            RB_ = rnn_bufs(RNN_OFF)
            rnn_group(0, RB_)
        P.op("dve", lambda e: e.tensor_scalar(out=hlast[:], in0=hlast[:], scalar1=flag_s[:, 0:1], scalar2=None, op0=ALU.mult),
             reads=[("hlast", c_) for c_ in range(RC)] + ["flag_s"], writes=[("hlast", c_) for c_ in range(RC)])
        P.op("dve", lambda e: e.tensor_scalar(out=rxhalo[:].rearrange("p a b -> p (a b)"),
                                              in0=rxhalo[:].rearrange("p a b -> p (a b)"),
                                              scalar1=flag_s[:, 0:1], scalar2=None, op0=ALU.mult),
             reads=[("rxhalo", c_) for c_ in range(RC)] + ["flag_s"], writes=[("rxhalo", c_) for c_ in range(RC)])
        if upto == "pc" and "dumps" not in SKIP:
            dump(kTc.rearrange("p a b -> p (a b)")[:, 0:2048], 128, 2048, [("kT", 0, h_, t_) for h_ in range(4) for t_ in range(8)], is_bf16=True)
            dump(Vc.rearrange("p a b -> p (a b)")[:, 0:2048], 128, 2048, [("V", 0, t_, s_) for t_ in range(8) for s_ in range(2)], is_bf16=True)
            dump(kiTc[:, 0:1024], 128, 1024, [("kiT", 0, t_) for t_ in range(8)], is_bf16=True)
            dump(hlast[:], 128, RC, [("hlast", c_) for c_ in range(RC)])
            dump(rxhalo[:].rearrange("p a b -> p (a b)"), 128, 4 * RC, [("rxhalo", c_) for c_ in range(RC)])
        barrier()
        if upto in ("ffnc", "pc"):
            P.dma("sp", lambda e: e.dma_start(out=out[0:128, :], in_=LB["yfull"]), reads=[("yfull",)], tag="outd")
            final_tags.append("outd")
            P.emit(final_wait_tags=final_tags)
            return nc, P
        sa, xf1 = ffn_phase_bufs()
        load_xT(1024, xf1)
        ffn(x2[1024:2048, :], "x2o", w1a, w1b, 0, h1d, "h1d", True, sa)
        barrier()
        o_ = OFF_T + 45056
        PB = dict(stg=[carve(o_, [128, 256], F32), carve(o_ + 1024, [128, 256], F32)],
                  tmps=[carve(o_ + 2048 + 128 * i_, [128, 32], F32) for i_ in range(4)])
        P.dma("sp", lambda e: e.dma_start(out=kT[:, :, 0:1024], in_=kTc_d.rearrange("p (a b) -> p a b", a=4)),
              reads=["kTc_d"], writes=[("kT", 0, h_, t_) for h_ in range(4) for t_ in range(8)], tag="kTr")
        P.dma("sp", lambda e: e.dma_start(out=Vt[:, 0:8, :], in_=Vc_d.rearrange("p (a b) -> p a b", a=8)),
              reads=["Vc_d"], writes=[("V", 0, t_, s_) for t_ in range(8) for s_ in range(2)], tag="Vr")
        P.dma("sp", lambda e: e.dma_start(out=kiT[:, 0:1024], in_=kiTc_d),
              reads=["kiTc_d"], writes=[("kiT", 0, t_) for t_ in range(8)], tag="kiTr")
        proj_group(1, PB, kT, Vt, kiT, 1024)
        if upto == "po":
            dump(bufQ[:, 0, :], 128, 1024, [("bufQ", 0, t_) for t_ in range(8)], is_bf16=True)
            dump(bufQ[:, 5, :], 128, 1024, [("bufQ", 5, t_) for t_ in range(8)], is_bf16=True)
            dump(kT[:, 1, :], 128, 2048, [("kT", g_, 1, t_) for g_ in range(2) for t_ in range(8)], is_bf16=True)
            dump(qiT[:, 3, :], 128, 1024, [("qiT", 3, t_) for t_ in range(8)], is_bf16=True)
            dump(kiT[:, :], 128, 2048, [("kiT", g_, t_) for g_ in range(2) for t_ in range(8)], is_bf16=True)
            dump(witm[:].rearrange("p a b -> p (a b)"), 128, 128, [("witm", t_) for t_ in range(8)])
            dump(Vt[:, 9, :], 128, 512, [("V", 1, 1, s_) for s_ in range(2)], is_bf16=True)
        barrier()
        h1T_d = nc.dram_tensor("h1T_d", [128, DC * T], BF16).ap()
        P.dma("sp", lambda e: e.dma_start(out=h1T_d, in_=actT.rearrange("p a b -> p (a b)")),
              reads=[("actT", dc_, t_) for dc_ in range(DC) for t_ in range(NT)], writes=["h1T_d"], tag="h1T_st")
        barrier()
        NQB = 8 if upto in ("all", "att", "rn", "mg") else 2
        if upto == "po":
            NQB = 0
        NKs = [1152 + 128 * j for j in range(8)]
        mb = []
        o_ = OFF_ACTT
        for j in range(8):
            mb.append(carve(o_, [128, NKs[j]], BF16)); o_ += NKs[j] * 2
        yh = [carve(o_, [128, 128], F32), carve(o_ + 512, [128, 128], F32)]
        o_ = OFF_T
        accs = []
        for k in range(4):
            accs.append(carve(o_, [128, 2048], F32)); o_ += 8192
        rls = [carve(o_, [128, 2048], F32), carve(o_ + 8192, [128, 2048], F32)]
        S = [ring[0][:, :].bitcast(F32), ring[1][:, :].bitcast(F32)]
        PTs2 = [ring[2][:, 0:2048].rearrange("p (a b) -> p a b", b=128),
                ring[2][:, 2048:4096].rearrange("p (a b) -> p a b", b=128)]
        m8s = [m8, sb("m8b", [128, 8], F32)]
        thrs = [thr, sb("thrb", [128, 1], F32)]
        hcnt = [0]

        def indexer_head(j, h):
            NK = NKs[j]
            nkc = 9 + j
            ai = j % 4
            acc = accs[ai]
            kres = [("kiT", kc // 8, kc % 8) for kc in range(nkc)]
            if h == 0:
                P.dma("sp", lambda e: e.dma_start(out=acc[:, 0:NK], in_=cbias[j * 128:(j + 1) * 128, 0:NK]),
                      writes=[("acc", ai)], tag=("cb", ai))
            pair, hf = h // 2, h % 2
            hp = hcnt[0] % 2
            hcnt[0] += 1
            rl = rls[hp]
            for half in range(2):
                k0 = half * 1024
                k1 = min(NK, k0 + 1024)
                if k1 <= k0:
                    continue
                for kb in range(2):
                    c0 = k0 + kb * 512
                    n = min(512, k1 - c0)
                    if n <= 0:
                        continue
                    P.op("pe", lambda e, kb=kb, n=n, c0=c0: e.matmul(
                        ps[:, kb * 512:kb * 512 + n],
                        lhsT=qiT[hf * 64:(hf + 1) * 64, pair, j * 128:(j + 1) * 128],
                        rhs=kiT[hf * 64:(hf + 1) * 64, c0:c0 + n], start=True, stop=True),
                        reads=[("qiT", pair, j)] + kres, writes=[("ps", kb)])
                P.op("act", lambda e, k0=k0, k1=k1: e.activation(
                    out=rl[:, k0:k1], in_=ps[:, 0:k1 - k0], func=AF.Relu),
                    reads=[("ps", 0), ("ps", 1)], writes=[("rl", hp)])
            if j < 2:
                P.op("dve", lambda e: e.scalar_tensor_tensor(
                    out=acc[:, 0:NK], in0=rl[:, 0:NK], scalar=witm[:, j, h:h + 1], in1=acc[:, 0:NK],
                    op0=ALU.mult, op1=ALU.add),
                    reads=[("rl", hp), ("witm", j), ("acc", ai)], writes=[("acc", ai)])
                return
            P.op("pool", lambda e: e.tensor_scalar(
                out=rl[:, 0:NK], in0=rl[:, 0:NK], scalar1=witm[:, j, h:h + 1], scalar2=0.0,
                op0=ALU.mult, op1=ALU.add),
                reads=[("rl", hp), ("witm", j)], writes=[("rl", hp)])
            P.op("pool", lambda e: e.tensor_tensor(out=acc[:, 0:NK], in0=acc[:, 0:NK], in1=rl[:, 0:NK], op=ALU.add),
                 reads=[("rl", hp), ("acc", ai)], writes=[("acc", ai)])

        def qk(j, h):
            NK = NKs[j]
            nkc = 9 + j
            nb = (NK + 511) // 512
            kvh = h // 4
            for kb in range(nb):
                n = min(512, NK - kb * 512)
                P.op("pe", lambda e, kb=kb, n=n: e.matmul(
                    ps[:, (2 + kb) * 512:(2 + kb) * 512 + n], lhsT=bufQ[:, h, j * 128:(j + 1) * 128],
                    rhs=kT[:, kvh, kb * 512:kb * 512 + n], start=True, stop=True),
                    reads=[("bufQ", h, j)] + [("kT", kc // 8, kvh, kc % 8) for kc in range(nkc)],
                    writes=[("ps", 2 + kb)])

        def a2_head(j, h):
            NK = NKs[j]
            nkc = 9 + j
            nb = (NK + 511) // 512
            kvh = h // 4
            hh = h % 2
            Sm = S[hh]
            PTs = PTs2[hh]
            sk = ("slot", hh)
            if h == 0:
                qk(j, 0)
            P.op("dve", lambda e: e.scalar_tensor_tensor(
                out=Sm[:, 0:NK], in0=ps[:, 1024:1024 + NK], scalar=SCALE, in1=mb[j][:, 0:NK], op0=ALU.mult, op1=ALU.add),
                reads=[("ps", 2 + kb) for kb in range(nb)] + [("mb", j), sk], writes=[("S", hh)])
            if h < 15:
                qk(j, h + 1)
            P.op("dve", lambda e: e.reduce_max(out=mx[:, hh:hh + 1], in_=Sm[:, 0:NK], axis=AX.X, negate=True),
                 reads=[("S", hh), sk], writes=[("mx", hh)])
            P.op("act", lambda e: e.activation(
                out=Sm[:, 0:NK], in_=Sm[:, 0:NK], func=AF.Exp, bias=mx[:, hh:hh + 1], scale=1.0,
                accum_out=rs[:, hh:hh + 1]),
                reads=[("S", hh), ("mx", hh), sk], writes=[("S", hh), ("rs", hh)])
            kc = 0
            while kc < nkc:
                n = min(4, nkc - kc)
                for i in range(n):
                    P.op("pe", lambda e, i=i, kc=kc: e.transpose(
                        bank(6, 128, i * 128), Sm[:, (kc + i) * 128:(kc + i + 1) * 128], identf[:]),
                        reads=[("S", hh), "identf", sk], writes=[("ps", 6)])
                P.op("act", lambda e, kc=kc, n=n: e.activation(
                    out=PTs[:, kc:kc + n, :], in_=bank(6, n * 128).rearrange("p (a b) -> p a b", b=128), func=AF.Copy),
                    reads=[("ps", 6), ("slot", 2)], writes=[("PTs", hh)])
                kc += n
            for kc in range(nkc):
                P.op("pe", lambda e, kc=kc: e.matmul(
                    bank(7, 128), lhsT=PTs[:, kc, :], rhs=Vt[:, kc, kvh * 128:(kvh + 1) * 128],
                    start=(kc == 0), stop=(kc == nkc - 1)),
                    reads=[("PTs", hh), ("V", kc // 8, kc % 8, kvh // 2), ("slot", 2)], writes=[("ps", 7)])
            P.op("dve", lambda e: e.reciprocal(out=rinv[:, hh:hh + 1], in_=rs[:, hh:hh + 1]),
                 reads=[("rs", hh)], writes=[("rinv", hh)])
            P.op("act", lambda e: e.activation(
                out=yh[hh], in_=bank(7, 128), func=AF.Copy, scale=rinv[:, hh:hh + 1]),
                reads=[("ps", 7), ("rinv", hh)], writes=[("yh", hh)])
            P.op("pe", lambda e: e.transpose(bank(6, 128, 0), yh[hh], identf[:]),
                 reads=[("yh", hh), "identf"], writes=[("ps", 6)])
            P.op("act", lambda e: e.activation(out=bufQ[:, h, j * 128:(j + 1) * 128], in_=bank(6, 128), func=AF.Copy),
                 reads=[("ps", 6)], writes=[("bufQ", h, j)])

        def topk_pair(js, fillers):
            for r_ in range(32):
                for c_, j in enumerate(js):
                    NK = NKs[j]
                    acc = accs[j % 4]
                    P.op("dve", lambda e, acc=acc, NK=NK, c_=c_: e.max(out=m8s[c_][:], in_=acc[:, 0:NK]),
                         reads=[("acc", j % 4)], writes=[("m8", c_)])
                    if r_ < 31:
                        P.op("dve", lambda e, acc=acc, NK=NK, c_=c_: e.match_replace(
                            out=acc[:, 0:NK], in_to_replace=m8s[c_][:], in_values=acc[:, 0:NK], imm_value=-3.0e38),
                            reads=[("acc", j % 4), ("m8", c_)], writes=[("acc", j % 4)])
                for f_ in fillers[r_]:
                    f_()
            for c_, j in enumerate(js):
                NK = NKs[j]
                acc = accs[j % 4]
                P.op("dve", lambda e, c_=c_: e.tensor_scalar(out=thrs[c_][:], in0=m8s[c_][:, 7:8], scalar1=-1.0e29,
                                                             scalar2=None, op0=ALU.max),
                     reads=[("m8", c_)], writes=[("thr", c_)])
                P.op("dve", lambda e, acc=acc, NK=NK, j=j: e.tensor_scalar(
                    out=mb[j][:, 0:NK], in0=acc[:, 0:NK], scalar1=-2.0e38, scalar2=NEG, op0=ALU.is_gt, op1=ALU.mult),
                    reads=[("acc", j % 4)], writes=[("mb", j)])
                P.op("dve", lambda e, acc=acc, NK=NK, j=j, c_=c_: e.scalar_tensor_tensor(
                    out=mb[j][:, 0:NK], in0=acc[:, 0:NK], scalar=thrs[c_][:, 0:1], in1=mb[j][:, 0:NK],
                    op0=ALU.is_lt, op1=ALU.mult),
                    reads=[("acc", j % 4), ("thr", c_), ("mb", j)], writes=[("mb", j)])
                if j < 2:
                    P.dma("sp", lambda e, acc=acc, NK=NK, j=j: e.dma_start(
                        out=acc[:, 0:NK], in_=cbias[j * 128:(j + 1) * 128, 0:NK]),
                        reads=[("mb", j)], writes=[("acc", j % 4)], tag=("cb", j % 4))
                    P.op("dve", lambda e, acc=acc, NK=NK, j=j: e.tensor_tensor(
                        out=mb[j][:, 0:NK], in0=mb[j][:, 0:NK], in1=acc[:, 0:NK], op=ALU.add),
                        reads=[("acc", j % 4), ("mb", j)], writes=[("mb", j)])

        npair = NQB // 2
        if npair:
            for h in range(16):
                indexer_head(0, h)
            for h in range(16):
                indexer_head(1, h)
        for p_ in range(npair):
            fillers = [[] for _ in range(32)]
            if p_ + 1 < npair:
                for i_ in range(32):
                    fillers[i_].append(lambda i_=i_, p_=p_: indexer_head(2 * (p_ + 1) + i_ // 16, i_ % 16))
            if p_ >= 1:
                for i_ in range(32):
                    fillers[i_].append(lambda i_=i_, p_=p_: a2_head(2 * (p_ - 1) + i_ // 16, i_ % 16))
            topk_pair((2 * p_, 2 * p_ + 1), fillers)
        if npair:
            for i_ in range(32):
                a2_head(2 * (npair - 1) + i_ // 16, i_ % 16)
        if upto == "att":
            dump(mb[1][:, :], 128, NKs[1], [("mb", 1)], is_bf16=True)
            dump(bufQ[:, :, 128:256], 128, 2048, [("bufQ", h_, 1) for h_ in range(16)], is_bf16=True)
        barrier()
        P.dma("sp", lambda e: e.dma_start(out=actT.rearrange("p a b -> p (a b)"), in_=h1T_d),
              reads=["h1T_d"], writes=[("actT", dc_, t_) for dc_ in range(DC) for t_ in range(NT)], tag="h1T_ld")
        barrier()
        if upto in ("po", "att"):
            P.dma("sp", lambda e: e.dma_start(out=out[0:128, :], in_=LB["yfull"]), reads=[("yfull",)], tag="outd")
            final_tags.append("outd")
            P.emit(final_wait_tags=final_tags)
            return nc, P
        RB_ = rnn_bufs(RNN_OFF)
        rnn_group(1, RB_)
        if upto == "rn":
            dump(yrT[:, 0, :], 128, 1024, [("yrT", 0, n_) for n_ in range(2)], is_bf16=True)
            dump(yrT[:, 7, :], 128, 1024, [("yrT", 7, n_) for n_ in range(2)], is_bf16=True)
            dump(yrT[:, 19, :], 128, 1024, [("yrT", 19, n_) for n_ in range(2)], is_bf16=True)
            dump(bufQ[:, 3, :], 128, 1024, [("bufQ", 3, t_) for t_ in range(8)], is_bf16=True)
            P.dma("sp", lambda e: e.dma_start(out=out[0:128, :], in_=LB["yfull"]), reads=[("yfull",)], tag="outd")
            final_tags.append("outd")
            P.emit(final_wait_tags=final_tags)
            return nc, P
        barrier()
        MOFF = 106496
        mergedT = carve(MOFF, [128, 16, T], BF16)
        o_ = MOFF + 32768
        sga, sgr, m1 = [], [], []
        for lst in (sga, sgr, m1):
            for k in range(4):
                lst.append(carve(o_, [128, 512], F32)); o_ += 2048
        LB["rt"] = [carve(o_, [128, 512], F32), carve(o_ + 2048, [128, 512], F32)]
        LB["yt"] = [carve(o_ + 4096, [128, 512], F32), carve(o_ + 6144, [128, 512], F32)]
        for op_ in range(8):
            for which, sidx, dstl, b0 in ((0, S_G + op_, sga, 0), (1, S_G + 8 + op_, sgr, 4)):
                s = ring_load(w3[sidx], 4096)
                for q in range(2):
                    for n_ in range(2):
                        b = b0 + q * 2 + n_
                        for dc in range(DC):
                            P.op("pe", lambda e, b=b, dc=dc, q=q, n_=n_, s=s: e.matmul(
                                bank(b), lhsT=ring[s][:, dc * 256 + q * 128:dc * 256 + q * 128 + 128],
                                rhs=actT[:, dc, n_ * 512:(n_ + 1) * 512], start=(dc == 0), stop=(dc == DC - 1)),
                                reads=[("slot", s)] + [("actT", dc, tt) for tt in range(n_ * 4, n_ * 4 + 4)],
                                writes=[("ps", b)])
                        P.op("act", lambda e, b=b, d_=dstl[q * 2 + n_]: e.activation(out=d_, in_=bank(b), func=AF.Sigmoid),
                             reads=[("ps", b)], writes=[("sg", which, q * 2 + n_)])
            s = ring_load(wab[op_], 4096)
            for q in range(2):
                for n_ in range(2):
                    b = q * 2 + n_
                    for kc in range(16):
                        P.op("pe", lambda e, b=b, kc=kc, q=q, n_=n_, s=s: e.matmul(
                            bank(b), lhsT=ring[s][:, kc * 256 + q * 128:kc * 256 + q * 128 + 128],
                            rhs=bufQ[:, kc, n_ * 512:(n_ + 1) * 512], start=(kc == 0), stop=(kc == 15)),
                            reads=[("slot", s)] + [("bufQ", kc, tt) for tt in range(n_ * 4, n_ * 4 + 4)],
                            writes=[("ps", b)])
                    P.op("dve", lambda e, b=b, i_=q * 2 + n_: e.tensor_tensor(out=m1[i_], in0=sga[i_], in1=bank(b), op=ALU.mult),
                         reads=[("ps", b), ("sg", 0, q * 2 + n_)], writes=[("m1", q * 2 + n_)])
            for q in range(2):
                oc = 2 * op_ + q
                s = ring_load(wrb[oc], 2560)
                for n_ in range(2):
                    b = 4 + q * 2 + n_
                    for kc in range(RC):
                        P.op("pe", lambda e, b=b, kc=kc, n_=n_, s=s: e.matmul(
                            bank(b), lhsT=ring[s][:, kc * 128:(kc + 1) * 128],
                            rhs=yrT[:, kc, n_ * 512:(n_ + 1) * 512], start=(kc == 0), stop=(kc == RC - 1)),
                            reads=[("slot", s), ("yrT", kc, n_)], writes=[("ps", b)])
                    i_ = q * 2 + n_
                    P.op("dve", lambda e, b=b, i_=i_: e.tensor_tensor(out=sgr[i_], in0=sgr[i_], in1=bank(b), op=ALU.mult),
                         reads=[("ps", b), ("sg", 1, i_)], writes=[("sg", 1, i_)])
                    P.op("dve", lambda e, i_=i_, oc=oc, n_=n_: e.tensor_tensor(
                        out=mergedT[:, oc, n_ * 512:(n_ + 1) * 512], in0=m1[i_], in1=sgr[i_], op=ALU.add),
                        reads=[("m1", i_), ("sg", 1, i_)], writes=[("mT", oc, n_)])
        if upto == "mg":
            dump(mergedT[:, 0, :], 128, 1024, [("mT", 0, n_) for n_ in range(2)], is_bf16=True)
            dump(mergedT[:, 9, :], 128, 1024, [("mT", 9, n_) for n_ in range(2)], is_bf16=True)
        tok_out_stage(lambda kc, t: mergedT[:, kc, t * 128:(t + 1) * 128],
                      lambda kc, t: [("mT", kc, t // 4)], 16, wo, h1d, "h1d")
        barrier()
        LB["yfull"] = carve(MOFF, [128, 2048], F32)
        LB["gB"] = carve(MOFF + 8192, [128, 2048], F32)
        LB["bB"] = carve(MOFF + 16384, [128, 2048], F32)
        LB["yfull2"] = carve(MOFF + 24576, [128, 2048], F32)
        LB["yfull2_key"] = ("yfull2",)
        layer_norm_pass(1, h2d if upto != "mg" else dbg_out[3072:4096, :], "h2d", True)
        barrier()
        if upto == "mg":
            final_tags += [("hst_h2d", 0), ("hst_h2d", 1)]
            P.dma("sp", lambda e: e.dma_start(out=out[0:128, :], in_=LB["yfull"]), reads=[("yfull",)], tag="outd")
            final_tags.append("outd")
            P.emit(final_wait_tags=final_tags)
            return nc, P
        sa, xf1 = ffn_phase_bufs()
        ffn(h2d, "h2d", w2a, w2b, 2, out, "out", False, sa)
        final_tags += [("hst_out", 0), ("hst_out", 1)]
        P.emit(final_wait_tags=final_tags)
    return nc, P


def _tile_a(w, c0, ncols, pad):
    K = w.shape[0]
    blk = np.zeros((128, K // 128, pad), np.float32)
    blk[:, :, :ncols] = w[:, c0:c0 + ncols].reshape(K // 128, 128, ncols).transpose(1, 0, 2)
    return blk


def _pack_ffn(w_in, w_out):
    wa = np.empty((FC, 128, 2, 16, 128), np.float32)
    for fc in range(FC):
        wa[fc, :, 0] = _tile_a(w_in, fc * 128, 128, 128)
        wa[fc, :, 1] = _tile_a(w_in, FF + fc * 128, 128, 128)
    wb = np.empty((4, 11, 128, 4, 512), np.float32)
    wo = w_out.reshape(11, 4, 128, 4, 512)
    wb[:] = wo.transpose(3, 0, 2, 1, 4)
    return np.ascontiguousarray(wa.reshape(FC, 128, 4096)), np.ascontiguousarray(wb.reshape(FC, 128, 2048))


def _pack_shared(inp):
    sh = {}
    sh["w1a"], sh["w1b"] = _pack_ffn(inp["ffn1_w_in"][0], inp["ffn1_w_out"][0])
    sh["w2a"], sh["w2b"] = _pack_ffn(inp["ffn2_w_in"][0], inp["ffn2_w_out"][0])
    w_in = inp["w_in"][0]
    sl = w3_slots()
    w3 = np.empty((len(sl), 128, 16, 256), np.float32)
    for i, (c0, nc_) in enumerate(sl):
        w3[i] = _tile_a(w_in, c0, nc_, 256)
    sh["w3"] = w3.reshape(len(sl), 128, 4096)
    wab = np.empty((8, 128, 16, 256), np.float32)
    for i in range(8):
        wab[i] = _tile_a(inp["w_attn_branch"][0], i * 256, 256, 256)
    sh["wab"] = wab.reshape(8, 128, 4096)
    wrb = np.empty((16, 128, 20, 128), np.float32)
    for i in range(16):
        wrb[i] = _tile_a(inp["w_rnn_branch"][0], i * 128, 128, 128)
    sh["wrb"] = wrb.reshape(16, 128, 2560)
    wo = inp["w_out"][0].reshape(4, 4, 128, 4, 512)
    sh["wo"] = np.ascontiguousarray(wo.transpose(3, 0, 2, 1, 4)).reshape(16, 128, 2048)
    pairs = gate_pairs()
    for name, key in (("wga", "lru_wa"), ("wgx", "lru_wx")):
        bd = np.zeros((DRNN, DRNN), np.float32)
        for nb in range(16):
            bd[nb * 160:(nb + 1) * 160, nb * 160:(nb + 1) * 160] = inp[key][0, nb]
        g = np.empty((128, len(pairs), 128), np.float32)
        for k, (j, i) in enumerate(pairs):
            g[:, k, :] = bd[j * 128:(j + 1) * 128, i * 128:(i + 1) * 128]
        sh[name] = g.reshape(128, len(pairs) * 128)
    sh["lnp"] = np.stack([inp["ln1_g"][0], inp["ln1_b"][0], inp["ln2_g"][0], inp["ln2_b"][0],
                          inp["ln3_g"][0], inp["ln3_b"][0]]).astype(np.float32)
    rn = np.empty((128, 8, RC), np.float32)
    for j in range(4):
        rn[:, j, :] = inp["conv_w"][0, j].reshape(RC, 128).T
    rn[:, 4, :] = inp["conv_b"][0].reshape(RC, 128).T
    rn[:, 5, :] = inp["lru_ba"][0].reshape(RC, 128).T
    rn[:, 6, :] = inp["lru_bx"][0].reshape(RC, 128).T
    rn[:, 7, :] = inp["lru_lambda"][0].reshape(RC, 128).T
    sh["rnnp"] = rn.reshape(128, 8 * RC)
    sh["ident"] = np.eye(128, dtype=np.float32)
    invf = np.zeros((128, 24), np.float32)
    invf[:, 0:16] = (500000.0 ** (-(np.arange(16, dtype=np.float32) * 2.0 / 32.0))).astype(np.float32)[None]
    invf[:, 16:24] = (500000.0 ** (-(np.arange(8, dtype=np.float32) * 2.0 / 16.0))).astype(np.float32)[None]
    sh["invf"] = invf
    return sh


def _core_inputs(inp, sh, c):
    b, half = c // 2, c % 2
    x = inp["x"]
    pos = inp["positions"]
    x2 = np.zeros((2048, 2048), np.float32)
    p2 = np.zeros((2, 1024), np.int32)
    if half == 1:
        x2[:1024] = x[b, :1024]
        p2[0] = pos[b, :1024]
    x2[1024:] = x[b, half * 1024:(half + 1) * 1024]
    p2[1] = pos[b, half * 1024:(half + 1) * 1024]
    posi = np.ascontiguousarray(p2.reshape(2, 8, 128).transpose(2, 0, 1).reshape(128, 16))
    q = np.arange(1024)[:, None] + 1024
    k = np.arange(2048)[None, :]
    valid = (k <= q) & (k >= (0 if half == 1 else 1024))
    cb = np.where(valid, 0.0, NEG).astype(np.float32)
    m = dict(sh)
    m.update(x2=x2, posi=posi, cbias=cb, flag=np.full((128, 1), float(half), np.float32))
    return m


_CACHE = {}


def kernel(**inputs):
    inp = {k: np.asarray(v) for k, v in inputs.items()}
    if "nc" not in _CACHE:
        _CACHE["nc"] = build("all")[0]
    nc = _CACHE["nc"]
    sh = _pack_shared(inp)
    in_maps = [_core_inputs(inp, sh, c) for c in range(8)]
    res = run_bass_kernel_spmd(nc, in_maps, core_ids=list(range(8)))
    outp = np.empty((4, 2048, 2048), np.float32)
    for c in range(8):
        b, half = c // 2, c % 2
        outp[b, half * 1024:(half + 1) * 1024] = res.results[c]["out"]
    return outp
```
